# Optimizing a Trainium2 kernel written in Bass

```python
import math
import jax, jax.numpy as jnp
from jax import lax
import numpy as np

D_MODEL = 1024
BATCH = 2
SEQ = 8192
DEPTH = 4

GRID_W = 64
CTX_LEN = 256
N_MIXERS = 3
N_A = (DEPTH + 2) // 3
N_B = (DEPTH + 1) // 3
N_C = DEPTH // 3
N_MOD = 9
EPS = 1e-6

FFN_DIM = 2816

SSD_EXPAND = 2
SSD_INNER = SSD_EXPAND * D_MODEL
SSD_HEADDIM = 64
SSD_HEADS = SSD_INNER // SSD_HEADDIM
SSD_STATE = 128
SSD_GROUPS = 4
SSD_HPG = SSD_HEADS // SSD_GROUPS
SSD_CONV = 5
SSD_CHUNK = 128
SSD_CONV_DIM = SSD_INNER + 2 * SSD_GROUPS * SSD_STATE
SSD_PROJ = 2 * SSD_INNER + 2 * SSD_GROUPS * SSD_STATE + 2 * SSD_HEADS

SC_WIDTH = 3

GM_CHUNK = 128
GM_DIM = 2 * D_MODEL
GM_GROUPS = 8
GM_GROUP_DIM = GM_DIM // GM_GROUPS

kernel_name = "hybrid_ssd_shortconv_chunkmlp_macaron_prefix"


def rmsnorm(x, g):
    xf = x.astype(jnp.float32)
    y = xf * lax.rsqrt(jnp.mean(xf * xf, axis=-1, keepdims=True) + EPS)
    return (y * g.astype(jnp.float32)).astype(x.dtype)


def layernorm(x, g):
    xf = x.astype(jnp.float32)
    xc = xf - jnp.mean(xf, axis=-1, keepdims=True)
    y = xc * lax.rsqrt(jnp.mean(xc * xc, axis=-1, keepdims=True) + EPS)
    return (y * g.astype(jnp.float32)).astype(x.dtype)


def modulate(x, shift, scale):
    return x * (1 + scale) + shift


def half_ffn(h, shift, scale, gate, g, wg, wu, wd):
    u = modulate(rmsnorm(h, g), shift, scale)
    return h + 0.5 * gate * ((jax.nn.silu(u @ wg) * (u @ wu)) @ wd)


def dwconv(x, w):
    k = w.shape[0]
    return lax.conv_general_dilated(
        x, w[:, None, :], window_strides=(1,), padding=[(k // 2, k // 2)],
        dimension_numbers=("NWC", "WIO", "NWC"), feature_group_count=x.shape[-1])


def ssd_scan(x, dt, a, bm, cm, s0):
    f32 = jnp.float32
    bsz, t, h, p = x.shape
    g, n = bm.shape[2], bm.shape[3]
    hg = h // g
    q = SSD_CHUNK
    nc = t // q
    xc = x.astype(f32).reshape(bsz, nc, q, g, hg, p)
    dtc = dt.astype(f32).reshape(bsz, nc, q, g, hg)
    bc = bm.astype(f32).reshape(bsz, nc, q, g, n)
    cc = cm.astype(f32).reshape(bsz, nc, q, g, n)
    cs = jnp.cumsum(dtc * a.astype(f32).reshape(g, hg), axis=2)
    xdt = xc * dtc[..., None]
    seg = cs[:, :, :, None] - cs[:, :, None, :]
    lower = jnp.tril(jnp.ones((q, q), bool))[None, None, :, :, None, None]
    decay = jnp.exp(jnp.where(lower, seg, -jnp.inf))
    cb = jnp.einsum('bcign,bcjgn->bcijg', cc, bc)
    y_diag = jnp.einsum('bcijgh,bcjghp->bcighp', decay * cb[..., None], xdt)
    to_end = jnp.exp(cs[:, :, -1:] - cs)
    states = jnp.einsum('bcjgn,bcjghp->bcghpn', bc, xdt * to_end[..., None])
    chunk_decay = jnp.exp(cs[:, :, -1])

    def step(s, inp):
        st, dec = inp
        return s * dec[..., None, None] + st, s

    s_final, prev = lax.scan(step, s0, (jnp.moveaxis(states, 1, 0), jnp.moveaxis(chunk_decay, 1, 0)))
    prev = jnp.moveaxis(prev, 0, 1)
    y_off = jnp.einsum('bcign,bcghpn->bcighp', cc, prev) * jnp.exp(cs)[..., None]
    return (y_diag + y_off).reshape(bsz, t, h, p), s_final


def ssd_stream(u, w_in, conv_w, conv_b, dt_bias, a_log, d_skip, norm_w, w_out, s0_f, s0_b):
    f32 = jnp.float32
    bsz, t, _ = u.shape
    z, xbc, dt = jnp.split(u @ w_in, [SSD_INNER, SSD_INNER + SSD_CONV_DIM], axis=-1)
    xbc = jax.nn.silu(dwconv(xbc, conv_w) + conv_b)
    xs, bm, cm = jnp.split(xbc, [SSD_INNER, SSD_INNER + SSD_GROUPS * SSD_STATE], axis=-1)
    xs = xs.reshape(bsz, t, SSD_HEADS, SSD_HEADDIM)
    bm = bm.reshape(bsz, t, SSD_GROUPS, SSD_STATE)
    cm = cm.reshape(bsz, t, SSD_GROUPS, SSD_STATE)
    dt = jax.nn.softplus(dt.astype(f32).reshape(bsz, t, 2, SSD_HEADS) + dt_bias.astype(f32))
    a = -jnp.exp(a_log.astype(f32))
    y_f, s_f = ssd_scan(xs, dt[:, :, 0], a[0], bm, cm, s0_f)
    flip = lambda v: jnp.flip(v, axis=1)
    y_b, s_b = ssd_scan(flip(xs), flip(dt[:, :, 1]), a[1], flip(bm), flip(cm), s0_b)
    y = y_f + flip(y_b) + d_skip.astype(f32)[:, None] * xs.astype(f32)
    y = y.reshape(bsz, t, SSD_INNER) * jax.nn.silu(z.astype(f32))
    y = y.reshape(bsz, t, SSD_GROUPS, SSD_INNER // SSD_GROUPS)
    y = y * lax.rsqrt(jnp.mean(y * y, axis=-1, keepdims=True) + EPS)
    y = y.reshape(bsz, t, SSD_INNER) * norm_w.astype(f32)
    return y.astype(u.dtype) @ w_out, s_f, s_b


def shortconv_stream(u, w_in, conv_w, w_out, rows):
    bsz, t, d = u.shape
    bg, cg, hv = jnp.split(u @ w_in, 3, axis=-1)
    v = cg * hv
    if rows is None:
        v = dwconv(v, conv_w)
    else:
        v = dwconv(v.reshape(bsz * rows, GRID_W, d), conv_w).reshape(bsz, t, d)
    return (bg * v) @ w_out


def chunk_mlp_stream(u, w_in, v_norm, w_s, b_s, w_out):
    bsz, t, _ = u.shape
    zu, zv = jnp.split(jax.nn.gelu(u @ w_in), 2, axis=-1)
    zv = layernorm(zv, v_norm).reshape(bsz, t // GM_CHUNK, GM_CHUNK, GM_GROUPS, GM_GROUP_DIM)
    s = jnp.einsum('gij,bnjgc->bnigc', w_s, zv) + b_s.T[None, None, :, :, None]
    return (zu * s.reshape(bsz, t, GM_DIM)) @ w_out


def setup_inputs(seed: int = 0) -> dict:
    key = jax.random.key(seed)
    ks = iter(jax.random.split(key, 40))
    f32 = jnp.float32
    d = D_MODEL

    def nrm(shape, scale):
        return jax.random.normal(next(ks), shape, f32) * scale

    x = nrm((BATCH, SEQ, d), 1.0)
    c = nrm((BATCH, d), 1.0)
    ctx = nrm((BATCH, CTX_LEN, d), 1.0)
    c_ctx = nrm((d,), 1.0)
    ada_w = nrm((DEPTH, d, N_MOD * d), 0.5 * d ** -0.5)
    ada_b = nrm((DEPTH, N_MOD * d), 0.02)
    norm_g = 1.0 + nrm((DEPTH, 3, d), 0.02)
    ffn_wg = nrm((DEPTH, 2, d, FFN_DIM), d ** -0.5)
    ffn_wu = nrm((DEPTH, 2, d, FFN_DIM), d ** -0.5)
    ffn_wd = nrm((DEPTH, 2, FFN_DIM, d), FFN_DIM ** -0.5)
    ssd_in = nrm((N_A, d, SSD_PROJ), d ** -0.5)
    ssd_conv_w = nrm((N_A, SSD_CONV, SSD_CONV_DIM), SSD_CONV ** -0.5)
    ssd_conv_b = nrm((N_A, SSD_CONV_DIM), 0.02)
    dt_init = jnp.exp(jax.random.uniform(next(ks), (N_A, 2, SSD_HEADS), f32,
                                         minval=math.log(1e-3), maxval=math.log(1e-1)))
    ssd_dt_bias = dt_init + jnp.log(-jnp.expm1(-dt_init))
    ssd_a_log = jnp.log(jax.random.uniform(next(ks), (N_A, 2, SSD_HEADS), f32, minval=1.0, maxval=16.0))
    ssd_d = 1.0 + nrm((N_A, SSD_HEADS), 0.02)
    ssd_norm = 1.0 + nrm((N_A, SSD_INNER), 0.02)
    ssd_out = nrm((N_A, SSD_INNER, d), SSD_INNER ** -0.5)
    sc_in = nrm((N_B, d, 3 * d), d ** -0.5)
    sc_conv = nrm((N_B, SC_WIDTH, d), SC_WIDTH ** -0.5)
    sc_out = nrm((N_B, d, d), d ** -0.5)
    gm_in = nrm((N_C, d, 2 * GM_DIM), d ** -0.5)
    gm_vnorm = 1.0 + nrm((N_C, GM_DIM), 0.02)
    gm_ws = nrm((N_C, GM_GROUPS, GM_CHUNK, GM_CHUNK), GM_CHUNK ** -0.5)
    gm_bs = nrm((N_C, GM_GROUPS, GM_CHUNK), 0.02)
    gm_out = nrm((N_C, GM_DIM, d), GM_DIM ** -0.5)
    final_norm = 1.0 + nrm((d,), 0.02)
    return {"x": x, "c": c, "ctx": ctx, "c_ctx": c_ctx, "ada_w": ada_w, "ada_b": ada_b,
            "norm_g": norm_g, "ffn_wg": ffn_wg, "ffn_wu": ffn_wu, "ffn_wd": ffn_wd,
            "ssd_in": ssd_in, "ssd_conv_w": ssd_conv_w, "ssd_conv_b": ssd_conv_b,
            "ssd_dt_bias": ssd_dt_bias, "ssd_a_log": ssd_a_log, "ssd_d": ssd_d,
            "ssd_norm": ssd_norm, "ssd_out": ssd_out, "sc_in": sc_in, "sc_conv": sc_conv,
            "sc_out": sc_out, "gm_in": gm_in, "gm_vnorm": gm_vnorm, "gm_ws": gm_ws,
            "gm_bs": gm_bs, "gm_out": gm_out, "final_norm": final_norm}


def reference(x, c, ctx, c_ctx, ada_w, ada_b, norm_g, ffn_wg, ffn_wu, ffn_wd,
              ssd_in, ssd_conv_w, ssd_conv_b, ssd_dt_bias, ssd_a_log, ssd_d, ssd_norm, ssd_out,
              sc_in, sc_conv, sc_out, gm_in, gm_vnorm, gm_ws, gm_bs, gm_out, final_norm):
    rows = x.shape[1] // GRID_W
    bsz = x.shape[0]
    lat, cx = x, ctx
    for i in range(DEPTH):
        last = i == DEPTH - 1
        m_lat = jnp.split((jax.nn.silu(c) @ ada_w[i] + ada_b[i])[:, None, :], N_MOD, axis=-1)
        m_ctx = jnp.split((jax.nn.silu(c_ctx) @ ada_w[i] + ada_b[i])[None, None, :], N_MOD, axis=-1)
        lat = half_ffn(lat, m_lat[0], m_lat[1], m_lat[2], norm_g[i, 0], ffn_wg[i, 0], ffn_wu[i, 0], ffn_wd[i, 0])
        cx = half_ffn(cx, m_ctx[0], m_ctx[1], m_ctx[2], norm_g[i, 0], ffn_wg[i, 0], ffn_wu[i, 0], ffn_wd[i, 0])
        u_lat = modulate(rmsnorm(lat, norm_g[i, 1]), m_lat[3], m_lat[4])
        u_ctx = modulate(rmsnorm(cx, norm_g[i, 1]), m_ctx[3], m_ctx[4])
        kind, j = i % N_MIXERS, i // N_MIXERS
        if kind == 0:
            s0 = jnp.zeros((bsz, SSD_GROUPS, SSD_HPG, SSD_HEADDIM, SSD_STATE), jnp.float32)
            y_ctx, s_f, s_b = ssd_stream(u_ctx, ssd_in[j], ssd_conv_w[j], ssd_conv_b[j], ssd_dt_bias[j],
                                         ssd_a_log[j], ssd_d[j], ssd_norm[j], ssd_out[j], s0, s0)
            y_lat, _, _ = ssd_stream(u_lat, ssd_in[j], ssd_conv_w[j], ssd_conv_b[j], ssd_dt_bias[j],
                                     ssd_a_log[j], ssd_d[j], ssd_norm[j], ssd_out[j], s_f, s_b)
        elif kind == 1:
            y_ctx = shortconv_stream(u_ctx, sc_in[j], sc_conv[j], sc_out[j], None)
            y_lat = shortconv_stream(u_lat, sc_in[j], sc_conv[j], sc_out[j], rows)
        else:
            y_ctx = chunk_mlp_stream(u_ctx, gm_in[j], gm_vnorm[j], gm_ws[j], gm_bs[j], gm_out[j])
            y_lat = chunk_mlp_stream(u_lat, gm_in[j], gm_vnorm[j], gm_ws[j], gm_bs[j], gm_out[j])
        lat = lat + m_lat[5] * y_lat
        lat = half_ffn(lat, m_lat[6], m_lat[7], m_lat[8], norm_g[i, 2], ffn_wg[i, 1], ffn_wu[i, 1], ffn_wd[i, 1])
        if not last:
            cx = cx + m_ctx[5] * y_ctx
            cx = half_ffn(cx, m_ctx[6], m_ctx[7], m_ctx[8], norm_g[i, 2], ffn_wg[i, 1], ffn_wu[i, 1], ffn_wd[i, 1])
    return rmsnorm(lat, final_norm)
```

```python
import numpy as np
import os
import ml_dtypes
import concourse.bass as bass
import concourse.mybir as mybir
from concourse.bass_utils import run_bass_kernel_spmd
from contextlib import ExitStack

F32 = mybir.dt.float32
BF16 = mybir.dt.bfloat16
AF = mybir.ActivationFunctionType
ALU = mybir.AluOpType

PE, ACT, DVE, POOL, SP = "pe", "act", "dve", "pool", "sp"
ENGS = (PE, ACT, DVE, POOL, SP)
EPOCH = 24000

D = 1024
KD = 8
FF = 2816
NJ = 22
CT = 256
EPS = 1e-6
NCORES = 8


class Tile:
    __slots__ = ("name", "w", "r", "olds")

    def __init__(self, name):
        self.name = name
        self.w = None
        self.r = {}
        self.olds = []


class KB:
    def __init__(self, nc, es):
        self.nc = nc
        self.es = es
        self.q = {e: [] for e in ENGS}
        self.sem = {}
        self.cnt = {e: 0 for e in ENGS}
        self.epoch = {e: 0 for e in ENGS}
        self.pending = {e: False for e in ENGS}
        self.seen = {e: {} for e in ENGS}
        self.dcnt = {}
        self.final = {}
        self.tiles = {}

    def sb(self, name, shape, dt):
        return self.es.enter_context(self.nc.sbuf_tensor("sb_" + name, list(shape), dt))

    def ps(self, name, shape, dt):
        return self.es.enter_context(self.nc.psum_tensor("ps_" + name, list(shape), dt))

    def tile(self, name):
        if name not in self.tiles:
            self.tiles[name] = Tile(name)
        return self.tiles[name]

    def _sem(self, key):
        if key not in self.sem:
            nm = "s_" + "_".join(str(k) for k in key)
            self.sem[key] = self.es.enter_context(self.nc.semaphore(nm))
        return self.sem[key]

    def _deps(self, reads, writes):
        deps = {}

        def add(ev):
            if ev is None:
                return
            k, v = ev
            if deps.get(k, 0) < v:
                deps[k] = v

        def addall(t):
            add(t.w)
            for k, v in t.r.items():
                add((k, v))

        for t in reads:
            add(t.w)
        for t in writes:
            addall(t)
            for o in t.olds:
                addall(o)
            t.olds = []
        return deps

    def _waits(self, eng, deps, skip_same=True):
        waits = []
        for k, v in deps.items():
            if skip_same and k[0] == eng:
                continue
            if self.seen[eng].get(k, 0) >= v:
                continue
            self.seen[eng][k] = v
            self._sem(k)
            waits.append((k, v))
        return waits

    def op(self, eng, fn, reads=(), writes=(), inc=True):
        deps = self._deps(reads, writes)
        waits = self._waits(eng, deps, skip_same=(eng == PE))
        if inc and not self.pending[eng] and self.cnt[eng] >= EPOCH:
            self.final[(eng, self.epoch[eng])] = self.cnt[eng]
            self.epoch[eng] += 1
            self.cnt[eng] = 0
        key = (eng, self.epoch[eng])
        val = self.cnt[eng] + 1
        if inc:
            self.cnt[eng] = val
            self.pending[eng] = False
            self._sem(key)
        else:
            self.pending[eng] = True
        self.q[eng].append((waits, fn, key if inc else None, 1))
        if os.environ.get("KBLOG"):
            self.log = getattr(self, "log", [])
            self.log.append((eng, val if inc else None, [t.name for t in reads], [t.name for t in writes], list(waits)))
        for t in writes:
            t.w = (key, val)
            t.r = {}
        for t in reads:
            if t.r.get(key, 0) < val:
                t.r[key] = val

    def dma(self, eng, out, in_, key, reads=(), writes=(), chain=True, fn=None, incv=16):
        key = ("dma", key)
        deps = self._deps(reads, writes)
        if chain and self.dcnt.get(key, 0) > 0:
            if deps.get(key, 0) < self.dcnt[key]:
                deps[key] = self.dcnt[key]
        waits = self._waits(eng, deps, skip_same=False)
        val = self.dcnt.get(key, 0) + incv
        self.dcnt[key] = val
        self._sem(key)

        if fn is None:
            def fn(e, out=out, in_=in_):
                return e.dma_start(out=out, in_=in_)

        self.q[eng].append((waits, fn, key, incv))
        for t in writes:
            t.w = (key, val)
            t.r = {}
        for t in reads:
            if t.r.get(key, 0) < val:
                t.r[key] = val

    def barrier(self, engs=ENGS):
        deps = {}
        for e in ENGS:
            for ep in range(self.epoch[e] + 1):
                k = (e, ep)
                if k in self.sem:
                    v = self.cnt[e] if ep == self.epoch[e] else self.final[k]
                    if v:
                        deps[k] = v
        for k, v in self.dcnt.items():
            deps[k] = v
        for e in engs:
            assert not self.pending[e]
            waits = self._waits(e, dict(deps))
            if waits:
                self.q[e].append((waits, None, None, 0))

    def emit(self):
        block = self.es.enter_context(self.nc.Block())
        kb = self

        def run(eng, e):
            for waits, fn, key, n in kb.q[eng]:
                for k, v in waits:
                    e.wait_ge(kb.sem[k], v)
                if fn is not None:
                    ins = fn(e)
                    if key is not None:
                        ins.then_inc(kb.sem[key], n)

        @block.tensor
        def _(e):
            run(PE, e)

        @block.scalar
        def _(e):
            run(ACT, e)

        @block.vector
        def _(e):
            run(DVE, e)

        @block.gpsimd
        def _(e):
            run(POOL, e)

        @block.sync
        def _(e):
            run(SP, e)

    def stats(self):
        return {e: len(self.q[e]) for e in ENGS}, len(self.sem)


class Ring:
    def __init__(self, kb, name, nelem, dt, nsem=6):
        self.kb = kb
        self.name = name
        self.buf = kb.sb(name, [128, nelem], dt)
        self.n = nelem
        self.ptr = 0
        self.live = []
        self.c = 0
        self.nsem = nsem

    def alloc(self, n):
        assert n <= self.n
        if self.ptr + n > self.n:
            self.ptr = 0
        s, e = self.ptr, self.ptr + n
        self.ptr = e
        olds = [t for (a, b, t) in self.live if a < e and s < b]
        self.live = [(a, b, t) for (a, b, t) in self.live if not (a < e and s < b)]
        self.c += 1
        t = Tile(f"{self.name}{self.c}")
        t.olds = olds
        self.live.append((s, e, t))
        return self.buf[:, s:e], t

    def semkey(self):
        return f"{self.name}_{self.c % self.nsem}"


def mm(kb, out, lhsT, rhs, start, stop, reads, writes, inc):
    kb.op(PE, lambda e: e.matmul(out, lhsT=lhsT, rhs=rhs, start=start, stop=stop), reads, writes, inc)


def act(kb, out, in_, func, reads, writes, bias=0.0, scale=1.0, accum_out=None):
    if accum_out is None:
        kb.op(ACT, lambda e: e.activation(out=out, in_=in_, func=func, bias=bias, scale=scale), reads, writes)
    else:
        kb.op(ACT, lambda e: e.activation(out=out, in_=in_, func=func, bias=bias, scale=scale,
                                          accum_out=accum_out), reads, writes)


def tt(kb, eng, out, in0, in1, op, reads, writes):
    kb.op(eng, lambda e: e.tensor_tensor(out=out, in0=in0, in1=in1, op=op), reads, writes)


def ts(kb, eng, out, in0, s1, s2, op0, op1, reads, writes):
    if s2 is None:
        kb.op(eng, lambda e: e.tensor_single_scalar(out=out, in_=in0, scalar=s1, op=op0), reads, writes)
    else:
        kb.op(eng, lambda e: e.tensor_scalar(out=out, in0=in0, scalar1=s1, scalar2=s2, op0=op0, op1=op1),
              reads, writes)


def stt(kb, eng, out, in0, scalar, in1, op0, op1, reads, writes):
    kb.op(eng, lambda e: e.scalar_tensor_tensor(out=out, in0=in0, scalar=scalar, in1=in1, op0=op0, op1=op1),
          reads, writes)


def tcopy(kb, eng, out, in_, reads, writes):
    if eng == ACT:
        kb.op(ACT, lambda e: e.copy(out=out, in_=in_), reads, writes)
    else:
        kb.op(eng, lambda e: e.tensor_copy(out=out, in_=in_), reads, writes)


def memset(kb, eng, ap, val, writes):
    kb.op(eng, lambda e: e.memset(ap, val), (), writes)


class TokProg:
    def __init__(self, TL, stages, name="tok"):
        self.TL = TL
        self.TT = TL + CT
        self.stages = stages
        blocks = []
        t0 = 0
        while t0 < TL:
            n = min(512, TL - t0)
            blocks.append((t0, n, 0))
            t0 += n
        blocks.append((TL, CT, 1))
        self.blocks = blocks
        self.nc = bass.Bass("TRN2", target_bir_lowering=False)
        self.dram = {}
        self.build()

    def din(self, name, shape, dt=F32):
        if name not in self.dram:
            self.dram[name] = self.nc.dram_tensor(name, list(shape), dt, kind="ExternalInput").ap()
        return self.dram[name]

    def dout(self, name, shape, dt=F32):
        if name not in self.dram:
            self.dram[name] = self.nc.dram_tensor(name, list(shape), dt, kind="ExternalOutput").ap()
        return self.dram[name]

    def build(self):
        nc = self.nc
        TT = self.TT
        with ExitStack() as es:
            kb = KB(nc, es)
            self.kb = kb
            self.lat = kb.sb("lat", [128, KD, TT], F32)
            self.u = kb.sb("u", [128, KD, TT], BF16)
            self.lat_t = [kb.tile(f"lat{b}") for b in range(len(self.blocks))]
            self.u_t = [kb.tile(f"u{b}") for b in range(len(self.blocks))]
            self.wring = Ring(kb, "wr", 18432, BF16, nsem=6)
            self.sring = Ring(kb, "sr", 3072, F32)
            self.bigA = kb.sb("bigA", [128, 4096], F32)
            self.bigB = kb.sb("bigB", [128, 4096], F32)
            self.t_bigA = kb.tile("bigA")
            self.t_bigB = kb.tile("bigB")
            self.rsb = [kb.sb(f"rs{x}", [128, 512], F32) for x in range(2)]
            self.t_rsb = [kb.tile(f"rs{x}") for x in range(2)]
            self.rsi = 0
            self.banks = [kb.ps(f"bank{i}", [128, 512], F32) for i in range(8)]
            self.bank_t = [kb.tile(f"bank{i}") for i in range(8)]
            self.rot = {"G": 0, "U": 0, "O": 0, "M": 0}
            self.ones_bf = kb.sb("ones_bf", [128, 128], BF16)
            self.t_ones = kb.tile("ones_bf")
            memset(kb, POOL, self.ones_bf[:], 1.0, [self.t_ones])
            self.cvec = kb.sb("cvec", [128, KD, 2], F32)
            self.scv = kb.sb("scv", [128, KD, 2], BF16)
            self.t_scv = kb.tile("scv")
            if any(st[0] == "mod" for st in self.stages):
                tcv = kb.tile("cvec")
                kb.dma(SP, self.cvec[:], self.din("cvec", [128, KD, 2]), "cvec", writes=[tcv])
                act(kb, self.scv[:], self.cvec[:], AF.Silu, [tcv], [self.t_scv])
            self.adab = kb.sb("adab", [128, 4, 72], F32)
            self.t_adab = kb.tile("adab")
            kb.dma(SP, self.adab[:], self.din("ada_b_l", [128, 4, 72]), "adab", writes=[self.t_adab])
            self.ng = kb.sb("ng", [128, 4, 3, KD], F32)
            self.t_ng = kb.tile("ng")
            kb.dma(SP, self.ng[:], self.din("norm_g_l", [128, 4, 3, KD]), "ng", writes=[self.t_ng])
            self.modraw = [kb.sb(f"modraw{i}", [128, 72, 2], F32) for i in range(4)]
            self.modA = [kb.sb(f"modA{i}", [128, 3, KD, 2], F32) for i in range(4)]
            self.modG = [kb.sb(f"modG{i}", [128, 3, KD, 2], F32) for i in range(4)]
            self.t_mod = [kb.tile(f"mod{i}") for i in range(4)]
            lat_in = self.din("lat_in", [128, KD, TT])
            for b, (t0, n, col) in enumerate(self.blocks):
                kb.dma(SP, self.lat[:, :, t0:t0 + n], lat_in[:, :, t0:t0 + n], f"latio{b}", writes=[self.lat_t[b]])
            for st in self.stages:
                getattr(self, "st_" + st[0])(*st[1:])
            kb.barrier()
            self.stat = kb.stats()
            kb.emit()

    def bank(self, role):
        base = {"G": 0, "U": 2, "O": 4, "M": 6}[role]
        i = base + self.rot[role]
        self.rot[role] ^= 1
        return self.banks[i], self.bank_t[i]

    def wload(self, src_ap, shape_str=None, **kw):
        n = 1
        for s in src_ap.shape[1:]:
            n *= s
        ap, t = self.wring.alloc(n)
        dst = ap
        if len(src_ap.shape) == 3:
            dst = ap.rearrange("p (a b) -> p a b", a=src_ap.shape[1])
        elif len(src_ap.shape) == 4:
            dst = ap.rearrange("p (a b c) -> p a b c", a=src_ap.shape[1], b=src_ap.shape[2])
        self.kb.dma(POOL, dst, src_ap, self.wring.semkey(), writes=[t])
        return dst, t

    def scratch(self, nelem, dt=F32):
        n32 = nelem if dt == F32 else (nelem + 1) // 2
        ap, t = self.sring.alloc(n32)
        if dt != F32:
            ap = ap.bitcast(dt)[:, 0:nelem]
        return ap, t

    def st_mod(self, i):
        kb = self.kb
        par = i
        src = self.din(f"ada_w{i}", [D, 9 * D]).rearrange("(k p) c -> p k c", p=128)
        bk, bt = self.bank("M")
        for cb in range(18):
            w, wt = self.wload(src[:, :, cb * 512:(cb + 1) * 512])
            for jj in range(4):
                j = cb * 4 + jj
                for k in range(KD):
                    mm(kb, bk[:, 2 * j:2 * j + 2], w[:, k, jj * 128:(jj + 1) * 128], self.scv[:, k, :],
                       k == 0, k == KD - 1, [wt, self.t_scv], [bt], inc=(k == KD - 1))
        mr = self.modraw[par]
        tm = self.t_mod[par]
        tt(kb, DVE, mr[:], bk[:, 0:144].rearrange("p (j c) -> p j c", c=2),
           self.adab[:, i, :].unsqueeze(2).to_broadcast([128, 72, 2]), ALU.add, [bt, self.t_adab], [tm])
        self.mod_derive(i)

    def st_modin(self):
        kb = self.kb
        mi = self.din("mod_in", [128, 4, 72, 2])
        for i in range(4):
            kb.dma(SP, self.modraw[i][:], mi[:, i], f"modin{i}", writes=[self.t_mod[i]])
            tt(kb, DVE, self.modraw[i][:], self.modraw[i][:],
               self.adab[:, i, :].unsqueeze(2).to_broadcast([128, 72, 2]), ALU.add, [self.t_mod[i], self.t_adab],
               [self.t_mod[i]])
            self.mod_derive(i)

    def mod_derive(self, i):
        kb = self.kb
        par = i
        mr = self.modraw[par]
        tm = self.t_mod[par]
        for s in range(3):
            stt(kb, DVE, self.modA[par][:, s], mr[:, (3 * s + 1) * 8:(3 * s + 2) * 8, :], 1.0,
                self.ng[:, i, s, :].unsqueeze(2).to_broadcast([128, KD, 2]), ALU.add, ALU.mult,
                [tm, self.t_ng], [tm])
            ts(kb, DVE, self.modG[par][:, s], mr[:, (3 * s + 2) * 8:(3 * s + 3) * 8, :],
               0.5 if s != 1 else 1.0, None, ALU.mult, ALU.bypass, [tm], [tm])

    def mod_views(self, i, s):
        par = i
        A = self.modA[par][:, s]
        S = self.modraw[par][:, 3 * s * 8:(3 * s + 1) * 8, :]
        G = self.modG[par][:, s]
        return A, S, G, self.t_mod[par]

    def rstd_block(self, b):
        kb = self.kb
        t0, n, col = self.blocks[b]
        sq, sqt = self.bigA[:].bitcast(BF16)[:, 0:KD * n].rearrange("p (k t) -> p k t", k=KD), self.t_bigA
        act(kb, sq, self.lat[:, :, t0:t0 + n], AF.Square, [self.lat_t[b]], [sqt])
        bk, bt = self.bank("M")
        for k in range(KD):
            mm(kb, bk[:, 0:n], self.ones_bf[:], sq[:, k, :], k == 0, k == KD - 1, [self.t_ones, sqt], [bt],
               inc=(k == KD - 1))
        rs, rst = self.rsb[self.rsi][:, 0:n], self.t_rsb[self.rsi]
        self.rsi ^= 1
        act(kb, rs, bk[:, 0:n], AF.Sqrt, [bt], [rst], bias=EPS, scale=1.0 / D)
        kb.op(DVE, lambda e: e.reciprocal(out=rs, in_=rs), [rst], [rst])
        return rs, rst

    def norm_mod(self, i, s, nblocks=None):
        kb = self.kb
        A, S, G, tm = self.mod_views(i, s)
        for b, (t0, n, col) in enumerate(self.blocks[:nblocks]):
            rs, rst = self.rstd_block(b)
            for k in range(KD):
                t1, t1t = self.scratch(n)
                tt(kb, DVE, t1, self.lat[:, k, t0:t0 + n], rs, ALU.mult, [self.lat_t[b], rst], [t1t])
                act(kb, self.u[:, k, t0:t0 + n], t1, AF.Identity, [t1t, tm], [self.u_t[b]],
                    bias=S[:, k, col:col + 1], scale=A[:, k, col:col + 1])

    def st_ffn(self, i, hs, do_ctx=True):
        kb = self.kb
        s = 0 if hs == 0 else 2
        nb = len(self.blocks) if do_ctx else len(self.blocks) - 1
        self.norm_mod(i, s, nb)
        A, S, G, tm = self.mod_views(i, s)
        wg = self.din(f"wg_{i}_{hs}", [D, FF]).rearrange("(k p) f -> p k f", p=128)
        wu = self.din(f"wu_{i}_{hs}", [D, FF]).rearrange("(k p) f -> p k f", p=128)
        wd = self.din(f"wd_{i}_{hs}", [FF, D]).rearrange("(j p) m -> p j m", p=128)
        hbv = self.bigB[:].bitcast(BF16)
        hbuf = [(hbv[:, x * 1024:(x + 1) * 1024], self.kb.tile(f"bigB_h{x}")) for x in range(2)]
        self.kb.tile("bigB_h0").olds.append(self.t_bigB)
        self.kb.tile("bigB_h1").olds.append(self.t_bigB)
        prev = None
        step = 0

        def cstep(pv):
            wdt_ap, wdt_t, hb, hbt, b = pv
            t0, n, col = self.blocks[b]
            for m in range(KD):
                bk, bt = self.bank("O")
                for jj in range(2):
                    mm(kb, bk[:, 0:n], wdt_ap[:, jj, m * 128:(m + 1) * 128], hb[:, jj, 0:n], jj == 0, jj == 1,
                       [wdt_t, hbt], [bt], inc=(jj == 1))
                stt(kb, DVE, self.lat[:, m, t0:t0 + n], bk[:, 0:n], G[:, m, col:col + 1],
                    self.lat[:, m, t0:t0 + n], ALU.mult, ALU.add, [bt, tm, self.lat_t[b]], [self.lat_t[b]])

        for t in range(NJ // 2):
            wgt, wgt_t = self.wload(wg[:, :, t * 256:(t + 1) * 256])
            wut, wut_t = self.wload(wu[:, :, t * 256:(t + 1) * 256])
            wdt, wdt_t = self.wload(wd[:, 2 * t:2 * t + 2, :])
            for b in range(nb):
                t0, n, col = self.blocks[b]
                hb, hbt = hbuf[step % 2]
                hb = hb.rearrange("p (j t) -> p j t", j=2)
                step += 1
                for jj in range(2):
                    gk, gt = self.bank("G")
                    uk, ut = self.bank("U")
                    for k in range(KD):
                        mm(kb, gk[:, 0:n], wgt[:, k, jj * 128:(jj + 1) * 128], self.u[:, k, t0:t0 + n],
                           k == 0, k == KD - 1, [wgt_t, self.u_t[b]], [gt], inc=(k == KD - 1))
                    for k in range(KD):
                        mm(kb, uk[:, 0:n], wut[:, k, jj * 128:(jj + 1) * 128], self.u[:, k, t0:t0 + n],
                           k == 0, k == KD - 1, [wut_t, self.u_t[b]], [ut], inc=(k == KD - 1))
                    sg, sgt = self.scratch(n)
                    act(kb, sg, gk[:, 0:n], AF.Silu, [gt], [sgt])
                    tt(kb, DVE, hb[:, jj, 0:n], sg, uk[:, 0:n], ALU.mult, [sgt, ut], [hbt])
                if prev is not None:
                    cstep(prev)
                prev = (wdt, wdt_t, hb, hbt, b)
        cstep(prev)
        self.t_bigB.olds += [hbuf[0][1], hbuf[1][1]]

    def scratch_fixed(self, name, nelem, dt):
        if not hasattr(self, "_fixed"):
            self._fixed = {}
        if name not in self._fixed:
            self._fixed[name] = (self.kb.sb(name, [128, nelem], dt), self.kb.tile(name))
        return self._fixed[name]

    def st_umix(self, i):
        kb = self.kb
        self.norm_mod(i, 1)
        uo = self.dout("u_out", [128, KD, self.TT], BF16)
        for b, (t0, n, col) in enumerate(self.blocks):
            kb.dma(SP, uo[:, :, t0:t0 + n], self.u[:, :, t0:t0 + n], f"uio{b}", reads=[self.u_t[b]])

    def st_latout(self):
        kb = self.kb
        lo = self.dout("lat_out", [128, KD, self.TT])
        for b, (t0, n, col) in enumerate(self.blocks):
            kb.dma(SP, lo[:, :, t0:t0 + n], self.lat[:, :, t0:t0 + n], f"latio{b}", reads=[self.lat_t[b]])

    def st_final(self):
        kb = self.kb
        fn = self.kb.sb("fnorm", [128, KD], F32)
        tfn = kb.tile("fnorm")
        kb.dma(SP, fn[:], self.din("final_norm_l", [128, KD]), "fnorm", writes=[tfn])
        out = self.dout("out_fm", [128, KD, self.TL])
        for b, (t0, n, col) in enumerate(self.blocks[:-1]):
            rs, rst = self.rstd_block(b)
            for k in range(KD):
                t1, t1t = self.scratch(n)
                tt(kb, DVE, t1, self.lat[:, k, t0:t0 + n], rs, ALU.mult, [self.lat_t[b], rst], [t1t])
                act(kb, self.lat[:, k, t0:t0 + n], t1, AF.Identity, [t1t, tfn], [self.lat_t[b]],
                    scale=fn[:, k:k + 1])
            kb.dma(SP, out[:, :, t0:t0 + n], self.lat[:, :, t0:t0 + n], f"latio{b}", reads=[self.lat_t[b]])

    def st_sconv(self, i):
        kb = self.kb
        self.norm_mod(i, 1)
        A, S, G, tm = self.mod_views(i, 1)
        win = self.din("sc_in", [1, D, 3 * D])[0].rearrange("(k p) (c m q) -> p k c m q", p=128, c=3, m=KD)
        wout = self.din("sc_out", [1, D, D])[0].rearrange("(k p) m -> p k m", p=128)
        cw = kb.sb("sc_cw", [128, 3, KD], F32)
        tcw = kb.tile("sc_cw")
        kb.dma(SP, cw[:], self.din("sc_conv_l", [128, 3, KD]), "sc_cw", writes=[tcw])
        nbk = len(self.blocks)
        for b, (t0, n, col) in enumerate(self.blocks):
            z, zt = self.bigB[:].bitcast(BF16)[:, 0:KD * n].rearrange("p (k t) -> p k t", k=KD), self.t_bigB
            R = 64 if col == 0 else n
            for m in range(KD):
                wis = [self.wload(win[:, :, c, m, :]) for c in range(3)]
                pb, pbt = self.bank("G")
                pc, pct = self.bank("U")
                ph, pht = self.bank("O")
                for c, (pk, pt) in enumerate(((pb, pbt), (pc, pct), (ph, pht))):
                    wi, wit = wis[c]
                    for k in range(KD):
                        mm(kb, pk[:, 0:n], wi[:, k, :], self.u[:, k, t0:t0 + n], k == 0, k == KD - 1,
                           [wit, self.u_t[b]], [pt], inc=(k == KD - 1))
                hv, hvt = self.scratch(n)
                tcopy(kb, ACT, hv, ph[:, 0:n], [pht], [hvt])
                v, vt = self.scratch(n)
                tt(kb, DVE, v, pc[:, 0:n], hv, ALU.mult, [pct, hvt], [vt])
                vc, vct = self.scratch(n)
                ts(kb, DVE, vc, v, cw[:, 1, m:m + 1], None, ALU.mult, ALU.bypass, [vt, tcw], [vct])
                v3 = v.rearrange("p (r w) -> p r w", w=R)
                vc3 = vc.rearrange("p (r w) -> p r w", w=R)
                stt(kb, DVE, vc3[:, :, 1:R], v3[:, :, 0:R - 1], cw[:, 0, m:m + 1], vc3[:, :, 1:R],
                    ALU.mult, ALU.add, [vt, tcw, vct], [vct])
                stt(kb, DVE, vc3[:, :, 0:R - 1], v3[:, :, 1:R], cw[:, 2, m:m + 1], vc3[:, :, 0:R - 1],
                    ALU.mult, ALU.add, [vt, tcw, vct], [vct])
                tt(kb, DVE, z[:, m, :], vc, pb[:, 0:n], ALU.mult, [vct, pbt], [zt])
            wo, wot = self.wload(wout)
            for m in range(KD):
                bk, bt = self.bank("M")
                for k in range(KD):
                    mm(kb, bk[:, 0:n], wo[:, k, m * 128:(m + 1) * 128], z[:, k, :], k == 0, k == KD - 1,
                       [wot, zt], [bt], inc=(k == KD - 1))
                stt(kb, DVE, self.lat[:, m, t0:t0 + n], bk[:, 0:n], G[:, m, col:col + 1],
                    self.lat[:, m, t0:t0 + n], ALU.mult, ALU.add, [bt, tm, self.lat_t[b]], [self.lat_t[b]])

    def st_gmlp(self, i):
        kb = self.kb
        self.norm_mod(i, 1)
        A, S, G, tm = self.mod_views(i, 1)
        gin = self.din("gm_in", [1, D, 4096])[0].rearrange("(k p) c -> p k c", p=128)
        gout = self.din("gm_out", [1, 2048, D])[0].rearrange("(c p) m -> p c m", p=128)
        wsT = kb.sb("gm_wsT", [128, 8, 128], BF16)
        twsT = kb.tile("gm_wsT")
        kb.dma(POOL, wsT[:], self.din("gm_wsT_l", [128, 8, 128]), "gm_wsT", writes=[twsT])
        bsr = kb.sb("gm_bs", [1, 8, 128], BF16)
        tbsr = kb.tile("gm_bs")
        kb.dma(POOL, bsr[:], self.din("gm_bs_l", [1, 8, 128]), "gm_bs", writes=[tbsr])
        vn = kb.sb("gm_vn", [128, 2048], BF16)
        tvn = kb.tile("gm_vn")
        kb.dma(POOL, vn[:], self.din("gm_vn_l", [128, 2048]), "gm_vn", writes=[tvn])
        for b, (t0, n, col) in enumerate(self.blocks):
            nch = n // 128
            zv, zvt = self.bigA[:].bitcast(BF16)[:, 0:nch * 2048].rearrange("p (c f) -> p c f", c=nch), self.t_bigA
            ssum = kb.sb(f"gm_ssum{b}", [128, 4, 4], F32)
            ssq = kb.sb(f"gm_ssq{b}", [128, 4, 4], F32)
            tst = kb.tile(f"gm_stat{b}")
            memset(kb, POOL, ssum[:], 0.0, [tst])
            memset(kb, POOL, ssq[:], 0.0, [tst])
            for nb4 in range(4):
                wv, wvt = self.wload(gin[:, :, 2048 + nb4 * 512:2048 + (nb4 + 1) * 512])
                for c in range(nch):
                    bk, bt = self.bank("G")
                    for k in range(KD):
                        mm(kb, bk[:, :], self.u[:, k, t0 + c * 128:t0 + (c + 1) * 128], wv[:, k, :],
                           k == 0, k == KD - 1, [wvt, self.u_t[b]], [bt], inc=(k == KD - 1))
                    act(kb, zv[:, c, nb4 * 512:(nb4 + 1) * 512], bk[:, :], AF.Gelu_apprx_tanh, [bt], [zvt, tst],
                        accum_out=ssum[:, c, nb4:nb4 + 1])
                    junk, jt = self.scratch(256)
                    act(kb, junk.bitcast(BF16), zv[:, c, nb4 * 512:(nb4 + 1) * 512], AF.Square, [zvt], [jt, tst],
                        accum_out=ssq[:, c, nb4:nb4 + 1])
            st = kb.sb(f"gm_st{b}", [128, 4, 4], F32)
            kb.op(DVE, lambda e, ssum=ssum, st=st: e.reduce_sum(out=st[:, :, 0], in_=ssum[:, :, :],
                                                               axis=mybir.AxisListType.X), [tst], [tst])
            kb.op(DVE, lambda e, ssq=ssq, st=st: e.reduce_sum(out=st[:, :, 1], in_=ssq[:, :, :],
                                                             axis=mybir.AxisListType.X), [tst], [tst])
            ts(kb, DVE, st[:, :, 0], st[:, :, 0], 1.0 / 2048, None, ALU.mult, ALU.bypass, [tst], [tst])
            tt(kb, DVE, st[:, :, 2], st[:, :, 0], st[:, :, 0], ALU.mult, [tst], [tst])
            stt(kb, DVE, st[:, :, 1], st[:, :, 1], 1.0 / 2048, st[:, :, 2], ALU.mult, ALU.subtract, [tst], [tst])
            act(kb, st[:, :, 1], st[:, :, 1], AF.Sqrt, [tst], [tst], bias=EPS)
            kb.op(DVE, lambda e, st=st: e.reciprocal(out=st[:, :, 1], in_=st[:, :, 1]), [tst], [tst])
            stt(kb, DVE, st[:, :, 3], st[:, :, 0], -1.0, st[:, :, 1], ALU.mult, ALU.mult, [tst], [tst])
            for c in range(nch):
                act(kb, zv[:, c, :], zv[:, c, :], AF.Identity, [zvt, tst], [zvt],
                    bias=st[:, c, 3:4], scale=st[:, c, 1:2])
                tt(kb, POOL, zv[:, c, :], zv[:, c, :], vn[:], ALU.mult, [zvt, tvn], [zvt])
            prod, prt = self.bigB[:].bitcast(BF16)[:, 0:16 * n].rearrange("p (c t) -> p c t", c=16), self.t_bigB
            for cc in range(16):
                g = cc // 2
                sk, skt = self.bank("U")
                for c in range(nch):
                    mm(kb, sk[:, c * 128:(c + 1) * 128], zv[:, c, cc * 128:(cc + 1) * 128], wsT[:, g, :],
                       True, False, [zvt, twsT], [skt], inc=False)
                    mm(kb, sk[:, c * 128:(c + 1) * 128], self.ones_bf[0:1, :], bsr[0:1, g, :],
                       False, True, [self.t_ones, tbsr], [skt], inc=(c == nch - 1))
                wuc, wuct = self.wload(gin[:, :, cc * 128:(cc + 1) * 128])
                zk, zkt = self.bank("O")
                for k in range(KD):
                    mm(kb, zk[:, 0:n], wuc[:, k, :], self.u[:, k, t0:t0 + n], k == 0, k == KD - 1,
                       [wuct, self.u_t[b]], [zkt], inc=(k == KD - 1))
                zu, zut = self.scratch(n)
                act(kb, zu, zk[:, 0:n], AF.Gelu_apprx_tanh, [zkt], [zut])
                tt(kb, DVE, prod[:, cc, :], zu, sk[:, 0:n], ALU.mult, [zut, skt], [prt])
            for m in range(KD):
                wo, wot = self.wload(gout[:, :, m * 128:(m + 1) * 128])
                bk, bt = self.bank("M")
                for cc in range(16):
                    mm(kb, bk[:, 0:n], wo[:, cc, :], prod[:, cc, :], cc == 0, cc == 15, [wot, prt], [bt],
                       inc=(cc == 15))
                stt(kb, DVE, self.lat[:, m, t0:t0 + n], bk[:, 0:n], G[:, m, col:col + 1],
                    self.lat[:, m, t0:t0 + n], ALU.mult, ALU.add, [bt, tm, self.lat_t[b]], [self.lat_t[b]])

    def st_ssdout(self, i, j, do_ctx=True):
        kb = self.kb
        A, S, G, tm = self.mod_views(i, 1)
        yn = self.din("yn_in", [128, 16, self.TT], BF16)
        wout = self.din(f"ssd_out{j}", [2048, D]).rearrange("(c p) m -> p c m", p=128)
        nb = len(self.blocks) if do_ctx else len(self.blocks) - 1
        for b, (t0, n, col) in enumerate(self.blocks[:nb]):
            bg, yt = (self.bigA, self.t_bigA) if b % 2 == 0 else (self.bigB, self.t_bigB)
            y = bg[:].bitcast(BF16)[:, 0:16 * n].rearrange("p (c t) -> p c t", c=16)
            kb.dma(SP, y, yn[:, :, t0:t0 + n], f"ynio{b % 2}", writes=[yt])
            for m in range(KD):
                wo, wot = self.wload(wout[:, :, m * 128:(m + 1) * 128])
                bk, bt = self.bank("M")
                for cc in range(16):
                    mm(kb, bk[:, 0:n], wo[:, cc, :], y[:, cc, :], cc == 0, cc == 15, [wot, yt], [bt],
                       inc=(cc == 15))
                stt(kb, DVE, self.lat[:, m, t0:t0 + n], bk[:, 0:n], G[:, m, col:col + 1],
                    self.lat[:, m, t0:t0 + n], ALU.mult, ALU.add, [bt, tm, self.lat_t[b]], [self.lat_t[b]])


def fm(a):
    T = a.shape[0]
    return np.ascontiguousarray(a.T.reshape(KD, 128, T).transpose(1, 0, 2))


def unfm(a):
    T = a.shape[2]
    return np.ascontiguousarray(a.transpose(1, 0, 2).reshape(D, T).T)


def prep_common(inp):
    f = np.float32
    c = {}
    c["ada_w"] = np.asarray(inp["ada_w"], f)
    c["ada_b_l"] = np.ascontiguousarray(np.asarray(inp["ada_b"], f).reshape(4, 72, 128).transpose(2, 0, 1))
    c["norm_g_l"] = np.ascontiguousarray(np.asarray(inp["norm_g"], f).reshape(4, 3, KD, 128).transpose(3, 0, 1, 2))
    c["ffn_wg"] = np.asarray(inp["ffn_wg"], f)
    c["ffn_wu"] = np.asarray(inp["ffn_wu"], f)
    c["ffn_wd"] = np.asarray(inp["ffn_wd"], f)
    c["final_norm_l"] = np.ascontiguousarray(np.asarray(inp["final_norm"], f).reshape(KD, 128).T)
    c["sc_in"] = np.asarray(inp["sc_in"], f)
    c["sc_out"] = np.asarray(inp["sc_out"], f)
    c["sc_conv_l"] = np.ascontiguousarray(np.asarray(inp["sc_conv"], f)[0].reshape(3, KD, 128).transpose(2, 0, 1))
    c["gm_in"] = np.asarray(inp["gm_in"], f)
    c["gm_out"] = np.asarray(inp["gm_out"], f)
    c["gm_wsT_l"] = np.ascontiguousarray(np.asarray(inp["gm_ws"], f)[0].transpose(2, 0, 1))
    c["gm_bs_l"] = np.ascontiguousarray(np.asarray(inp["gm_bs"], f)[0][None])
    c["gm_vn_l"] = np.ascontiguousarray(np.broadcast_to(np.asarray(inp["gm_vnorm"], f)[0][None, :], (128, 2048)))
    c["ssd_out"] = np.asarray(inp["ssd_out"], f)
    return c


def cvec_for(inp, b):
    cv = np.empty((128, KD, 2), np.float32)
    cv[:, :, 0] = np.asarray(inp["c"], np.float32)[b].reshape(KD, 128).T
    cv[:, :, 1] = np.asarray(inp["c_ctx"], np.float32).reshape(KD, 128).T
    return cv


def run_prog(prog, in_maps):
    names = set(prog.dram.keys())
    maps = [{k: v for k, v in m.items() if k in names} for m in in_maps]
    for m in maps:
        missing = [k for k in names if k not in m and k not in ("u_out", "lat_out", "out_fm", "yn", "mod_out", "sprev_out")]
        assert not missing, missing
    res = run_bass_kernel_spmd(prog.nc, maps, core_ids=list(range(NCORES)))
    return res.results


NEG = -1.0e9


class SsdProg:
    def __init__(self, T, name="ssd", dbg=99, phase="both"):
        self.T = T
        self.dbg = dbg
        self.phase = phase
        self.NT = CT + T
        self.nc = bass.Bass("TRN2", target_bir_lowering=False)
        self.dram = {}
        sbs = [(0, CT, 0, CT)]
        t0 = CT
        while t0 < self.NT:
            n = min(512, self.NT - t0)
            sbs.append((t0, n, CT, self.NT))
            t0 += n
        self.sbs = sbs
        self.build()

    din = TokProg.din
    dout = TokProg.dout

    def build(self):
        nc = self.nc
        NT = self.NT
        NCH = NT // 128
        with ExitStack() as es:
            kb = KB(nc, es)
            self.kb = kb
            self.banks = [kb.ps(f"bank{i}", [128, 512], F32) for i in range(8)]
            self.bt = {}
            ident = kb.sb("ident", [128, 128], BF16); t_c = kb.tile("consts")
            Um = kb.sb("Um", [128, 128], F32); Lm = kb.sb("Lm", [128, 128], F32)
            nmf = kb.sb("nmf", [128, 128], BF16); nmb = kb.sb("nmb", [128, 128], BF16)
            ones32 = kb.sb("ones32", [128, 128], F32)
            onesr = kb.sb("onesr", [1, 128], BF16)
            for ap, op, cm, step, base, fill in ((ident, ALU.not_equal, 1, -1, 0, 1.0), (Um, ALU.is_gt, 1, -1, 0, 1.0),
                                                 (Lm, ALU.is_gt, -1, 1, 0, 1.0), (nmf, ALU.is_gt, -1, 1, 1, NEG),
                                                 (nmb, ALU.is_gt, 1, -1, 1, NEG)):
                memset(kb, POOL, ap[:], 0.0, [t_c])
                kb.op(POOL, lambda e, ap=ap, op=op, cm=cm, step=step, base=base, fill=fill: e.affine_select(
                    out=ap[:], in_=ap[:], compare_op=op, fill=fill, base=base, pattern=[[step, 128]],
                    channel_multiplier=cm), [t_c], [t_c])
            memset(kb, POOL, ones32[:], 1.0, [t_c])
            memset(kb, POOL, onesr[:], 1.0, [t_c])
            self.onesr5 = kb.sb("onesr5", [1, 512], BF16)
            memset(kb, POOL, self.onesr5[:], 1.0, [t_c])
            self.ident, self.Um, self.Lm, self.nmf, self.nmb, self.ones32, self.onesr = ident, Um, Lm, nmf, nmb, ones32, onesr
            self.t_c = t_c
            t_p = kb.tile("params")
            self.t_p = t_p
            wx = kb.sb("wx", [128, KD, 768], BF16)
            wz = kb.sb("wz", [128, KD, 512], BF16)
            wdt = kb.sb("wdt", [128, KD, 16], BF16)
            kb.dma(POOL, wx[:], self.din("w_x", [D, 768]).rearrange("(k p) c -> p k c", p=128), "p_wx", writes=[t_p])
            kb.dma(POOL, wz[:], self.din("w_z", [D, 512]).rearrange("(k p) c -> p k c", p=128), "p_wz", writes=[t_p])
            kb.dma(POOL, wdt[:], self.din("w_dt", [D, 16]).rearrange("(k p) c -> p k c", p=128), "p_wdt", writes=[t_p])
            cwp = kb.sb("cwp", [128, 6, 5], F32)
            cbp = kb.sb("cbp", [128, 6], F32)
            cbr = kb.sb("cbr", [1, 768], BF16)
            dtb = kb.sb("dtb", [128, 16], F32)
            abc = kb.sb("abc", [128, 16], F32)
            dsk = kb.sb("dsk", [128, 8], F32)
            nwb = kb.sb("nwb", [128, 512], F32)
            kb.dma(SP, cwp[:], self.din("cw_p", [128, 6, 5]), "p_cwp", writes=[t_p])
            kb.dma(SP, cbp[:], self.din("cb_p", [128, 6]), "p_cbp", writes=[t_p])
            kb.dma(POOL, cbr[:], self.din("cb_r", [1, 768]), "p_cbr", writes=[t_p])
            kb.dma(SP, dtb[:], self.din("dtb_bc", [128, 16]), "p_dtb", writes=[t_p])
            kb.dma(SP, abc[:], self.din("alog_bc", [128, 16]), "p_abc", writes=[t_p])
            kb.dma(SP, dsk[:], self.din("dsk_bc", [128, 8]), "p_dsk", writes=[t_p])
            kb.dma(SP, nwb[:], self.din("nw_bc", [128, 512]), "p_nwb", writes=[t_p])
            act(kb, abc[:], abc[:], AF.Exp, [t_p], [t_p])
            ts(kb, DVE, abc[:], abc[:], -1.0, None, ALU.mult, ALU.bypass, [t_p], [t_p])
            diag = kb.sb("diag", [128, 6, 5, 128], BF16)
            for fc in range(6):
                for tap in range(5):
                    ts(kb, DVE, diag[:, fc, tap, :], ident[:], cwp[:, fc, tap:tap + 1], None, ALU.mult, ALU.bypass,
                       [t_c, t_p], [t_p])
            self.wx, self.wz, self.wdt, self.cbp, self.cbr, self.dtb, self.abc, self.dsk, self.nwb, self.diag = \
                wx, wz, wdt, cbp, cbr, dtb, abc, dsk, nwb, diag
            self.Sprev_b = kb.sb("Sprev_b", [128, NCH, 512], BF16)
            self.t_Sprev = [kb.tile(f"Sprev{c}") for c in range(NCH)]
            self.S = [kb.sb(f"S{d}", [128, 512], F32) for d in range(2)]
            self.Sbf = kb.sb("Sbf", [128, 512], BF16)
            self.t_S = [kb.tile(f"S{d}") for d in range(2)]
            self.t_Sbf = kb.tile("Sbf")
            memset(kb, DVE, self.S[0][:], 0.0, [self.t_S[0]])
            memset(kb, DVE, self.S[1][:], 0.0, [self.t_S[1]])
            memset(kb, DVE, self.Sbf[:], 0.0, [self.t_Sbf])
            self.ublk = [kb.sb(f"ublk{x}", [128, KD, 516], BF16) for x in range(2)]
            self.t_ublk = [kb.tile(f"ublk{x}") for x in range(2)]
            self.pre = kb.sb("pre", [128, 6, 516], BF16); self.t_pre = kb.tile("pre")
            self.Bfm = kb.sb("Bfm", [128, 512], BF16); self.Cfm = kb.sb("Cfm", [128, 512], BF16)
            self.t_BC = kb.tile("BCfm")
            self.Braw = kb.sb("Braw", [128, 512], F32); self.Craw = kb.sb("Craw", [128, 512], F32)
            self.t_BCraw = kb.tile("BCraw")
            self.ynT = kb.sb("ynT", [128, 4, 512], BF16); self.t_ynT = kb.tile("ynT")
            self.ui = 0
            u_all = self.din("u_all", [128, KD, NT], BF16)
            yn = self.dout("yn", [128, 4, NT], BF16)
            order = [self.sbs[0]] + list(reversed(self.sbs[1:]))
            if self.phase == "main":
                order = []
                kb.dma(SP, self.Sprev_b[:], self.din("sprev_in", [128, NCH, 512], BF16), "sprev", writes=self.t_Sprev)
            for (s0, ns, lo, hi) in order:
                ub, ubt = self.load_u(u_all, s0, ns, lo, hi)
                self.preconv(ub, ubt, ns, fcs=(0, 1, 2, 3, 4))
                for c in reversed(range(ns // 128)):
                    if self.dbg >= 2:
                        self.chunk(ub, ubt, s0, c, main=False)
                if s0 == 0:
                    pass
            if self.phase == "pre":
                kb.dma(SP, self.dout("sprev_out", [128, NCH, 512], BF16), self.Sprev_b[:], "sprev", reads=self.t_Sprev)
            for (s0, ns, lo, hi) in (self.sbs if self.phase != "pre" else []):
                ub, ubt = self.load_u(u_all, s0, ns, lo, hi)
                self.preconv(ub, ubt, ns, fcs=(0, 1, 2, 3, 4, 5) if not os.environ.get("V2") else (0, 1, 2, 3, 4))
                if not os.environ.get("V1"):
                    self.conv_fm(ns)
                if os.environ.get("V3"):
                    kb.barrier()
                for c in range(ns // 128):
                    if self.dbg == 29:
                        self.chunk(ub, ubt, s0, c, main=False)
                    elif self.dbg >= 3:
                        self.chunk(ub, ubt, s0, c, main=True)
                if self.dbg < 3 or self.dbg == 29:
                    memset(kb, DVE, self.ynT[:], 0.0, [self.t_ynT])
                kb.dma(SP, yn[:, :, s0:s0 + ns], self.ynT[:, :, 0:ns], "ynout", reads=[self.t_ynT])
            kb.barrier()
            self.stat = kb.stats()
            kb.emit()

    def B(self, i, name):
        return self.banks[i], self.kb.tile(f"bank{i}")

    def load_u(self, u_all, s0, ns, lo, hi):
        kb = self.kb
        x = self.ui
        self.ui ^= 1
        ub, ubt = self.ublk[x], self.t_ublk[x]
        a = max(s0 - 2, lo)
        b = min(s0 + ns + 2, hi)
        if a > s0 - 2:
            memset(kb, DVE, ub[:, :, 0:2], 0.0, [ubt])
        if b < s0 + ns + 2:
            memset(kb, DVE, ub[:, :, ns + 2:ns + 4], 0.0, [ubt])
        kb.dma(SP, ub[:, :, a - (s0 - 2):b - (s0 - 2)], u_all[:, :, a:b], f"uload{x}", writes=[ubt])
        return ub, ubt

    def preconv(self, ub, ubt, ns, fcs):
        kb = self.kb
        for n_i, fc in enumerate(fcs):
            bk, bt = self.B(0 if n_i % 2 == 0 else 7, "pre")
            bh, bht = self.B(1, "small")
            for k in range(KD):
                mm(kb, bk[:, 0:ns], self.wx[:, k, fc * 128:(fc + 1) * 128], ub[:, k, 0:ns], k == 0, k == KD - 1,
                   [self.t_p, ubt], [bt], inc=(k == KD - 1))
            for k in range(KD):
                mm(kb, bh[:, 0:4], self.wx[:, k, fc * 128:(fc + 1) * 128], ub[:, k, ns:ns + 4], k == 0, k == KD - 1,
                   [self.t_p, ubt], [bht], inc=(k == KD - 1))
            tcopy(kb, ACT, self.pre[:, fc, 0:ns], bk[:, 0:ns], [bt], [self.t_pre])
            tcopy(kb, DVE, self.pre[:, fc, ns:ns + 4], bh[:, 0:4], [bht], [self.t_pre])

    def conv_fm(self, ns):
        kb = self.kb
        for fc, dst in ((4, self.Bfm), (5, self.Cfm)):
            bk, bt = self.B(0 if fc == 4 else 7, "pre")
            for tap in range(5):
                mm(kb, bk[:, 0:ns], self.diag[:, fc, tap, :], self.pre[:, fc, tap:tap + ns], tap == 0, False,
                   [self.t_p, self.t_pre], [bt], inc=False)
            mm(kb, bk[:, 0:ns], self.cbr[0:1, fc * 128:(fc + 1) * 128], self.onesr5[0:1, 0:ns], False, True,
               [self.t_p, self.t_c], [bt], inc=True)
            raw = self.Braw if fc == 4 else self.Craw
            act(kb, raw[:, 0:ns], bk[:, 0:ns], AF.Identity, [bt], [self.t_BCraw])
            if os.environ.get("V7"):
                dm, dmt = self.tmp("dummy", [128, 8], F32)
                act(kb, dm[:], self.cbp[:, 0:6].rearrange("p a -> p a")[:, 0:6] if False else self.dtb[:, 0:8], AF.Identity, [self.t_p], [dmt])

    def tmp(self, name, shape, dt):
        name = f"{name}_{getattr(self, 'par', 0)}"
        if not hasattr(self, "_tmp"):
            self._tmp = {}
        if name not in self._tmp:
            self._tmp[name] = (self.kb.sb(name, shape, dt), self.kb.tile(name))
        return self._tmp[name]

    def chunk(self, ub, ubt, s0, c, main):
        kb = self.kb
        co = c * 128
        gch = (s0 + co) // 128
        self.par = gch % 2
        ident, onesr = self.ident, self.onesr
        tc_, tp = self.t_c, self.t_p
        bx, bxt = self.B(2, "xs")
        for fc in range(4):
            for tap in range(5):
                mm(kb, bx[:, fc * 128:(fc + 1) * 128], self.pre[:, fc, co + tap:co + tap + 128],
                   self.diag[:, fc, tap, :], tap == 0, False, [self.t_pre, tp], [bxt], inc=False)
            mm(kb, bx[:, fc * 128:(fc + 1) * 128], onesr[0:1, :], self.cbr[0:1, fc * 128:(fc + 1) * 128],
               False, True, [tc_, tp], [bxt], inc=(fc == 3))
        xs, xst = self.tmp("xs", [128, 512], F32)
        act(kb, xs[:], bx[:, :], AF.Silu, [bxt], [xst])
        bs, bst = self.B(1, "small")
        for tap in range(5):
            mm(kb, bs[:, 128:256], self.pre[:, 4, co + tap:co + tap + 128], self.diag[:, 4, tap, :],
               tap == 0, False, [self.t_pre, tp], [bst], inc=False)
        mm(kb, bs[:, 128:256], onesr[0:1, :], self.cbr[0:1, 512:640], False, True, [tc_, tp], [bst], inc=False)
        for k in range(KD):
            mm(kb, bs[:, 16:32], ub[:, k, 2 + co:2 + co + 128], self.wdt[:, k, :], k == 0, k == KD - 1,
               [ubt, tp], [bst], inc=(k == KD - 1))
        Btm, Btmt = self.tmp("Btm", [128, 128], BF16)
        act(kb, Btm[:], bs[:, 128:256], AF.Silu, [bst], [Btmt])
        if main:
            act(kb, self.Bfm[:, co:co + 128], self.Braw[:, co:co + 128], AF.Silu, [self.t_BCraw], [self.t_BC])
            act(kb, self.Cfm[:, co:co + 128], self.Craw[:, co:co + 128], AF.Silu, [self.t_BCraw], [self.t_BC])
        sm, smt = self.tmp("sm", [128, 8, 16], F32)
        tt(kb, DVE, sm[:, 0, :], bs[:, 16:32], self.dtb[:], ALU.add, [bst, tp], [smt])
        ts(kb, DVE, sm[:, 1, :], sm[:, 0, :], -1.0, None, ALU.mult, ALU.bypass, [smt], [smt])
        tt(kb, DVE, sm[:, 1, :], sm[:, 1, :], sm[:, 0, :], ALU.max, [smt], [smt])
        act(kb, sm[:, 1, :], sm[:, 1, :], AF.Exp, [smt], [smt], scale=-1.0)
        act(kb, sm[:, 1, :], sm[:, 1, :], AF.Ln, [smt], [smt], bias=1.0)
        stt(kb, DVE, sm[:, 2, :], sm[:, 0, :], 0.0, sm[:, 1, :], ALU.max, ALU.add, [smt], [smt])
        tt(kb, DVE, sm[:, 3, :], sm[:, 2, :], self.abc[:], ALU.mult, [smt, tp], [smt])
        mm(kb, bs[:, 32:40], self.Um[:], sm[:, 3, 0:8], True, True, [tc_, smt], [bst], inc=False)
        mm(kb, bs[:, 40:48], self.Lm[:], sm[:, 3, 8:16], True, True, [tc_, smt], [bst], inc=False)
        mm(kb, bs[:, 48:64], self.ones32[:], sm[:, 3, :], True, True, [tc_, smt], [bst], inc=True)
        sm2, sm2t = self.tmp("sm2", [128, 6, 16], F32)
        tcopy(kb, DVE, sm2[:, 0, :], bs[:, 32:48], [bst], [sm2t])
        ts(kb, DVE, sm2[:, 1, :], sm2[:, 0, :], -1.0, None, ALU.mult, ALU.bypass, [sm2t], [sm2t])
        act(kb, sm2[:, 2, :], sm2[:, 0, :], AF.Exp, [sm2t], [sm2t])
        tt(kb, DVE, sm2[:, 3, :], bs[:, 48:64], sm2[:, 0, :], ALU.subtract, [bst, sm2t], [sm2t])
        act(kb, sm2[:, 3, :], sm2[:, 3, :], AF.Exp, [sm2t], [sm2t])
        act(kb, sm2[:, 4, :], bs[:, 48:64], AF.Exp, [bst], [sm2t])
        tt(kb, DVE, sm2[:, 5, :], sm2[:, 3, :], sm[:, 2, :], ALU.mult, [sm2t, smt], [sm2t])
        xs3 = xs[:].rearrange("p (h q) -> p h q", h=8)

        def bc(ap):
            return ap.unsqueeze(2).to_broadcast([128, 8, 64])

        dirs = (0,) if main else (1,)
        xte = {}
        for d in dirs:
            xt_, xtt = self.tmp(f"xte{d}", [128, 8, 64], BF16)
            tt(kb, POOL, xt_[:], xs3, bc(sm2[:, 5, d * 8:(d + 1) * 8]), ALU.mult,
               [xst, sm2t], [xtt])
            xte[d] = (xt_, xtt)
        if main:
            if self.dbg == 30:
                memset(kb, DVE, self.ynT[:, :, co:co + 128], 0.0, [self.t_ynT])
                return
            bz, bzt = self.B(3, "z")
            for k in range(KD):
                mm(kb, bz[:, :], ub[:, k, 2 + co:2 + co + 128], self.wz[:, k, :], k == 0, k == KD - 1,
                   [ubt, tp], [bzt], inc=(k == KD - 1))
            sz, szt = self.tmp("sz", [128, 512], F32)
            act(kb, sz[:], bz[:, :], AF.Silu, [bzt], [szt])
            if self.dbg == 31:
                memset(kb, DVE, self.ynT[:, :, co:co + 128], 0.0, [self.t_ynT])
                return
            xdt = {}
            for d in (0, 1):
                x_, x_t = self.tmp(f"xdt{d}", [128, 8, 64], BF16)
                tt(kb, DVE if d == 0 else POOL, x_[:], xs3, bc(sm[:, 2, d * 8:(d + 1) * 8]), ALU.mult,
                   [xst, smt], [x_t])
                xdt[d] = (x_, x_t)
            xsd, xsdt = self.tmp("xsd", [128, 8, 64], BF16)
            tt(kb, POOL, xsd[:], xs3, bc(self.dsk[:]), ALU.mult, [xst, tp], [xsdt])
            if self.dbg == 32:
                memset(kb, DVE, self.ynT[:, :, co:co + 128], 0.0, [self.t_ynT])
                return
            mm(kb, bs[:, 256:384], self.Bfm[:, co:co + 128], self.Cfm[:, co:co + 128], True, True,
               [self.t_BC], [bst], inc=True)
            cbT, cbTt = self.tmp("cbT", [128, 128], F32)
            tcopy(kb, ACT, cbT[:], bs[:, 256:384], [bst], [cbTt])
            if self.dbg == 33:
                memset(kb, DVE, self.ynT[:, :, co:co + 128], 0.0, [self.t_ynT])
                return
            W = {}
            for d in (0, 1):
                for hq in range(2):
                    br, brt = self.B(4 + (d * 2 + hq) % 2, "R")
                    for hh in range(4):
                        h = hq * 4 + hh
                        col = d * 8 + h
                        mm(kb, br[:, hh * 128:(hh + 1) * 128], sm[:, 3, col:col + 1].to_broadcast([128, 128]),
                           self.Um[:] if d == 0 else self.Lm[:], True, False, [smt, tc_], [brt], inc=False)
                        mm(kb, br[:, hh * 128:(hh + 1) * 128], ident[:], (self.nmf if d == 0 else self.nmb)[:],
                           False, True, [tc_], [brt], inc=(hh == 3))
                    Dm, Dmt = self.tmp(f"Dm{hq}", [128, 4, 128], F32)
                    for hh in range(4):
                        col = d * 8 + hq * 4 + hh
                        act(kb, Dm[:, hh, :], br[:, hh * 128:(hh + 1) * 128], AF.Exp, [brt, sm2t], [Dmt],
                            bias=sm2[:, 1, col:col + 1])
                    Wt, Wtt = self.tmp(f"W{d}{hq}", [128, 4, 128], BF16)
                    tt(kb, DVE, Wt[:], Dm[:], cbT[:].unsqueeze(1).to_broadcast([128, 4, 128]), ALU.mult,
                       [Dmt, cbTt], [Wtt])
                    W[(d, hq)] = (Wt, Wtt)
            if self.dbg == 3:
                memset(kb, DVE, self.ynT[:, :, co:co + 128], 0.0, [self.t_ynT])
                return
            by, byt = self.B(6, "Y")
            for h in range(8):
                hq, hh = h // 4, h % 4
                mm(kb, by[:, h * 64:(h + 1) * 64], W[(0, hq)][0][:, hh, :], xdt[0][0][:, h, :], True, False,
                   [W[(0, hq)][1], xdt[0][1]], [byt], inc=False)
                mm(kb, by[:, h * 64:(h + 1) * 64], W[(1, hq)][0][:, hh, :], xdt[1][0][:, h, :], False, False,
                   [W[(1, hq)][1], xdt[1][1]], [byt], inc=False)
                mm(kb, by[:, h * 64:(h + 1) * 64], ident[:], xsd[:, h, :], False, True, [tc_, xsdt], [byt],
                   inc=(h == 7))
            if self.dbg == 4:
                memset(kb, DVE, self.ynT[:, :, co:co + 128], 0.0, [self.t_ynT])
                return
            bo0, bo0t = self.B(2, "xs")
            mm(kb, bo0[:, :], self.Cfm[:, co:co + 128], self.Sbf[:], True, True, [self.t_BC, self.t_Sbf], [bo0t],
               inc=True)
            bo1, bo1t = self.B(3, "z")
            mm(kb, bo1[:, :], self.Cfm[:, co:co + 128], self.Sprev_b[:, gch, :], True, True,
               [self.t_BC, self.t_Sprev[gch]], [bo1t], inc=True)
            y, yt_ = self.tmp("y", [128, 512], F32)
            t1, t1t = self.tmp("t1", [128, 512], F32)
            y3 = y[:].rearrange("p (h q) -> p h q", h=8)
            t13 = t1[:].rearrange("p (h q) -> p h q", h=8)
            tt(kb, DVE, y3, bo0[:, :].rearrange("p (h q) -> p h q", h=8), bc(sm2[:, 2, 0:8]), ALU.mult,
               [bo0t, sm2t], [yt_])
            tt(kb, DVE, t13, bo1[:, :].rearrange("p (h q) -> p h q", h=8), bc(sm2[:, 2, 8:16]), ALU.mult,
               [bo1t, sm2t], [t1t])
            tt(kb, DVE, y[:], y[:], by[:, :], ALU.add, [yt_, byt], [yt_])
            tt(kb, POOL, y[:], y[:], t1[:], ALU.add, [yt_, t1t], [yt_])
            if self.dbg == 5:
                memset(kb, DVE, self.ynT[:, :, co:co + 128], 0.0, [self.t_ynT])
                return
            tt(kb, DVE, y[:], y[:], sz[:], ALU.mult, [yt_, szt], [yt_])
            ss, sst = self.tmp("ss", [128, 2], F32)
            memset(kb, POOL, ss[:], 0.0, [sst])
            act(kb, t1[:], y[:], AF.Square, [yt_, t1t], [t1t, sst], accum_out=ss[:, 0:1])
            act(kb, ss[:, 1:2], ss[:, 0:1], AF.Sqrt, [sst], [sst], bias=EPS, scale=1.0 / 512)
            kb.op(DVE, lambda e, ss=ss: e.reciprocal(out=ss[:, 1:2], in_=ss[:, 1:2]), [sst], [sst])
            ynb, ynbt = self.tmp("ynb", [128, 512], BF16)
            stt(kb, DVE, ynb[:], y[:], ss[:, 1:2], self.nwb[:], ALU.mult, ALU.mult, [yt_, sst, tp], [ynbt])
            if self.dbg == 6:
                memset(kb, DVE, self.ynT[:, :, co:co + 128], 0.0, [self.t_ynT])
                return
            btr, btrt = self.B(0, "pre")
            trv = btr[:, 0:256].bitcast(BF16)
            for fc in range(4):
                kb.op(PE, lambda e, fc=fc, trv=trv, ynb=ynb: e.transpose(trv[:, fc * 128:(fc + 1) * 128],
                                                                       ynb[:, fc * 128:(fc + 1) * 128], ident[:]),
                      [ynbt, tc_], [btrt], inc=(fc == 3))
            tcopy(kb, ACT, self.ynT[:, :, co:co + 128], trv.rearrange("p (f t) -> p f t", f=4), [btrt], [self.t_ynT])
        for d in dirs:
            if d == 1:
                tcopy(kb, ACT, self.Sprev_b[:, gch, :], self.S[1][:], [self.t_S[1]], [self.t_Sprev[gch]])
            bst_, bstt = self.B(7 if d == 0 else 6, "St")
            if not main:
                bst_, bstt = self.B(6, "St")
            mm(kb, bst_[:, :], Btm[:], xte[d][0][:].rearrange("p h q -> p (h q)"), True, True,
               [Btmt, xte[d][1]], [bstt], inc=True)
            S3 = self.S[d][:].rearrange("p (h q) -> p h q", h=8)
            tt(kb, DVE, S3, S3, bc(sm2[:, 4, d * 8:(d + 1) * 8]), ALU.mult, [self.t_S[d], sm2t], [self.t_S[d]])
            tt(kb, DVE, self.S[d][:], self.S[d][:], bst_[:, :], ALU.add, [self.t_S[d], bstt], [self.t_S[d]])
            if d == 0:
                tcopy(kb, ACT, self.Sbf[:], self.S[0][:], [self.t_S[0]], [self.t_Sbf])


def prep_ssd(inp, j, g):
    f = np.float32
    w_in = np.asarray(inp["ssd_in"], f)[j]
    m = {}
    m["w_z"] = np.ascontiguousarray(w_in[:, g * 512:(g + 1) * 512])
    m["w_x"] = np.ascontiguousarray(np.concatenate([
        w_in[:, 2048 + g * 512:2048 + (g + 1) * 512],
        w_in[:, 4096 + g * 128:4096 + (g + 1) * 128],
        w_in[:, 4608 + g * 128:4608 + (g + 1) * 128]], axis=1))
    m["w_dt"] = np.ascontiguousarray(np.concatenate([
        w_in[:, 5120 + g * 8:5120 + (g + 1) * 8],
        w_in[:, 5152 + g * 8:5152 + (g + 1) * 8]], axis=1))
    ch = np.concatenate([np.arange(g * 512, (g + 1) * 512), 2048 + np.arange(g * 128, (g + 1) * 128),
                         2560 + np.arange(g * 128, (g + 1) * 128)])
    cw = np.asarray(inp["ssd_conv_w"], f)[j][:, ch]
    cb = np.asarray(inp["ssd_conv_b"], f)[j][ch]
    m["cw_p"] = np.ascontiguousarray(cw.reshape(5, 6, 128).transpose(2, 1, 0))
    m["cb_p"] = np.ascontiguousarray(cb.reshape(6, 128).T)
    m["cb_r"] = np.ascontiguousarray(cb[None, :])
    hs = slice(g * 8, (g + 1) * 8)
    dtb = np.asarray(inp["ssd_dt_bias"], f)[j][:, hs].reshape(16)
    alog = np.asarray(inp["ssd_a_log"], f)[j][:, hs].reshape(16)
    m["dtb_bc"] = np.ascontiguousarray(np.broadcast_to(dtb[None], (128, 16)))
    m["alog_bc"] = np.ascontiguousarray(np.broadcast_to(alog[None], (128, 16)))
    m["dsk_bc"] = np.ascontiguousarray(np.broadcast_to(np.asarray(inp["ssd_d"], f)[j][hs][None], (128, 8)))
    m["nw_bc"] = np.ascontiguousarray(np.broadcast_to(np.asarray(inp["ssd_norm"], f)[j][g * 512:(g + 1) * 512][None], (128, 512)))
    return m


class ModProg:
    def __init__(self):
        self.nc = bass.Bass("TRN2", target_bir_lowering=False)
        self.dram = {}
        nc = self.nc
        with ExitStack() as es:
            kb = KB(nc, es)
            self.kb = kb
            cv = kb.sb("cv3", [128, KD, 3], F32); tcv = kb.tile("cv3")
            scv = kb.sb("scv3", [128, KD, 3], BF16); tscv = kb.tile("scv3")
            kb.dma(SP, cv[:], self.din("cvec3", [128, KD, 3]), "cv3", writes=[tcv])
            act(kb, scv[:], cv[:], AF.Silu, [tcv], [tscv])
            bank = kb.ps("bank0", [128, 512], F32); tb = kb.tile("bank0")
            res = kb.sb("res", [128, 4, 9, 3], F32); tres = kb.tile("res")
            aw = self.din("ada_w_sh", [4, D, 1152])
            ws = [kb.sb(f"w{x}", [128, KD, 1152], BF16) for x in range(2)]
            for i in range(4):
                w = ws[i % 2]; wt = kb.tile(f"w{i % 2}")
                kb.dma(POOL, w[:], aw[i].rearrange("(k p) c -> p k c", p=128), f"w{i % 2}", writes=[wt])
                for jj in range(9):
                    for k in range(KD):
                        mm(kb, bank[:, (i * 9 + jj) * 3:(i * 9 + jj) * 3 + 3], w[:, k, jj * 128:(jj + 1) * 128],
                           scv[:, k, :], k == 0, k == KD - 1, [wt, tscv], [tb], inc=(k == KD - 1))
            tcopy(kb, DVE, res[:], bank[:, 0:108].rearrange("p (i j c) -> p i j c", i=4, j=9), [tb], [tres])
            kb.dma(SP, self.dout("mod_part", [128, 4, 9, 3]), res[:], "res", reads=[tres])
            kb.barrier()
            kb.emit()

    din = TokProg.din
    dout = TokProg.dout


_PROGS = {}


def _prog(key, ctor):
    if key not in _PROGS:
        _PROGS[key] = ctor()
    return _PROGS[key]


def run_prog(prog, in_maps):
    names = set(prog.dram.keys())
    outs = ("u_out", "lat_out", "out_fm", "yn", "mod_part", "sprev_out")
    maps = [{k: v for k, v in m.items() if k in names} for m in in_maps]
    for m in maps:
        missing = [k for k in names if k not in m and k not in outs]
        assert not missing, missing
    res = run_bass_kernel_spmd(prog.nc, maps, core_ids=list(range(NCORES)))
    return res.results


def kernel(x, c, ctx, c_ctx, ada_w, ada_b, norm_g, ffn_wg, ffn_wu, ffn_wd,
           ssd_in, ssd_conv_w, ssd_conv_b, ssd_dt_bias, ssd_a_log, ssd_d, ssd_norm, ssd_out,
           sc_in, sc_conv, sc_out, gm_in, gm_vnorm, gm_ws, gm_bs, gm_out, final_norm):
    f = np.float32
    inp = dict(x=x, c=c, ctx=ctx, c_ctx=c_ctx, ada_w=ada_w, ada_b=ada_b, norm_g=norm_g, ffn_wg=ffn_wg,
               ffn_wu=ffn_wu, ffn_wd=ffn_wd, ssd_in=ssd_in, ssd_conv_w=ssd_conv_w, ssd_conv_b=ssd_conv_b,
               ssd_dt_bias=ssd_dt_bias, ssd_a_log=ssd_a_log, ssd_d=ssd_d, ssd_norm=ssd_norm, ssd_out=ssd_out,
               sc_in=sc_in, sc_conv=sc_conv, sc_out=sc_out, gm_in=gm_in, gm_vnorm=gm_vnorm, gm_ws=gm_ws,
               gm_bs=gm_bs, gm_out=gm_out, final_norm=final_norm)
    inp = {k: np.asarray(v) for k, v in inp.items()}
    B, SEQ = inp["x"].shape[0], inp["x"].shape[1]
    TL = SEQ // 4
    bf = ml_dtypes.bfloat16
    com = {}
    com["ada_b_l"] = np.ascontiguousarray(inp["ada_b"].astype(f).reshape(4, 72, 128).transpose(2, 0, 1))
    com["norm_g_l"] = np.ascontiguousarray(inp["norm_g"].astype(f).reshape(4, 3, KD, 128).transpose(3, 0, 1, 2))
    for i in range(4):
        for hs in range(2):
            com[f"wg_{i}_{hs}"] = np.ascontiguousarray(inp["ffn_wg"][i, hs], f)
            com[f"wu_{i}_{hs}"] = np.ascontiguousarray(inp["ffn_wu"][i, hs], f)
            com[f"wd_{i}_{hs}"] = np.ascontiguousarray(inp["ffn_wd"][i, hs], f)
    com["final_norm_l"] = np.ascontiguousarray(inp["final_norm"].astype(f).reshape(KD, 128).T)
    com["sc_in"] = inp["sc_in"].astype(f)
    com["sc_out"] = inp["sc_out"].astype(f)
    com["sc_conv_l"] = np.ascontiguousarray(inp["sc_conv"].astype(f)[0].reshape(3, KD, 128).transpose(2, 0, 1))
    com["gm_in"] = inp["gm_in"].astype(f)
    com["gm_out"] = inp["gm_out"].astype(f)
    com["gm_wsT_l"] = np.ascontiguousarray(inp["gm_ws"].astype(f)[0].transpose(2, 0, 1))
    com["gm_bs_l"] = np.ascontiguousarray(inp["gm_bs"].astype(f)[0][None])
    com["gm_vn_l"] = np.ascontiguousarray(np.broadcast_to(inp["gm_vnorm"].astype(f)[0][None, :], (128, 2048)))
    com["ssd_out0"] = np.ascontiguousarray(inp["ssd_out"][0], f)
    com["ssd_out1"] = np.ascontiguousarray(inp["ssd_out"][1], f)

    cv3 = np.empty((128, KD, 3), f)
    cv3[:, :, 0] = inp["c"].astype(f)[0].reshape(KD, 128).T
    cv3[:, :, 1] = inp["c"].astype(f)[1].reshape(KD, 128).T
    cv3[:, :, 2] = inp["c_ctx"].astype(f).reshape(KD, 128).T
    mp = _prog("mod", ModProg)
    aw = inp["ada_w"].astype(f)
    rm = run_prog(mp, [{"cvec3": cv3, "ada_w_sh": np.ascontiguousarray(aw[:, :, r * 1152:(r + 1) * 1152])}
                       for r in range(NCORES)])
    mod_in = []
    for b in range(B):
        m = np.empty((128, 4, 72, 2), f)
        for r in range(NCORES):
            m[:, :, r * 9:(r + 1) * 9, 0] = rm[r]["mod_part"][:, :, :, b]
            m[:, :, r * 9:(r + 1) * 9, 1] = rm[r]["mod_part"][:, :, :, 2]
        mod_in.append(m)

    def cores():
        for core in range(NCORES):
            yield core, core // 4, core % 4

    maps = []
    for core, b, r in cores():
        m = dict(com)
        m["lat_in"] = np.concatenate([fm(inp["x"][b, r * TL:(r + 1) * TL].astype(f)), fm(inp["ctx"][b].astype(f))], axis=2)
        m["mod_in"] = mod_in[b]
        maps.append(m)
    pA = _prog(("tokA", TL), lambda: TokProg(TL, [("modin",), ("ffn", 0, 0), ("umix", 0), ("latout",)]))
    rA = run_prog(pA, maps)

    def ssd_layer(res_tok, j):
        smaps = []
        for core, b, g in cores():
            m = prep_ssd(inp, j, g)
            parts = [res_tok[b * 4]["u_out"][:, :, TL:TL + CT]] + [res_tok[b * 4 + r]["u_out"][:, :, 0:TL] for r in range(4)]
            m["u_all"] = np.ascontiguousarray(np.concatenate(parts, axis=2))
            smaps.append(m)
        p2 = _prog(("ssd", SEQ), lambda: SsdProg(SEQ, phase="both"))
        r2 = run_prog(p2, smaps)
        yn_in = []
        for core, b, r in cores():
            y = np.empty((128, 16, TL + CT), bf)
            for g in range(4):
                yg = r2[b * 4 + g]["yn"]
                y[:, g * 4:(g + 1) * 4, 0:TL] = yg[:, :, CT + r * TL:CT + (r + 1) * TL]
                y[:, g * 4:(g + 1) * 4, TL:TL + CT] = yg[:, :, 0:CT]
            yn_in.append(y)
        return yn_in

    yn0 = ssd_layer(rA, 0)
    maps = []
    for core, b, r in cores():
        m = dict(com)
        m["lat_in"] = rA[core]["lat_out"]
        m["mod_in"] = mod_in[b]
        m["yn_in"] = yn0[core]
        maps.append(m)
    stB = [("modin",), ("ssdout", 0, 0), ("ffn", 0, 1),
           ("ffn", 1, 0), ("sconv", 1), ("ffn", 1, 1),
           ("ffn", 2, 0), ("gmlp", 2), ("ffn", 2, 1),
           ("ffn", 3, 0), ("umix", 3), ("latout",)]
    pB = _prog(("tokB", TL), lambda: TokProg(TL, stB))
    rB = run_prog(pB, maps)
    yn3 = ssd_layer(rB, 1)
    maps = []
    for core, b, r in cores():
        m = dict(com)
        m["lat_in"] = rB[core]["lat_out"]
        m["mod_in"] = mod_in[b]
        m["yn_in"] = yn3[core]
        maps.append(m)
    pC = _prog(("tokC", TL), lambda: TokProg(TL, [("modin",), ("ssdout", 3, 1, False), ("ffn", 3, 1, False), ("final",)]))
    rC = run_prog(pC, maps)
    out = np.empty((B, SEQ, D), f)
    for core, b, r in cores():
        out[b, r * TL:(r + 1) * TL] = unfm(rC[core]["out_fm"])
    return out
```

```python
import numpy as np
import os
import ml_dtypes
import concourse.bass as bass
import concourse.mybir as mybir
from concourse.bass_utils import run_bass_kernel_spmd
from contextlib import ExitStack

F32 = mybir.dt.float32
BF16 = mybir.dt.bfloat16
AF = mybir.ActivationFunctionType
ALU = mybir.AluOpType

PE, ACT, DVE, POOL, SP = "pe", "act", "dve", "pool", "sp"
ENGS = (PE, ACT, DVE, POOL, SP)
EPOCH = 24000

D = 1024
KD = 8
FF = 2816
NJ = 22
CT = 256
EPS = 1e-6
NCORES = 8


class Tile:
    __slots__ = ("name", "w", "r", "olds")

    def __init__(self, name):
        self.name = name
        self.w = None
        self.r = {}
        self.olds = []


class KB:
    def __init__(self, nc, es):
        self.nc = nc
        self.es = es
        self.q = {e: [] for e in ENGS}
        self.sem = {}
        self.cnt = {e: 0 for e in ENGS}
        self.epoch = {e: 0 for e in ENGS}
        self.pending = {e: False for e in ENGS}
        self.seen = {e: {} for e in ENGS}
        self.dcnt = {}
        self.final = {}
        self.tiles = {}

    def sb(self, name, shape, dt):
        return self.es.enter_context(self.nc.sbuf_tensor("sb_" + name, list(shape), dt))

    def ps(self, name, shape, dt):
        return self.es.enter_context(self.nc.psum_tensor("ps_" + name, list(shape), dt))

    def tile(self, name):
        if name not in self.tiles:
            self.tiles[name] = Tile(name)
        return self.tiles[name]

    def _sem(self, key):
        if key not in self.sem:
            nm = "s_" + "_".join(str(k) for k in key)
            self.sem[key] = self.es.enter_context(self.nc.semaphore(nm))
        return self.sem[key]

    def _deps(self, reads, writes):
        deps = {}

        def add(ev):
            if ev is None:
                return
            k, v = ev
            if deps.get(k, 0) < v:
                deps[k] = v

        def addall(t):
            add(t.w)
            for k, v in t.r.items():
                add((k, v))

        for t in reads:
            add(t.w)
        for t in writes:
            addall(t)
            for o in t.olds:
                addall(o)
            t.olds = []
        return deps

    def _waits(self, eng, deps, skip_same=True):
        waits = []
        for k, v in deps.items():
            if skip_same and k[0] == eng:
                continue
            if self.seen[eng].get(k, 0) >= v:
                continue
            self.seen[eng][k] = v
            self._sem(k)
            waits.append((k, v))
        return waits

    def op(self, eng, fn, reads=(), writes=(), inc=True):
        deps = self._deps(reads, writes)
        waits = self._waits(eng, deps, skip_same=(eng == PE))
        if inc and not self.pending[eng] and self.cnt[eng] >= EPOCH:
            self.final[(eng, self.epoch[eng])] = self.cnt[eng]
            self.epoch[eng] += 1
            self.cnt[eng] = 0
        key = (eng, self.epoch[eng])
        val = self.cnt[eng] + 1
        if inc:
            self.cnt[eng] = val
            self.pending[eng] = False
            self._sem(key)
        else:
            self.pending[eng] = True
        self.q[eng].append((waits, fn, key if inc else None, 1))
        if os.environ.get("KBLOG"):
            self.log = getattr(self, "log", [])
            self.log.append((eng, val if inc else None, [t.name for t in reads], [t.name for t in writes], list(waits)))
        for t in writes:
            t.w = (key, val)
            t.r = {}
        for t in reads:
            if t.r.get(key, 0) < val:
                t.r[key] = val

    def dma(self, eng, out, in_, key, reads=(), writes=(), chain=True, fn=None, incv=16):
        key = ("dma", key)
        deps = self._deps(reads, writes)
        if chain and self.dcnt.get(key, 0) > 0:
            if deps.get(key, 0) < self.dcnt[key]:
                deps[key] = self.dcnt[key]
        waits = self._waits(eng, deps, skip_same=False)
        val = self.dcnt.get(key, 0) + incv
        self.dcnt[key] = val
        self._sem(key)

        if fn is None:
            def fn(e, out=out, in_=in_):
                return e.dma_start(out=out, in_=in_)

        self.q[eng].append((waits, fn, key, incv))
        for t in writes:
            t.w = (key, val)
            t.r = {}
        for t in reads:
            if t.r.get(key, 0) < val:
                t.r[key] = val

    def barrier(self, engs=ENGS):
        deps = {}
        for e in ENGS:
            for ep in range(self.epoch[e] + 1):
                k = (e, ep)
                if k in self.sem:
                    v = self.cnt[e] if ep == self.epoch[e] else self.final[k]
                    if v:
                        deps[k] = v
        for k, v in self.dcnt.items():
            deps[k] = v
        for e in engs:
            assert not self.pending[e]
            waits = self._waits(e, dict(deps))
            if waits:
                self.q[e].append((waits, None, None, 0))

    def emit(self):
        block = self.es.enter_context(self.nc.Block())
        kb = self

        def run(eng, e):
            for waits, fn, key, n in kb.q[eng]:
                for k, v in waits:
                    e.wait_ge(kb.sem[k], v)
                if fn is not None:
                    ins = fn(e)
                    if key is not None:
                        ins.then_inc(kb.sem[key], n)

        @block.tensor
        def _(e):
            run(PE, e)

        @block.scalar
        def _(e):
            run(ACT, e)

        @block.vector
        def _(e):
            run(DVE, e)

        @block.gpsimd
        def _(e):
            run(POOL, e)

        @block.sync
        def _(e):
            run(SP, e)

    def stats(self):
        return {e: len(self.q[e]) for e in ENGS}, len(self.sem)


class Ring:
    def __init__(self, kb, name, nelem, dt, nsem=6):
        self.kb = kb
        self.name = name
        self.buf = kb.sb(name, [128, nelem], dt)
        self.n = nelem
        self.ptr = 0
        self.live = []
        self.c = 0
        self.nsem = nsem

    def alloc(self, n):
        assert n <= self.n
        if self.ptr + n > self.n:
            self.ptr = 0
        s, e = self.ptr, self.ptr + n
        self.ptr = e
        olds = [t for (a, b, t) in self.live if a < e and s < b]
        self.live = [(a, b, t) for (a, b, t) in self.live if not (a < e and s < b)]
        self.c += 1
        t = Tile(f"{self.name}{self.c}")
        t.olds = olds
        self.live.append((s, e, t))
        return self.buf[:, s:e], t

    def semkey(self):
        return f"{self.name}_{self.c % self.nsem}"


def mm(kb, out, lhsT, rhs, start, stop, reads, writes, inc):
    kb.op(PE, lambda e: e.matmul(out, lhsT=lhsT, rhs=rhs, start=start, stop=stop), reads, writes, inc)


def act(kb, out, in_, func, reads, writes, bias=0.0, scale=1.0, accum_out=None):
    if accum_out is None:
        kb.op(ACT, lambda e: e.activation(out=out, in_=in_, func=func, bias=bias, scale=scale), reads, writes)
    else:
        kb.op(ACT, lambda e: e.activation(out=out, in_=in_, func=func, bias=bias, scale=scale,
                                          accum_out=accum_out), reads, writes)


def tt(kb, eng, out, in0, in1, op, reads, writes):
    kb.op(eng, lambda e: e.tensor_tensor(out=out, in0=in0, in1=in1, op=op), reads, writes)


def ts(kb, eng, out, in0, s1, s2, op0, op1, reads, writes):
    if s2 is None:
        kb.op(eng, lambda e: e.tensor_single_scalar(out=out, in_=in0, scalar=s1, op=op0), reads, writes)
    else:
        kb.op(eng, lambda e: e.tensor_scalar(out=out, in0=in0, scalar1=s1, scalar2=s2, op0=op0, op1=op1),
              reads, writes)


def stt(kb, eng, out, in0, scalar, in1, op0, op1, reads, writes):
    kb.op(eng, lambda e: e.scalar_tensor_tensor(out=out, in0=in0, scalar=scalar, in1=in1, op0=op0, op1=op1),
          reads, writes)


def tcopy(kb, eng, out, in_, reads, writes):
    if eng == ACT:
        kb.op(ACT, lambda e: e.copy(out=out, in_=in_), reads, writes)
    else:
        kb.op(eng, lambda e: e.tensor_copy(out=out, in_=in_), reads, writes)


def memset(kb, eng, ap, val, writes):
    kb.op(eng, lambda e: e.memset(ap, val), (), writes)


class TokProg:
    def __init__(self, TL, stages, name="tok"):
        self.TL = TL
        self.TT = TL + CT
        self.stages = stages
        blocks = []
        t0 = 0
        while t0 < TL:
            n = min(512, TL - t0)
            blocks.append((t0, n, 0))
            t0 += n
        blocks.append((TL, CT, 1))
        self.blocks = blocks
        self.nc = bass.Bass("TRN2", target_bir_lowering=False)
        self.dram = {}
        self.build()

    def din(self, name, shape, dt=F32):
        if name not in self.dram:
            self.dram[name] = self.nc.dram_tensor(name, list(shape), dt, kind="ExternalInput").ap()
        return self.dram[name]

    def dout(self, name, shape, dt=F32):
        if name not in self.dram:
            self.dram[name] = self.nc.dram_tensor(name, list(shape), dt, kind="ExternalOutput").ap()
        return self.dram[name]

    def build(self):
        nc = self.nc
        TT = self.TT
        with ExitStack() as es:
            kb = KB(nc, es)
            self.kb = kb
            self.lat = kb.sb("lat", [128, KD, TT], F32)
            self.u = kb.sb("u", [128, KD, TT], BF16)
            self.lat_t = [kb.tile(f"lat{b}") for b in range(len(self.blocks))]
            self.u_t = [kb.tile(f"u{b}") for b in range(len(self.blocks))]
            self.wring = Ring(kb, "wr", 18432, BF16, nsem=6)
            self.sring = Ring(kb, "sr", 3072, F32)
            self.bigA = kb.sb("bigA", [128, 4096], F32)
            self.bigB = kb.sb("bigB", [128, 4096], F32)
            self.t_bigA = kb.tile("bigA")
            self.t_bigB = kb.tile("bigB")
            self.rsb = [kb.sb(f"rs{x}", [128, 512], F32) for x in range(2)]
            self.t_rsb = [kb.tile(f"rs{x}") for x in range(2)]
            self.rsi = 0
            self.banks = [kb.ps(f"bank{i}", [128, 512], F32) for i in range(8)]
            self.bank_t = [kb.tile(f"bank{i}") for i in range(8)]
            self.rot = {"G": 0, "U": 0, "O": 0, "M": 0}
            self.ones_bf = kb.sb("ones_bf", [128, 128], BF16)
            self.t_ones = kb.tile("ones_bf")
            memset(kb, POOL, self.ones_bf[:], 1.0, [self.t_ones])
            self.cvec = kb.sb("cvec", [128, KD, 2], F32)
            self.scv = kb.sb("scv", [128, KD, 2], BF16)
            self.t_scv = kb.tile("scv")
            if any(st[0] == "mod" for st in self.stages):
                tcv = kb.tile("cvec")
                kb.dma(SP, self.cvec[:], self.din("cvec", [128, KD, 2]), "cvec", writes=[tcv])
                act(kb, self.scv[:], self.cvec[:], AF.Silu, [tcv], [self.t_scv])
            self.adab = kb.sb("adab", [128, 4, 72], F32)
            self.t_adab = kb.tile("adab")
            kb.dma(SP, self.adab[:], self.din("ada_b_l", [128, 4, 72]), "adab", writes=[self.t_adab])
            self.ng = kb.sb("ng", [128, 4, 3, KD], F32)
            self.t_ng = kb.tile("ng")
            kb.dma(SP, self.ng[:], self.din("norm_g_l", [128, 4, 3, KD]), "ng", writes=[self.t_ng])
            self.modraw = [kb.sb(f"modraw{i}", [128, 72, 2], F32) for i in range(4)]
            self.modA = [kb.sb(f"modA{i}", [128, 3, KD, 2], F32) for i in range(4)]
            self.modG = [kb.sb(f"modG{i}", [128, 3, KD, 2], F32) for i in range(4)]
            self.t_mod = [kb.tile(f"mod{i}") for i in range(4)]
            lat_in = self.din("lat_in", [128, KD, TT])
            for b, (t0, n, col) in enumerate(self.blocks):
                kb.dma(SP, self.lat[:, :, t0:t0 + n], lat_in[:, :, t0:t0 + n], f"latio{b}", writes=[self.lat_t[b]])
            for st in self.stages:
                getattr(self, "st_" + st[0])(*st[1:])
            kb.barrier()
            self.stat = kb.stats()
            kb.emit()

    def bank(self, role):
        base = {"G": 0, "U": 2, "O": 4, "M": 6}[role]
        i = base + self.rot[role]
        self.rot[role] ^= 1
        return self.banks[i], self.bank_t[i]

    def wload(self, src_ap, shape_str=None, **kw):
        n = 1
        for s in src_ap.shape[1:]:
            n *= s
        ap, t = self.wring.alloc(n)
        dst = ap
        if len(src_ap.shape) == 3:
            dst = ap.rearrange("p (a b) -> p a b", a=src_ap.shape[1])
        elif len(src_ap.shape) == 4:
            dst = ap.rearrange("p (a b c) -> p a b c", a=src_ap.shape[1], b=src_ap.shape[2])
        self.kb.dma(POOL, dst, src_ap, self.wring.semkey(), writes=[t])
        return dst, t

    def scratch(self, nelem, dt=F32):
        n32 = nelem if dt == F32 else (nelem + 1) // 2
        ap, t = self.sring.alloc(n32)
        if dt != F32:
            ap = ap.bitcast(dt)[:, 0:nelem]
        return ap, t

    def st_mod(self, i):
        kb = self.kb
        par = i
        src = self.din(f"ada_w{i}", [D, 9 * D]).rearrange("(k p) c -> p k c", p=128)
        bk, bt = self.bank("M")
        for cb in range(18):
            w, wt = self.wload(src[:, :, cb * 512:(cb + 1) * 512])
            for jj in range(4):
                j = cb * 4 + jj
                for k in range(KD):
                    mm(kb, bk[:, 2 * j:2 * j + 2], w[:, k, jj * 128:(jj + 1) * 128], self.scv[:, k, :],
                       k == 0, k == KD - 1, [wt, self.t_scv], [bt], inc=(k == KD - 1))
        mr = self.modraw[par]
        tm = self.t_mod[par]
        tt(kb, DVE, mr[:], bk[:, 0:144].rearrange("p (j c) -> p j c", c=2),
           self.adab[:, i, :].unsqueeze(2).to_broadcast([128, 72, 2]), ALU.add, [bt, self.t_adab], [tm])
        self.mod_derive(i)

    def st_modin(self):
        kb = self.kb
        mi = self.din("mod_in", [128, 4, 72, 2])
        for i in range(4):
            kb.dma(SP, self.modraw[i][:], mi[:, i], f"modin{i}", writes=[self.t_mod[i]])
            tt(kb, DVE, self.modraw[i][:], self.modraw[i][:],
               self.adab[:, i, :].unsqueeze(2).to_broadcast([128, 72, 2]), ALU.add, [self.t_mod[i], self.t_adab],
               [self.t_mod[i]])
            self.mod_derive(i)

    def mod_derive(self, i):
        kb = self.kb
        par = i
        mr = self.modraw[par]
        tm = self.t_mod[par]
        for s in range(3):
            stt(kb, DVE, self.modA[par][:, s], mr[:, (3 * s + 1) * 8:(3 * s + 2) * 8, :], 1.0,
                self.ng[:, i, s, :].unsqueeze(2).to_broadcast([128, KD, 2]), ALU.add, ALU.mult,
                [tm, self.t_ng], [tm])
            ts(kb, DVE, self.modG[par][:, s], mr[:, (3 * s + 2) * 8:(3 * s + 3) * 8, :],
               0.5 if s != 1 else 1.0, None, ALU.mult, ALU.bypass, [tm], [tm])

    def mod_views(self, i, s):
        par = i
        A = self.modA[par][:, s]
        S = self.modraw[par][:, 3 * s * 8:(3 * s + 1) * 8, :]
        G = self.modG[par][:, s]
        return A, S, G, self.t_mod[par]

    def rstd_block(self, b):
        kb = self.kb
        t0, n, col = self.blocks[b]
        sq, sqt = self.bigA[:].bitcast(BF16)[:, 0:KD * n].rearrange("p (k t) -> p k t", k=KD), self.t_bigA
        act(kb, sq, self.lat[:, :, t0:t0 + n], AF.Square, [self.lat_t[b]], [sqt])
        bk, bt = self.bank("M")
        for k in range(KD):
            mm(kb, bk[:, 0:n], self.ones_bf[:], sq[:, k, :], k == 0, k == KD - 1, [self.t_ones, sqt], [bt],
               inc=(k == KD - 1))
        rs, rst = self.rsb[self.rsi][:, 0:n], self.t_rsb[self.rsi]
        self.rsi ^= 1
        act(kb, rs, bk[:, 0:n], AF.Sqrt, [bt], [rst], bias=EPS, scale=1.0 / D)
        kb.op(DVE, lambda e: e.reciprocal(out=rs, in_=rs), [rst], [rst])
        return rs, rst

    def norm_mod(self, i, s, nblocks=None):
        kb = self.kb
        A, S, G, tm = self.mod_views(i, s)
        for b, (t0, n, col) in enumerate(self.blocks[:nblocks]):
            rs, rst = self.rstd_block(b)
            for k in range(KD):
                t1, t1t = self.scratch(n)
                tt(kb, DVE, t1, self.lat[:, k, t0:t0 + n], rs, ALU.mult, [self.lat_t[b], rst], [t1t])
                act(kb, self.u[:, k, t0:t0 + n], t1, AF.Identity, [t1t, tm], [self.u_t[b]],
                    bias=S[:, k, col:col + 1], scale=A[:, k, col:col + 1])

    def st_ffn(self, i, hs, do_ctx=True):
        kb = self.kb
        s = 0 if hs == 0 else 2
        nb = len(self.blocks) if do_ctx else len(self.blocks) - 1
        self.norm_mod(i, s, nb)
        A, S, G, tm = self.mod_views(i, s)
        wg = self.din(f"wg_{i}_{hs}", [D, FF]).rearrange("(k p) f -> p k f", p=128)
        wu = self.din(f"wu_{i}_{hs}", [D, FF]).rearrange("(k p) f -> p k f", p=128)
        wd = self.din(f"wd_{i}_{hs}", [FF, D]).rearrange("(j p) m -> p j m", p=128)
        hbv = self.bigB[:].bitcast(BF16)
        hbuf = [(hbv[:, x * 1024:(x + 1) * 1024], self.kb.tile(f"bigB_h{x}")) for x in range(2)]
        self.kb.tile("bigB_h0").olds.append(self.t_bigB)
        self.kb.tile("bigB_h1").olds.append(self.t_bigB)
        prev = None
        step = 0

        def cstep(pv):
            wdt_ap, wdt_t, hb, hbt, b = pv
            t0, n, col = self.blocks[b]
            for m in range(KD):
                self.o4 = (getattr(self, "o4", 0) + 1) % 4
                bk, bt = self.banks[4 + self.o4], self.bank_t[4 + self.o4]
                for jj in range(2):
                    mm(kb, bk[:, 0:n], wdt_ap[:, jj, m * 128:(m + 1) * 128], hb[:, jj, 0:n], jj == 0, jj == 1,
                       [wdt_t, hbt], [bt], inc=(jj == 1))
                stt(kb, DVE, self.lat[:, m, t0:t0 + n], bk[:, 0:n], G[:, m, col:col + 1],
                    self.lat[:, m, t0:t0 + n], ALU.mult, ALU.add, [bt, tm, self.lat_t[b]], [self.lat_t[b]])

        for t in range(NJ // 2):
            wgt, wgt_t = self.wload(wg[:, :, t * 256:(t + 1) * 256])
            wut, wut_t = self.wload(wu[:, :, t * 256:(t + 1) * 256])
            wdt, wdt_t = self.wload(wd[:, 2 * t:2 * t + 2, :])
            for b in range(nb):
                t0, n, col = self.blocks[b]
                hb, hbt = hbuf[step % 2]
                hb = hb.rearrange("p (j t) -> p j t", j=2)
                step += 1
                for jj in range(2):
                    gk, gt = self.bank("G")
                    uk, ut = self.bank("U")
                    for k in range(KD):
                        mm(kb, gk[:, 0:n], wgt[:, k, jj * 128:(jj + 1) * 128], self.u[:, k, t0:t0 + n],
                           k == 0, k == KD - 1, [wgt_t, self.u_t[b]], [gt], inc=(k == KD - 1))
                    for k in range(KD):
                        mm(kb, uk[:, 0:n], wut[:, k, jj * 128:(jj + 1) * 128], self.u[:, k, t0:t0 + n],
                           k == 0, k == KD - 1, [wut_t, self.u_t[b]], [ut], inc=(k == KD - 1))
                    sg, sgt = self.scratch(n)
                    act(kb, sg, gk[:, 0:n], AF.Silu, [gt], [sgt])
                    tt(kb, DVE, hb[:, jj, 0:n], sg, uk[:, 0:n], ALU.mult, [sgt, ut], [hbt])
                if prev is not None:
                    cstep(prev)
                prev = (wdt, wdt_t, hb, hbt, b)
        cstep(prev)
        self.t_bigB.olds += [hbuf[0][1], hbuf[1][1]]

    def scratch_fixed(self, name, nelem, dt):
        if not hasattr(self, "_fixed"):
            self._fixed = {}
        if name not in self._fixed:
            self._fixed[name] = (self.kb.sb(name, [128, nelem], dt), self.kb.tile(name))
        return self._fixed[name]

    def st_umix(self, i):
        kb = self.kb
        self.norm_mod(i, 1)
        uo = self.dout("u_out", [128, KD, self.TT], BF16)
        for b, (t0, n, col) in enumerate(self.blocks):
            kb.dma(SP, uo[:, :, t0:t0 + n], self.u[:, :, t0:t0 + n], f"uio{b}", reads=[self.u_t[b]])

    def st_latout(self):
        kb = self.kb
        lo = self.dout("lat_out", [128, KD, self.TT])
        for b, (t0, n, col) in enumerate(self.blocks):
            kb.dma(SP, lo[:, :, t0:t0 + n], self.lat[:, :, t0:t0 + n], f"latio{b}", reads=[self.lat_t[b]])

    def st_final(self):
        kb = self.kb
        fn = self.kb.sb("fnorm", [128, KD], F32)
        tfn = kb.tile("fnorm")
        kb.dma(SP, fn[:], self.din("final_norm_l", [128, KD]), "fnorm", writes=[tfn])
        out = self.dout("out_fm", [128, KD, self.TL])
        for b, (t0, n, col) in enumerate(self.blocks[:-1]):
            rs, rst = self.rstd_block(b)
            for k in range(KD):
                t1, t1t = self.scratch(n)
                tt(kb, DVE, t1, self.lat[:, k, t0:t0 + n], rs, ALU.mult, [self.lat_t[b], rst], [t1t])
                act(kb, self.lat[:, k, t0:t0 + n], t1, AF.Identity, [t1t, tfn], [self.lat_t[b]],
                    scale=fn[:, k:k + 1])
            kb.dma(SP, out[:, :, t0:t0 + n], self.lat[:, :, t0:t0 + n], f"latio{b}", reads=[self.lat_t[b]])

    def st_sconv(self, i):
        kb = self.kb
        self.norm_mod(i, 1)
        A, S, G, tm = self.mod_views(i, 1)
        win = self.din("sc_in", [1, D, 3 * D])[0].rearrange("(k p) (c m q) -> p k c m q", p=128, c=3, m=KD)
        wout = self.din("sc_out", [1, D, D])[0].rearrange("(k p) m -> p k m", p=128)
        cw = kb.sb("sc_cw", [128, 3, KD], F32)
        tcw = kb.tile("sc_cw")
        kb.dma(SP, cw[:], self.din("sc_conv_l", [128, 3, KD]), "sc_cw", writes=[tcw])
        nbk = len(self.blocks)
        for b, (t0, n, col) in enumerate(self.blocks):
            z, zt = self.bigB[:].bitcast(BF16)[:, 0:KD * n].rearrange("p (k t) -> p k t", k=KD), self.t_bigB
            R = 64 if col == 0 else n
            for m in range(KD):
                wis = [self.wload(win[:, :, c, m, :]) for c in range(3)]
                pb, pbt = self.bank("G")
                pc, pct = self.bank("U")
                ph, pht = self.bank("O")
                for c, (pk, pt) in enumerate(((pb, pbt), (pc, pct), (ph, pht))):
                    wi, wit = wis[c]
                    for k in range(KD):
                        mm(kb, pk[:, 0:n], wi[:, k, :], self.u[:, k, t0:t0 + n], k == 0, k == KD - 1,
                           [wit, self.u_t[b]], [pt], inc=(k == KD - 1))
                hv, hvt = self.scratch(n)
                tcopy(kb, ACT, hv, ph[:, 0:n], [pht], [hvt])
                v, vt = self.scratch(n)
                tt(kb, DVE, v, pc[:, 0:n], hv, ALU.mult, [pct, hvt], [vt])
                vc, vct = self.scratch(n)
                ts(kb, DVE, vc, v, cw[:, 1, m:m + 1], None, ALU.mult, ALU.bypass, [vt, tcw], [vct])
                v3 = v.rearrange("p (r w) -> p r w", w=R)
                vc3 = vc.rearrange("p (r w) -> p r w", w=R)
                stt(kb, DVE, vc3[:, :, 1:R], v3[:, :, 0:R - 1], cw[:, 0, m:m + 1], vc3[:, :, 1:R],
                    ALU.mult, ALU.add, [vt, tcw, vct], [vct])
                stt(kb, DVE, vc3[:, :, 0:R - 1], v3[:, :, 1:R], cw[:, 2, m:m + 1], vc3[:, :, 0:R - 1],
                    ALU.mult, ALU.add, [vt, tcw, vct], [vct])
                tt(kb, DVE, z[:, m, :], vc, pb[:, 0:n], ALU.mult, [vct, pbt], [zt])
            wo, wot = self.wload(wout)
            for m in range(KD):
                bk, bt = self.bank("M")
                for k in range(KD):
                    mm(kb, bk[:, 0:n], wo[:, k, m * 128:(m + 1) * 128], z[:, k, :], k == 0, k == KD - 1,
                       [wot, zt], [bt], inc=(k == KD - 1))
                stt(kb, DVE, self.lat[:, m, t0:t0 + n], bk[:, 0:n], G[:, m, col:col + 1],
                    self.lat[:, m, t0:t0 + n], ALU.mult, ALU.add, [bt, tm, self.lat_t[b]], [self.lat_t[b]])

    def st_gmlp(self, i):
        kb = self.kb
        self.norm_mod(i, 1)
        A, S, G, tm = self.mod_views(i, 1)
        gin = self.din("gm_in", [1, D, 4096])[0].rearrange("(k p) c -> p k c", p=128)
        gout = self.din("gm_out", [1, 2048, D])[0].rearrange("(c p) m -> p c m", p=128)
        wsT = kb.sb("gm_wsT", [128, 8, 128], BF16)
        twsT = kb.tile("gm_wsT")
        kb.dma(POOL, wsT[:], self.din("gm_wsT_l", [128, 8, 128]), "gm_wsT", writes=[twsT])
        bsr = kb.sb("gm_bs", [1, 8, 128], BF16)
        tbsr = kb.tile("gm_bs")
        kb.dma(POOL, bsr[:], self.din("gm_bs_l", [1, 8, 128]), "gm_bs", writes=[tbsr])
        vn = kb.sb("gm_vn", [128, 2048], BF16)
        tvn = kb.tile("gm_vn")
        kb.dma(POOL, vn[:], self.din("gm_vn_l", [128, 2048]), "gm_vn", writes=[tvn])
        for b, (t0, n, col) in enumerate(self.blocks):
            nch = n // 128
            zv, zvt = self.bigA[:].bitcast(BF16)[:, 0:nch * 2048].rearrange("p (c f) -> p c f", c=nch), self.t_bigA
            ssum = kb.sb(f"gm_ssum{b}", [128, 4, 4], F32)
            ssq = kb.sb(f"gm_ssq{b}", [128, 4, 4], F32)
            tst = kb.tile(f"gm_stat{b}")
            memset(kb, POOL, ssum[:], 0.0, [tst])
            memset(kb, POOL, ssq[:], 0.0, [tst])
            for nb4 in range(4):
                wv, wvt = self.wload(gin[:, :, 2048 + nb4 * 512:2048 + (nb4 + 1) * 512])
                for c in range(nch):
                    bk, bt = self.bank("G")
                    for k in range(KD):
                        mm(kb, bk[:, :], self.u[:, k, t0 + c * 128:t0 + (c + 1) * 128], wv[:, k, :],
                           k == 0, k == KD - 1, [wvt, self.u_t[b]], [bt], inc=(k == KD - 1))
                    act(kb, zv[:, c, nb4 * 512:(nb4 + 1) * 512], bk[:, :], AF.Gelu_apprx_tanh, [bt], [zvt, tst],
                        accum_out=ssum[:, c, nb4:nb4 + 1])
                    junk, jt = self.scratch(256)
                    act(kb, junk.bitcast(BF16), zv[:, c, nb4 * 512:(nb4 + 1) * 512], AF.Square, [zvt], [jt, tst],
                        accum_out=ssq[:, c, nb4:nb4 + 1])
            st = kb.sb(f"gm_st{b}", [128, 4, 4], F32)
            kb.op(DVE, lambda e, ssum=ssum, st=st: e.reduce_sum(out=st[:, :, 0], in_=ssum[:, :, :],
                                                               axis=mybir.AxisListType.X), [tst], [tst])
            kb.op(DVE, lambda e, ssq=ssq, st=st: e.reduce_sum(out=st[:, :, 1], in_=ssq[:, :, :],
                                                             axis=mybir.AxisListType.X), [tst], [tst])
            ts(kb, DVE, st[:, :, 0], st[:, :, 0], 1.0 / 2048, None, ALU.mult, ALU.bypass, [tst], [tst])
            tt(kb, DVE, st[:, :, 2], st[:, :, 0], st[:, :, 0], ALU.mult, [tst], [tst])
            stt(kb, DVE, st[:, :, 1], st[:, :, 1], 1.0 / 2048, st[:, :, 2], ALU.mult, ALU.subtract, [tst], [tst])
            act(kb, st[:, :, 1], st[:, :, 1], AF.Sqrt, [tst], [tst], bias=EPS)
            kb.op(DVE, lambda e, st=st: e.reciprocal(out=st[:, :, 1], in_=st[:, :, 1]), [tst], [tst])
            stt(kb, DVE, st[:, :, 3], st[:, :, 0], -1.0, st[:, :, 1], ALU.mult, ALU.mult, [tst], [tst])
            for c in range(nch):
                act(kb, zv[:, c, :], zv[:, c, :], AF.Identity, [zvt, tst], [zvt],
                    bias=st[:, c, 3:4], scale=st[:, c, 1:2])
                tt(kb, POOL, zv[:, c, :], zv[:, c, :], vn[:], ALU.mult, [zvt, tvn], [zvt])
            prod, prt = self.bigB[:].bitcast(BF16)[:, 0:16 * n].rearrange("p (c t) -> p c t", c=16), self.t_bigB
            for cc in range(16):
                g = cc // 2
                sk, skt = self.bank("U")
                for c in range(nch):
                    mm(kb, sk[:, c * 128:(c + 1) * 128], zv[:, c, cc * 128:(cc + 1) * 128], wsT[:, g, :],
                       True, False, [zvt, twsT], [skt], inc=False)
                    mm(kb, sk[:, c * 128:(c + 1) * 128], self.ones_bf[0:1, :], bsr[0:1, g, :],
                       False, True, [self.t_ones, tbsr], [skt], inc=(c == nch - 1))
                wuc, wuct = self.wload(gin[:, :, cc * 128:(cc + 1) * 128])
                zk, zkt = self.bank("O")
                for k in range(KD):
                    mm(kb, zk[:, 0:n], wuc[:, k, :], self.u[:, k, t0:t0 + n], k == 0, k == KD - 1,
                       [wuct, self.u_t[b]], [zkt], inc=(k == KD - 1))
                zu, zut = self.scratch(n)
                act(kb, zu, zk[:, 0:n], AF.Gelu_apprx_tanh, [zkt], [zut])
                tt(kb, DVE, prod[:, cc, :], zu, sk[:, 0:n], ALU.mult, [zut, skt], [prt])
            for m in range(KD):
                wo, wot = self.wload(gout[:, :, m * 128:(m + 1) * 128])
                bk, bt = self.bank("M")
                for cc in range(16):
                    mm(kb, bk[:, 0:n], wo[:, cc, :], prod[:, cc, :], cc == 0, cc == 15, [wot, prt], [bt],
                       inc=(cc == 15))
                stt(kb, DVE, self.lat[:, m, t0:t0 + n], bk[:, 0:n], G[:, m, col:col + 1],
                    self.lat[:, m, t0:t0 + n], ALU.mult, ALU.add, [bt, tm, self.lat_t[b]], [self.lat_t[b]])

    def st_ssdout(self, i, j, do_ctx=True):
        kb = self.kb
        A, S, G, tm = self.mod_views(i, 1)
        yn = self.din("yn_in", [128, 16, self.TT], BF16)
        wout = self.din(f"ssd_out{j}", [2048, D]).rearrange("(c p) m -> p c m", p=128)
        nb = len(self.blocks) if do_ctx else len(self.blocks) - 1
        for b, (t0, n, col) in enumerate(self.blocks[:nb]):
            bg, yt = (self.bigA, self.t_bigA) if b % 2 == 0 else (self.bigB, self.t_bigB)
            y = bg[:].bitcast(BF16)[:, 0:16 * n].rearrange("p (c t) -> p c t", c=16)
            kb.dma(SP, y, yn[:, :, t0:t0 + n], f"ynio{b % 2}", writes=[yt])
            for m in range(KD):
                wo, wot = self.wload(wout[:, :, m * 128:(m + 1) * 128])
                bk, bt = self.bank("M")
                for cc in range(16):
                    mm(kb, bk[:, 0:n], wo[:, cc, :], y[:, cc, :], cc == 0, cc == 15, [wot, yt], [bt],
                       inc=(cc == 15))
                stt(kb, DVE, self.lat[:, m, t0:t0 + n], bk[:, 0:n], G[:, m, col:col + 1],
                    self.lat[:, m, t0:t0 + n], ALU.mult, ALU.add, [bt, tm, self.lat_t[b]], [self.lat_t[b]])


def fm(a):
    T = a.shape[0]
    return np.ascontiguousarray(a.T.reshape(KD, 128, T).transpose(1, 0, 2))


def unfm(a):
    T = a.shape[2]
    return np.ascontiguousarray(a.transpose(1, 0, 2).reshape(D, T).T)


def prep_common(inp):
    f = np.float32
    c = {}
    c["ada_w"] = np.asarray(inp["ada_w"], f)
    c["ada_b_l"] = np.ascontiguousarray(np.asarray(inp["ada_b"], f).reshape(4, 72, 128).transpose(2, 0, 1))
    c["norm_g_l"] = np.ascontiguousarray(np.asarray(inp["norm_g"], f).reshape(4, 3, KD, 128).transpose(3, 0, 1, 2))
    c["ffn_wg"] = np.asarray(inp["ffn_wg"], f)
    c["ffn_wu"] = np.asarray(inp["ffn_wu"], f)
    c["ffn_wd"] = np.asarray(inp["ffn_wd"], f)
    c["final_norm_l"] = np.ascontiguousarray(np.asarray(inp["final_norm"], f).reshape(KD, 128).T)
    c["sc_in"] = np.asarray(inp["sc_in"], f)
    c["sc_out"] = np.asarray(inp["sc_out"], f)
    c["sc_conv_l"] = np.ascontiguousarray(np.asarray(inp["sc_conv"], f)[0].reshape(3, KD, 128).transpose(2, 0, 1))
    c["gm_in"] = np.asarray(inp["gm_in"], f)
    c["gm_out"] = np.asarray(inp["gm_out"], f)
    c["gm_wsT_l"] = np.ascontiguousarray(np.asarray(inp["gm_ws"], f)[0].transpose(2, 0, 1))
    c["gm_bs_l"] = np.ascontiguousarray(np.asarray(inp["gm_bs"], f)[0][None])
    c["gm_vn_l"] = np.ascontiguousarray(np.broadcast_to(np.asarray(inp["gm_vnorm"], f)[0][None, :], (128, 2048)))
    c["ssd_out"] = np.asarray(inp["ssd_out"], f)
    return c


def cvec_for(inp, b):
    cv = np.empty((128, KD, 2), np.float32)
    cv[:, :, 0] = np.asarray(inp["c"], np.float32)[b].reshape(KD, 128).T
    cv[:, :, 1] = np.asarray(inp["c_ctx"], np.float32).reshape(KD, 128).T
    return cv


def run_prog(prog, in_maps):
    names = set(prog.dram.keys())
    maps = [{k: v for k, v in m.items() if k in names} for m in in_maps]
    for m in maps:
        missing = [k for k in names if k not in m and k not in ("u_out", "lat_out", "out_fm", "yn", "mod_out", "sprev_out")]
        assert not missing, missing
    res = run_bass_kernel_spmd(prog.nc, maps, core_ids=list(range(NCORES)))
    return res.results


NEG = -1.0e9


class SsdProg:
    def __init__(self, T, name="ssd", dbg=99, phase="both"):
        self.T = T
        self.dbg = dbg
        self.phase = phase
        self.NT = CT + T
        self.nc = bass.Bass("TRN2", target_bir_lowering=False)
        self.dram = {}
        sbs = [(0, CT, 0, CT)]
        t0 = CT
        while t0 < self.NT:
            n = min(512, self.NT - t0)
            sbs.append((t0, n, CT, self.NT))
            t0 += n
        self.sbs = sbs
        self.build()

    din = TokProg.din
    dout = TokProg.dout

    def build(self):
        nc = self.nc
        NT = self.NT
        NCH = NT // 128
        with ExitStack() as es:
            kb = KB(nc, es)
            self.kb = kb
            self.banks = [kb.ps(f"bank{i}", [128, 512], F32) for i in range(8)]
            self.bt = {}
            ident = kb.sb("ident", [128, 128], BF16); t_c = kb.tile("consts")
            Um = kb.sb("Um", [128, 128], F32); Lm = kb.sb("Lm", [128, 128], F32)
            nmf = kb.sb("nmf", [128, 128], BF16); nmb = kb.sb("nmb", [128, 128], BF16)
            ones32 = kb.sb("ones32", [128, 128], F32)
            onesr = kb.sb("onesr", [1, 128], BF16)
            for ap, op, cm, step, base, fill in ((ident, ALU.not_equal, 1, -1, 0, 1.0), (Um, ALU.is_gt, 1, -1, 0, 1.0),
                                                 (Lm, ALU.is_gt, -1, 1, 0, 1.0), (nmf, ALU.is_gt, -1, 1, 1, NEG),
                                                 (nmb, ALU.is_gt, 1, -1, 1, NEG)):
                memset(kb, POOL, ap[:], 0.0, [t_c])
                kb.op(POOL, lambda e, ap=ap, op=op, cm=cm, step=step, base=base, fill=fill: e.affine_select(
                    out=ap[:], in_=ap[:], compare_op=op, fill=fill, base=base, pattern=[[step, 128]],
                    channel_multiplier=cm), [t_c], [t_c])
            memset(kb, POOL, ones32[:], 1.0, [t_c])
            memset(kb, POOL, onesr[:], 1.0, [t_c])
            self.onesr5 = kb.sb("onesr5", [1, 512], BF16)
            memset(kb, POOL, self.onesr5[:], 1.0, [t_c])
            self.ident, self.Um, self.Lm, self.nmf, self.nmb, self.ones32, self.onesr = ident, Um, Lm, nmf, nmb, ones32, onesr
            self.t_c = t_c
            t_p = kb.tile("params")
            self.t_p = t_p
            wx = kb.sb("wx", [128, KD, 768], BF16)
            wz = kb.sb("wz", [128, KD, 512], BF16)
            wdt = kb.sb("wdt", [128, KD, 16], BF16)
            kb.dma(POOL, wx[:], self.din("w_x", [D, 768]).rearrange("(k p) c -> p k c", p=128), "p_wx", writes=[t_p])
            kb.dma(POOL, wz[:], self.din("w_z", [D, 512]).rearrange("(k p) c -> p k c", p=128), "p_wz", writes=[t_p])
            kb.dma(POOL, wdt[:], self.din("w_dt", [D, 16]).rearrange("(k p) c -> p k c", p=128), "p_wdt", writes=[t_p])
            cwp = kb.sb("cwp", [128, 6, 5], F32)
            cbp = kb.sb("cbp", [128, 6], F32)
            cbr = kb.sb("cbr", [1, 768], BF16)
            dtb = kb.sb("dtb", [128, 16], F32)
            abc = kb.sb("abc", [128, 16], F32)
            dsk = kb.sb("dsk", [128, 8], F32)
            nwb = kb.sb("nwb", [128, 512], F32)
            kb.dma(SP, cwp[:], self.din("cw_p", [128, 6, 5]), "p_cwp", writes=[t_p])
            kb.dma(SP, cbp[:], self.din("cb_p", [128, 6]), "p_cbp", writes=[t_p])
            kb.dma(POOL, cbr[:], self.din("cb_r", [1, 768]), "p_cbr", writes=[t_p])
            kb.dma(SP, dtb[:], self.din("dtb_bc", [128, 16]), "p_dtb", writes=[t_p])
            kb.dma(SP, abc[:], self.din("alog_bc", [128, 16]), "p_abc", writes=[t_p])
            kb.dma(SP, dsk[:], self.din("dsk_bc", [128, 8]), "p_dsk", writes=[t_p])
            kb.dma(SP, nwb[:], self.din("nw_bc", [128, 512]), "p_nwb", writes=[t_p])
            act(kb, abc[:], abc[:], AF.Exp, [t_p], [t_p])
            ts(kb, DVE, abc[:], abc[:], -1.0, None, ALU.mult, ALU.bypass, [t_p], [t_p])
            diag = kb.sb("diag", [128, 6, 5, 128], BF16)
            for fc in range(6):
                for tap in range(5):
                    ts(kb, DVE, diag[:, fc, tap, :], ident[:], cwp[:, fc, tap:tap + 1], None, ALU.mult, ALU.bypass,
                       [t_c, t_p], [t_p])
            self.wx, self.wz, self.wdt, self.cbp, self.cbr, self.dtb, self.abc, self.dsk, self.nwb, self.diag = \
                wx, wz, wdt, cbp, cbr, dtb, abc, dsk, nwb, diag
            self.Sprev_b = kb.sb("Sprev_b", [128, NCH, 512], BF16)
            self.t_Sprev = [kb.tile(f"Sprev{c}") for c in range(NCH)]
            self.S = [kb.sb(f"S{d}", [128, 512], F32) for d in range(2)]
            self.Sbf = kb.sb("Sbf", [128, 512], BF16)
            self.t_S = [kb.tile(f"S{d}") for d in range(2)]
            self.t_Sbf = kb.tile("Sbf")
            memset(kb, DVE, self.S[0][:], 0.0, [self.t_S[0]])
            memset(kb, DVE, self.S[1][:], 0.0, [self.t_S[1]])
            memset(kb, DVE, self.Sbf[:], 0.0, [self.t_Sbf])
            self.ublk = [kb.sb(f"ublk{x}", [128, KD, 516], BF16) for x in range(2)]
            self.t_ublk = [kb.tile(f"ublk{x}") for x in range(2)]
            self.pre = kb.sb("pre", [128, 6, 516], BF16); self.t_pre = kb.tile("pre")
            self.Bfm = kb.sb("Bfm", [128, 512], BF16); self.Cfm = kb.sb("Cfm", [128, 512], BF16)
            self.t_BC = kb.tile("BCfm")
            self.Braw = kb.sb("Braw", [128, 512], F32); self.Craw = kb.sb("Craw", [128, 512], F32)
            self.t_BCraw = kb.tile("BCraw")
            self.ynT = kb.sb("ynT", [128, 4, 512], BF16); self.t_ynT = kb.tile("ynT")
            self.ui = 0
            u_all = self.din("u_all", [128, KD, NT], BF16)
            yn = self.dout("yn", [128, 4, NT], BF16)
            order = [self.sbs[0]] + list(reversed(self.sbs[1:]))
            if self.phase == "main":
                order = []
                kb.dma(SP, self.Sprev_b[:], self.din("sprev_in", [128, NCH, 512], BF16), "sprev", writes=self.t_Sprev)
            for (s0, ns, lo, hi) in order:
                ub, ubt = self.load_u(u_all, s0, ns, lo, hi)
                self.preconv(ub, ubt, ns, fcs=(0, 1, 2, 3, 4))
                for c in reversed(range(ns // 128)):
                    if self.dbg >= 2:
                        self.chunk(ub, ubt, s0, c, main=False)
                if s0 == 0:
                    pass
            if self.phase == "pre":
                kb.dma(SP, self.dout("sprev_out", [128, NCH, 512], BF16), self.Sprev_b[:], "sprev", reads=self.t_Sprev)
            for (s0, ns, lo, hi) in (self.sbs if self.phase != "pre" else []):
                ub, ubt = self.load_u(u_all, s0, ns, lo, hi)
                self.preconv(ub, ubt, ns, fcs=(0, 1, 2, 3, 4, 5) if not os.environ.get("V2") else (0, 1, 2, 3, 4))
                if not os.environ.get("V1"):
                    self.conv_fm(ns)
                if os.environ.get("V3"):
                    kb.barrier()
                for c in range(ns // 128):
                    if self.dbg == 29:
                        self.chunk(ub, ubt, s0, c, main=False)
                    elif self.dbg >= 3:
                        self.chunk(ub, ubt, s0, c, main=True)
                if self.dbg < 3 or self.dbg == 29:
                    memset(kb, DVE, self.ynT[:], 0.0, [self.t_ynT])
                kb.dma(SP, yn[:, :, s0:s0 + ns], self.ynT[:, :, 0:ns], "ynout", reads=[self.t_ynT])
            kb.barrier()
            self.stat = kb.stats()
            kb.emit()

    def B(self, i, name):
        return self.banks[i], self.kb.tile(f"bank{i}")

    def load_u(self, u_all, s0, ns, lo, hi):
        kb = self.kb
        x = self.ui
        self.ui ^= 1
        ub, ubt = self.ublk[x], self.t_ublk[x]
        a = max(s0 - 2, lo)
        b = min(s0 + ns + 2, hi)
        if a > s0 - 2:
            memset(kb, DVE, ub[:, :, 0:2], 0.0, [ubt])
        if b < s0 + ns + 2:
            memset(kb, DVE, ub[:, :, ns + 2:ns + 4], 0.0, [ubt])
        kb.dma(SP, ub[:, :, a - (s0 - 2):b - (s0 - 2)], u_all[:, :, a:b], f"uload{x}", writes=[ubt])
        return ub, ubt

    def preconv(self, ub, ubt, ns, fcs):
        kb = self.kb
        for n_i, fc in enumerate(fcs):
            bk, bt = self.B(0 if n_i % 2 == 0 else 7, "pre")
            bh, bht = self.B(1, "small")
            for k in range(KD):
                mm(kb, bk[:, 0:ns], self.wx[:, k, fc * 128:(fc + 1) * 128], ub[:, k, 0:ns], k == 0, k == KD - 1,
                   [self.t_p, ubt], [bt], inc=(k == KD - 1))
            for k in range(KD):
                mm(kb, bh[:, 0:4], self.wx[:, k, fc * 128:(fc + 1) * 128], ub[:, k, ns:ns + 4], k == 0, k == KD - 1,
                   [self.t_p, ubt], [bht], inc=(k == KD - 1))
            tcopy(kb, ACT, self.pre[:, fc, 0:ns], bk[:, 0:ns], [bt], [self.t_pre])
            tcopy(kb, DVE, self.pre[:, fc, ns:ns + 4], bh[:, 0:4], [bht], [self.t_pre])

    def conv_fm(self, ns):
        kb = self.kb
        for fc, dst in ((4, self.Bfm), (5, self.Cfm)):
            bk, bt = self.B(0 if fc == 4 else 7, "pre")
            for tap in range(5):
                mm(kb, bk[:, 0:ns], self.diag[:, fc, tap, :], self.pre[:, fc, tap:tap + ns], tap == 0, False,
                   [self.t_p, self.t_pre], [bt], inc=False)
            mm(kb, bk[:, 0:ns], self.cbr[0:1, fc * 128:(fc + 1) * 128], self.onesr5[0:1, 0:ns], False, True,
               [self.t_p, self.t_c], [bt], inc=True)
            raw = self.Braw if fc == 4 else self.Craw
            act(kb, raw[:, 0:ns], bk[:, 0:ns], AF.Identity, [bt], [self.t_BCraw])
            if os.environ.get("V7"):
                dm, dmt = self.tmp("dummy", [128, 8], F32)
                act(kb, dm[:], self.cbp[:, 0:6].rearrange("p a -> p a")[:, 0:6] if False else self.dtb[:, 0:8], AF.Identity, [self.t_p], [dmt])

    def tmp(self, name, shape, dt):
        name = f"{name}_{getattr(self, 'par', 0)}"
        if not hasattr(self, "_tmp"):
            self._tmp = {}
        if name not in self._tmp:
            self._tmp[name] = (self.kb.sb(name, shape, dt), self.kb.tile(name))
        return self._tmp[name]

    def chunk(self, ub, ubt, s0, c, main):
        kb = self.kb
        co = c * 128
        gch = (s0 + co) // 128
        self.par = gch % 2
        ident, onesr = self.ident, self.onesr
        tc_, tp = self.t_c, self.t_p
        bx, bxt = self.B(2, "xs")
        for fc in range(4):
            for tap in range(5):
                mm(kb, bx[:, fc * 128:(fc + 1) * 128], self.pre[:, fc, co + tap:co + tap + 128],
                   self.diag[:, fc, tap, :], tap == 0, False, [self.t_pre, tp], [bxt], inc=False)
            mm(kb, bx[:, fc * 128:(fc + 1) * 128], onesr[0:1, :], self.cbr[0:1, fc * 128:(fc + 1) * 128],
               False, True, [tc_, tp], [bxt], inc=(fc == 3))
        xs, xst = self.tmp("xs", [128, 512], F32)
        act(kb, xs[:], bx[:, :], AF.Silu, [bxt], [xst])
        bs, bst = self.B(1, "small")
        for tap in range(5):
            mm(kb, bs[:, 128:256], self.pre[:, 4, co + tap:co + tap + 128], self.diag[:, 4, tap, :],
               tap == 0, False, [self.t_pre, tp], [bst], inc=False)
        mm(kb, bs[:, 128:256], onesr[0:1, :], self.cbr[0:1, 512:640], False, True, [tc_, tp], [bst], inc=False)
        for k in range(KD):
            mm(kb, bs[:, 16:32], ub[:, k, 2 + co:2 + co + 128], self.wdt[:, k, :], k == 0, k == KD - 1,
               [ubt, tp], [bst], inc=(k == KD - 1))
        Btm, Btmt = self.tmp("Btm", [128, 128], BF16)
        act(kb, Btm[:], bs[:, 128:256], AF.Silu, [bst], [Btmt])
        if main:
            act(kb, self.Bfm[:, co:co + 128], self.Braw[:, co:co + 128], AF.Silu, [self.t_BCraw], [self.t_BC])
            act(kb, self.Cfm[:, co:co + 128], self.Craw[:, co:co + 128], AF.Silu, [self.t_BCraw], [self.t_BC])
        sm, smt = self.tmp("sm", [128, 8, 16], F32)
        tt(kb, DVE, sm[:, 0, :], bs[:, 16:32], self.dtb[:], ALU.add, [bst, tp], [smt])
        ts(kb, DVE, sm[:, 1, :], sm[:, 0, :], -1.0, None, ALU.mult, ALU.bypass, [smt], [smt])
        tt(kb, DVE, sm[:, 1, :], sm[:, 1, :], sm[:, 0, :], ALU.max, [smt], [smt])
        act(kb, sm[:, 1, :], sm[:, 1, :], AF.Exp, [smt], [smt], scale=-1.0)
        act(kb, sm[:, 1, :], sm[:, 1, :], AF.Ln, [smt], [smt], bias=1.0)
        stt(kb, DVE, sm[:, 2, :], sm[:, 0, :], 0.0, sm[:, 1, :], ALU.max, ALU.add, [smt], [smt])
        tt(kb, DVE, sm[:, 3, :], sm[:, 2, :], self.abc[:], ALU.mult, [smt, tp], [smt])
        mm(kb, bs[:, 32:40], self.Um[:], sm[:, 3, 0:8], True, True, [tc_, smt], [bst], inc=False)
        mm(kb, bs[:, 40:48], self.Lm[:], sm[:, 3, 8:16], True, True, [tc_, smt], [bst], inc=False)
        mm(kb, bs[:, 48:64], self.ones32[:], sm[:, 3, :], True, True, [tc_, smt], [bst], inc=True)
        sm2, sm2t = self.tmp("sm2", [128, 6, 16], F32)
        tcopy(kb, DVE, sm2[:, 0, :], bs[:, 32:48], [bst], [sm2t])
        ts(kb, DVE, sm2[:, 1, :], sm2[:, 0, :], -1.0, None, ALU.mult, ALU.bypass, [sm2t], [sm2t])
        act(kb, sm2[:, 2, :], sm2[:, 0, :], AF.Exp, [sm2t], [sm2t])
        tt(kb, DVE, sm2[:, 3, :], bs[:, 48:64], sm2[:, 0, :], ALU.subtract, [bst, sm2t], [sm2t])
        act(kb, sm2[:, 3, :], sm2[:, 3, :], AF.Exp, [sm2t], [sm2t])
        act(kb, sm2[:, 4, :], bs[:, 48:64], AF.Exp, [bst], [sm2t])
        tt(kb, DVE, sm2[:, 5, :], sm2[:, 3, :], sm[:, 2, :], ALU.mult, [sm2t, smt], [sm2t])
        xs3 = xs[:].rearrange("p (h q) -> p h q", h=8)

        def bc(ap):
            return ap.unsqueeze(2).to_broadcast([128, 8, 64])

        dirs = (0,) if main else (1,)
        xte = {}
        for d in dirs:
            xt_, xtt = self.tmp(f"xte{d}", [128, 8, 64], BF16)
            tt(kb, POOL, xt_[:], xs3, bc(sm2[:, 5, d * 8:(d + 1) * 8]), ALU.mult,
               [xst, sm2t], [xtt])
            xte[d] = (xt_, xtt)
        if main:
            if self.dbg == 30:
                memset(kb, DVE, self.ynT[:, :, co:co + 128], 0.0, [self.t_ynT])
                return
            bz, bzt = self.B(3, "z")
            for k in range(KD):
                mm(kb, bz[:, :], ub[:, k, 2 + co:2 + co + 128], self.wz[:, k, :], k == 0, k == KD - 1,
                   [ubt, tp], [bzt], inc=(k == KD - 1))
            sz, szt = self.tmp("sz", [128, 512], F32)
            act(kb, sz[:], bz[:, :], AF.Silu, [bzt], [szt])
            if self.dbg == 31:
                memset(kb, DVE, self.ynT[:, :, co:co + 128], 0.0, [self.t_ynT])
                return
            xdt = {}
            for d in (0, 1):
                x_, x_t = self.tmp(f"xdt{d}", [128, 8, 64], BF16)
                tt(kb, DVE if d == 0 else POOL, x_[:], xs3, bc(sm[:, 2, d * 8:(d + 1) * 8]), ALU.mult,
                   [xst, smt], [x_t])
                xdt[d] = (x_, x_t)
            xsd, xsdt = self.tmp("xsd", [128, 8, 64], BF16)
            tt(kb, POOL, xsd[:], xs3, bc(self.dsk[:]), ALU.mult, [xst, tp], [xsdt])
            if self.dbg == 32:
                memset(kb, DVE, self.ynT[:, :, co:co + 128], 0.0, [self.t_ynT])
                return
            mm(kb, bs[:, 256:384], self.Bfm[:, co:co + 128], self.Cfm[:, co:co + 128], True, True,
               [self.t_BC], [bst], inc=True)
            cbT, cbTt = self.tmp("cbT", [128, 128], F32)
            tcopy(kb, ACT, cbT[:], bs[:, 256:384], [bst], [cbTt])
            if self.dbg == 33:
                memset(kb, DVE, self.ynT[:, :, co:co + 128], 0.0, [self.t_ynT])
                return
            W = {}
            for d in (0, 1):
                for hq in range(2):
                    br, brt = self.B(4 + (d * 2 + hq) % 2, "R")
                    for hh in range(4):
                        h = hq * 4 + hh
                        col = d * 8 + h
                        mm(kb, br[:, hh * 128:(hh + 1) * 128], sm[:, 3, col:col + 1].to_broadcast([128, 128]),
                           self.Um[:] if d == 0 else self.Lm[:], True, False, [smt, tc_], [brt], inc=False)
                        mm(kb, br[:, hh * 128:(hh + 1) * 128], ident[:], (self.nmf if d == 0 else self.nmb)[:],
                           False, True, [tc_], [brt], inc=(hh == 3))
                    Dm, Dmt = self.tmp(f"Dm{hq}", [128, 4, 128], F32)
                    for hh in range(4):
                        col = d * 8 + hq * 4 + hh
                        act(kb, Dm[:, hh, :], br[:, hh * 128:(hh + 1) * 128], AF.Exp, [brt, sm2t], [Dmt],
                            bias=sm2[:, 1, col:col + 1])
                    Wt, Wtt = self.tmp(f"W{d}{hq}", [128, 4, 128], BF16)
                    tt(kb, DVE, Wt[:], Dm[:], cbT[:].unsqueeze(1).to_broadcast([128, 4, 128]), ALU.mult,
                       [Dmt, cbTt], [Wtt])
                    W[(d, hq)] = (Wt, Wtt)
            if self.dbg == 3:
                memset(kb, DVE, self.ynT[:, :, co:co + 128], 0.0, [self.t_ynT])
                return
            by, byt = self.B(6, "Y")
            for h in range(8):
                hq, hh = h // 4, h % 4
                mm(kb, by[:, h * 64:(h + 1) * 64], W[(0, hq)][0][:, hh, :], xdt[0][0][:, h, :], True, False,
                   [W[(0, hq)][1], xdt[0][1]], [byt], inc=False)
                mm(kb, by[:, h * 64:(h + 1) * 64], W[(1, hq)][0][:, hh, :], xdt[1][0][:, h, :], False, False,
                   [W[(1, hq)][1], xdt[1][1]], [byt], inc=False)
                mm(kb, by[:, h * 64:(h + 1) * 64], ident[:], xsd[:, h, :], False, True, [tc_, xsdt], [byt],
                   inc=(h == 7))
            if self.dbg == 4:
                memset(kb, DVE, self.ynT[:, :, co:co + 128], 0.0, [self.t_ynT])
                return
            bo0, bo0t = self.B(2, "xs")
            mm(kb, bo0[:, :], self.Cfm[:, co:co + 128], self.Sbf[:], True, True, [self.t_BC, self.t_Sbf], [bo0t],
               inc=True)
            bo1, bo1t = self.B(3, "z")
            mm(kb, bo1[:, :], self.Cfm[:, co:co + 128], self.Sprev_b[:, gch, :], True, True,
               [self.t_BC, self.t_Sprev[gch]], [bo1t], inc=True)
            y, yt_ = self.tmp("y", [128, 512], F32)
            t1, t1t = self.tmp("t1", [128, 512], F32)
            y3 = y[:].rearrange("p (h q) -> p h q", h=8)
            t13 = t1[:].rearrange("p (h q) -> p h q", h=8)
            tt(kb, DVE, y3, bo0[:, :].rearrange("p (h q) -> p h q", h=8), bc(sm2[:, 2, 0:8]), ALU.mult,
               [bo0t, sm2t], [yt_])
            tt(kb, DVE, t13, bo1[:, :].rearrange("p (h q) -> p h q", h=8), bc(sm2[:, 2, 8:16]), ALU.mult,
               [bo1t, sm2t], [t1t])
            tt(kb, DVE, y[:], y[:], by[:, :], ALU.add, [yt_, byt], [yt_])
            tt(kb, POOL, y[:], y[:], t1[:], ALU.add, [yt_, t1t], [yt_])
            if self.dbg == 5:
                memset(kb, DVE, self.ynT[:, :, co:co + 128], 0.0, [self.t_ynT])
                return
            tt(kb, DVE, y[:], y[:], sz[:], ALU.mult, [yt_, szt], [yt_])
            ss, sst = self.tmp("ss", [128, 2], F32)
            memset(kb, POOL, ss[:], 0.0, [sst])
            act(kb, t1[:], y[:], AF.Square, [yt_, t1t], [t1t, sst], accum_out=ss[:, 0:1])
            act(kb, ss[:, 1:2], ss[:, 0:1], AF.Sqrt, [sst], [sst], bias=EPS, scale=1.0 / 512)
            kb.op(DVE, lambda e, ss=ss: e.reciprocal(out=ss[:, 1:2], in_=ss[:, 1:2]), [sst], [sst])
            ynb, ynbt = self.tmp("ynb", [128, 512], BF16)
            stt(kb, DVE, ynb[:], y[:], ss[:, 1:2], self.nwb[:], ALU.mult, ALU.mult, [yt_, sst, tp], [ynbt])
            if self.dbg == 6:
                memset(kb, DVE, self.ynT[:, :, co:co + 128], 0.0, [self.t_ynT])
                return
            btr, btrt = self.B(0, "pre")
            trv = btr[:, 0:256].bitcast(BF16)
            for fc in range(4):
                kb.op(PE, lambda e, fc=fc, trv=trv, ynb=ynb: e.transpose(trv[:, fc * 128:(fc + 1) * 128],
                                                                       ynb[:, fc * 128:(fc + 1) * 128], ident[:]),
                      [ynbt, tc_], [btrt], inc=(fc == 3))
            tcopy(kb, ACT, self.ynT[:, :, co:co + 128], trv.rearrange("p (f t) -> p f t", f=4), [btrt], [self.t_ynT])
        for d in dirs:
            if d == 1:
                tcopy(kb, ACT, self.Sprev_b[:, gch, :], self.S[1][:], [self.t_S[1]], [self.t_Sprev[gch]])
            bst_, bstt = self.B(7 if d == 0 else 6, "St")
            if not main:
                bst_, bstt = self.B(6, "St")
            mm(kb, bst_[:, :], Btm[:], xte[d][0][:].rearrange("p h q -> p (h q)"), True, True,
               [Btmt, xte[d][1]], [bstt], inc=True)
            S3 = self.S[d][:].rearrange("p (h q) -> p h q", h=8)
            tt(kb, DVE, S3, S3, bc(sm2[:, 4, d * 8:(d + 1) * 8]), ALU.mult, [self.t_S[d], sm2t], [self.t_S[d]])
            tt(kb, DVE, self.S[d][:], self.S[d][:], bst_[:, :], ALU.add, [self.t_S[d], bstt], [self.t_S[d]])
            if d == 0:
                tcopy(kb, ACT, self.Sbf[:], self.S[0][:], [self.t_S[0]], [self.t_Sbf])


def prep_ssd(inp, j, g):
    f = np.float32
    w_in = np.asarray(inp["ssd_in"], f)[j]
    m = {}
    m["w_z"] = np.ascontiguousarray(w_in[:, g * 512:(g + 1) * 512])
    m["w_x"] = np.ascontiguousarray(np.concatenate([
        w_in[:, 2048 + g * 512:2048 + (g + 1) * 512],
        w_in[:, 4096 + g * 128:4096 + (g + 1) * 128],
        w_in[:, 4608 + g * 128:4608 + (g + 1) * 128]], axis=1))
    m["w_dt"] = np.ascontiguousarray(np.concatenate([
        w_in[:, 5120 + g * 8:5120 + (g + 1) * 8],
        w_in[:, 5152 + g * 8:5152 + (g + 1) * 8]], axis=1))
    ch = np.concatenate([np.arange(g * 512, (g + 1) * 512), 2048 + np.arange(g * 128, (g + 1) * 128),
                         2560 + np.arange(g * 128, (g + 1) * 128)])
    cw = np.asarray(inp["ssd_conv_w"], f)[j][:, ch]
    cb = np.asarray(inp["ssd_conv_b"], f)[j][ch]
    m["cw_p"] = np.ascontiguousarray(cw.reshape(5, 6, 128).transpose(2, 1, 0))
    m["cb_p"] = np.ascontiguousarray(cb.reshape(6, 128).T)
    m["cb_r"] = np.ascontiguousarray(cb[None, :])
    hs = slice(g * 8, (g + 1) * 8)
    dtb = np.asarray(inp["ssd_dt_bias"], f)[j][:, hs].reshape(16)
    alog = np.asarray(inp["ssd_a_log"], f)[j][:, hs].reshape(16)
    m["dtb_bc"] = np.ascontiguousarray(np.broadcast_to(dtb[None], (128, 16)))
    m["alog_bc"] = np.ascontiguousarray(np.broadcast_to(alog[None], (128, 16)))
    m["dsk_bc"] = np.ascontiguousarray(np.broadcast_to(np.asarray(inp["ssd_d"], f)[j][hs][None], (128, 8)))
    m["nw_bc"] = np.ascontiguousarray(np.broadcast_to(np.asarray(inp["ssd_norm"], f)[j][g * 512:(g + 1) * 512][None], (128, 512)))
    return m


class ModProg:
    def __init__(self):
        self.nc = bass.Bass("TRN2", target_bir_lowering=False)
        self.dram = {}
        nc = self.nc
        with ExitStack() as es:
            kb = KB(nc, es)
            self.kb = kb
            cv = kb.sb("cv3", [128, KD, 3], F32); tcv = kb.tile("cv3")
            scv = kb.sb("scv3", [128, KD, 3], BF16); tscv = kb.tile("scv3")
            kb.dma(SP, cv[:], self.din("cvec3", [128, KD, 3]), "cv3", writes=[tcv])
            act(kb, scv[:], cv[:], AF.Silu, [tcv], [tscv])
            bank = kb.ps("bank0", [128, 512], F32); tb = kb.tile("bank0")
            res = kb.sb("res", [128, 4, 9, 3], F32); tres = kb.tile("res")
            aw = self.din("ada_w_sh", [4, D, 1152])
            ws = [kb.sb(f"w{x}", [128, KD, 1152], BF16) for x in range(2)]
            for i in range(4):
                w = ws[i % 2]; wt = kb.tile(f"w{i % 2}")
                kb.dma(POOL, w[:], aw[i].rearrange("(k p) c -> p k c", p=128), f"w{i % 2}", writes=[wt])
                for jj in range(9):
                    for k in range(KD):
                        mm(kb, bank[:, (i * 9 + jj) * 3:(i * 9 + jj) * 3 + 3], w[:, k, jj * 128:(jj + 1) * 128],
                           scv[:, k, :], k == 0, k == KD - 1, [wt, tscv], [tb], inc=(k == KD - 1))
            tcopy(kb, DVE, res[:], bank[:, 0:108].rearrange("p (i j c) -> p i j c", i=4, j=9), [tb], [tres])
            kb.dma(SP, self.dout("mod_part", [128, 4, 9, 3]), res[:], "res", reads=[tres])
            kb.barrier()
            kb.emit()

    din = TokProg.din
    dout = TokProg.dout


_PROGS = {}


def _prog(key, ctor):
    if key not in _PROGS:
        _PROGS[key] = ctor()
    return _PROGS[key]


def run_prog(prog, in_maps):
    names = set(prog.dram.keys())
    outs = ("u_out", "lat_out", "out_fm", "yn", "mod_part", "sprev_out")
    maps = [{k: v for k, v in m.items() if k in names} for m in in_maps]
    for m in maps:
        missing = [k for k in names if k not in m and k not in outs]
        assert not missing, missing
    res = run_bass_kernel_spmd(prog.nc, maps, core_ids=list(range(NCORES)))
    return res.results


def kernel(x, c, ctx, c_ctx, ada_w, ada_b, norm_g, ffn_wg, ffn_wu, ffn_wd,
           ssd_in, ssd_conv_w, ssd_conv_b, ssd_dt_bias, ssd_a_log, ssd_d, ssd_norm, ssd_out,
           sc_in, sc_conv, sc_out, gm_in, gm_vnorm, gm_ws, gm_bs, gm_out, final_norm):
    f = np.float32
    inp = dict(x=x, c=c, ctx=ctx, c_ctx=c_ctx, ada_w=ada_w, ada_b=ada_b, norm_g=norm_g, ffn_wg=ffn_wg,
               ffn_wu=ffn_wu, ffn_wd=ffn_wd, ssd_in=ssd_in, ssd_conv_w=ssd_conv_w, ssd_conv_b=ssd_conv_b,
               ssd_dt_bias=ssd_dt_bias, ssd_a_log=ssd_a_log, ssd_d=ssd_d, ssd_norm=ssd_norm, ssd_out=ssd_out,
               sc_in=sc_in, sc_conv=sc_conv, sc_out=sc_out, gm_in=gm_in, gm_vnorm=gm_vnorm, gm_ws=gm_ws,
               gm_bs=gm_bs, gm_out=gm_out, final_norm=final_norm)
    inp = {k: np.asarray(v) for k, v in inp.items()}
    B, SEQ = inp["x"].shape[0], inp["x"].shape[1]
    TL = SEQ // 4
    bf = ml_dtypes.bfloat16
    com = {}
    com["ada_b_l"] = np.ascontiguousarray(inp["ada_b"].astype(f).reshape(4, 72, 128).transpose(2, 0, 1))
    com["norm_g_l"] = np.ascontiguousarray(inp["norm_g"].astype(f).reshape(4, 3, KD, 128).transpose(3, 0, 1, 2))
    for i in range(4):
        for hs in range(2):
            com[f"wg_{i}_{hs}"] = np.ascontiguousarray(inp["ffn_wg"][i, hs], f)
            com[f"wu_{i}_{hs}"] = np.ascontiguousarray(inp["ffn_wu"][i, hs], f)
            com[f"wd_{i}_{hs}"] = np.ascontiguousarray(inp["ffn_wd"][i, hs], f)
    com["final_norm_l"] = np.ascontiguousarray(inp["final_norm"].astype(f).reshape(KD, 128).T)
    com["sc_in"] = inp["sc_in"].astype(f)
    com["sc_out"] = inp["sc_out"].astype(f)
    com["sc_conv_l"] = np.ascontiguousarray(inp["sc_conv"].astype(f)[0].reshape(3, KD, 128).transpose(2, 0, 1))
    com["gm_in"] = inp["gm_in"].astype(f)
    com["gm_out"] = inp["gm_out"].astype(f)
    com["gm_wsT_l"] = np.ascontiguousarray(inp["gm_ws"].astype(f)[0].transpose(2, 0, 1))
    com["gm_bs_l"] = np.ascontiguousarray(inp["gm_bs"].astype(f)[0][None])
    com["gm_vn_l"] = np.ascontiguousarray(np.broadcast_to(inp["gm_vnorm"].astype(f)[0][None, :], (128, 2048)))
    com["ssd_out0"] = np.ascontiguousarray(inp["ssd_out"][0], f)
    com["ssd_out1"] = np.ascontiguousarray(inp["ssd_out"][1], f)

    cv3 = np.empty((128, KD, 3), f)
    cv3[:, :, 0] = inp["c"].astype(f)[0].reshape(KD, 128).T
    cv3[:, :, 1] = inp["c"].astype(f)[1].reshape(KD, 128).T
    cv3[:, :, 2] = inp["c_ctx"].astype(f).reshape(KD, 128).T
    mp = _prog("mod", ModProg)
    aw = inp["ada_w"].astype(f)
    rm = run_prog(mp, [{"cvec3": cv3, "ada_w_sh": np.ascontiguousarray(aw[:, :, r * 1152:(r + 1) * 1152])}
                       for r in range(NCORES)])
    mod_in = []
    for b in range(B):
        m = np.empty((128, 4, 72, 2), f)
        for r in range(NCORES):
            m[:, :, r * 9:(r + 1) * 9, 0] = rm[r]["mod_part"][:, :, :, b]
            m[:, :, r * 9:(r + 1) * 9, 1] = rm[r]["mod_part"][:, :, :, 2]
        mod_in.append(m)

    def cores():
        for core in range(NCORES):
            yield core, core // 4, core % 4

    maps = []
    for core, b, r in cores():
        m = dict(com)
        m["lat_in"] = np.concatenate([fm(inp["x"][b, r * TL:(r + 1) * TL].astype(f)), fm(inp["ctx"][b].astype(f))], axis=2)
        m["mod_in"] = mod_in[b]
        maps.append(m)
    pA = _prog(("tokA", TL), lambda: TokProg(TL, [("modin",), ("ffn", 0, 0), ("umix", 0), ("latout",)]))
    rA = run_prog(pA, maps)

    def ssd_layer(res_tok, j):
        smaps = []
        for core, b, g in cores():
            m = prep_ssd(inp, j, g)
            parts = [res_tok[b * 4]["u_out"][:, :, TL:TL + CT]] + [res_tok[b * 4 + r]["u_out"][:, :, 0:TL] for r in range(4)]
            m["u_all"] = np.ascontiguousarray(np.concatenate(parts, axis=2))
            smaps.append(m)
        p2 = _prog(("ssd", SEQ), lambda: SsdProg(SEQ, phase="both"))
        r2 = run_prog(p2, smaps)
        yn_in = []
        for core, b, r in cores():
            y = np.empty((128, 16, TL + CT), bf)
            for g in range(4):
                yg = r2[b * 4 + g]["yn"]
                y[:, g * 4:(g + 1) * 4, 0:TL] = yg[:, :, CT + r * TL:CT + (r + 1) * TL]
                y[:, g * 4:(g + 1) * 4, TL:TL + CT] = yg[:, :, 0:CT]
            yn_in.append(y)
        return yn_in

    yn0 = ssd_layer(rA, 0)
    maps = []
    for core, b, r in cores():
        m = dict(com)
        m["lat_in"] = rA[core]["lat_out"]
        m["mod_in"] = mod_in[b]
        m["yn_in"] = yn0[core]
        maps.append(m)
    stB = [("modin",), ("ssdout", 0, 0), ("ffn", 0, 1),
           ("ffn", 1, 0), ("sconv", 1), ("ffn", 1, 1),
           ("ffn", 2, 0), ("gmlp", 2), ("ffn", 2, 1),
           ("ffn", 3, 0), ("umix", 3), ("latout",)]
    pB = _prog(("tokB", TL), lambda: TokProg(TL, stB))
    rB = run_prog(pB, maps)
    yn3 = ssd_layer(rB, 1)
    maps = []
    for core, b, r in cores():
        m = dict(com)
        m["lat_in"] = rB[core]["lat_out"]
        m["mod_in"] = mod_in[b]
        m["yn_in"] = yn3[core]
        maps.append(m)
    pC = _prog(("tokC", TL), lambda: TokProg(TL, [("modin",), ("ssdout", 3, 1, False), ("ffn", 3, 1, False), ("final",)]))
    rC = run_prog(pC, maps)
    out = np.empty((B, SEQ, D), f)
    for core, b, r in cores():
        out[b, r * TL:(r + 1) * TL] = unfm(rC[core]["out_fm"])
    return out
```

```python
import numpy as np
import os
import ml_dtypes
import concourse.bass as bass
import concourse.mybir as mybir
from concourse.bass_utils import run_bass_kernel_spmd
from contextlib import ExitStack

F32 = mybir.dt.float32
BF16 = mybir.dt.bfloat16
AF = mybir.ActivationFunctionType
ALU = mybir.AluOpType

PE, ACT, DVE, POOL, SP = "pe", "act", "dve", "pool", "sp"
ENGS = (PE, ACT, DVE, POOL, SP)
EPOCH = 24000

D = 1024
KD = 8
FF = 2816
NJ = 22
CT = 256
EPS = 1e-6
NCORES = 8


class Tile:
    __slots__ = ("name", "w", "r", "olds")

    def __init__(self, name):
        self.name = name
        self.w = None
        self.r = {}
        self.olds = []


class KB:
    def __init__(self, nc, es):
        self.nc = nc
        self.es = es
        self.q = {e: [] for e in ENGS}
        self.sem = {}
        self.cnt = {e: 0 for e in ENGS}
        self.epoch = {e: 0 for e in ENGS}
        self.pending = {e: False for e in ENGS}
        self.seen = {e: {} for e in ENGS}
        self.dcnt = {}
        self.final = {}
        self.tiles = {}

    def sb(self, name, shape, dt):
        return self.es.enter_context(self.nc.sbuf_tensor("sb_" + name, list(shape), dt))

    def ps(self, name, shape, dt):
        return self.es.enter_context(self.nc.psum_tensor("ps_" + name, list(shape), dt))

    def tile(self, name):
        if name not in self.tiles:
            self.tiles[name] = Tile(name)
        return self.tiles[name]

    def _sem(self, key):
        if key not in self.sem:
            nm = "s_" + "_".join(str(k) for k in key)
            self.sem[key] = self.es.enter_context(self.nc.semaphore(nm))
        return self.sem[key]

    def _deps(self, reads, writes):
        deps = {}

        def add(ev):
            if ev is None:
                return
            k, v = ev
            if deps.get(k, 0) < v:
                deps[k] = v

        def addall(t):
            add(t.w)
            for k, v in t.r.items():
                add((k, v))

        for t in reads:
            add(t.w)
        for t in writes:
            addall(t)
            for o in t.olds:
                addall(o)
            t.olds = []
        return deps

    def _waits(self, eng, deps, skip_same=True):
        waits = []
        for k, v in deps.items():
            if skip_same and k[0] == eng:
                continue
            if self.seen[eng].get(k, 0) >= v:
                continue
            self.seen[eng][k] = v
            self._sem(k)
            waits.append((k, v))
        return waits

    def op(self, eng, fn, reads=(), writes=(), inc=True):
        deps = self._deps(reads, writes)
        waits = self._waits(eng, deps, skip_same=(eng == PE))
        if inc and not self.pending[eng] and self.cnt[eng] >= EPOCH:
            self.final[(eng, self.epoch[eng])] = self.cnt[eng]
            self.epoch[eng] += 1
            self.cnt[eng] = 0
        key = (eng, self.epoch[eng])
        val = self.cnt[eng] + 1
        if inc:
            self.cnt[eng] = val
            self.pending[eng] = False
            self._sem(key)
        else:
            self.pending[eng] = True
        self.q[eng].append((waits, fn, key if inc else None, 1))
        if os.environ.get("KBLOG"):
            self.log = getattr(self, "log", [])
            self.log.append((eng, val if inc else None, [t.name for t in reads], [t.name for t in writes], list(waits)))
        for t in writes:
            t.w = (key, val)
            t.r = {}
        for t in reads:
            if t.r.get(key, 0) < val:
                t.r[key] = val

    def dma(self, eng, out, in_, key, reads=(), writes=(), chain=True, fn=None, incv=16):
        key = ("dma", key)
        deps = self._deps(reads, writes)
        if chain and self.dcnt.get(key, 0) > 0:
            if deps.get(key, 0) < self.dcnt[key]:
                deps[key] = self.dcnt[key]
        waits = self._waits(eng, deps, skip_same=False)
        val = self.dcnt.get(key, 0) + incv
        self.dcnt[key] = val
        self._sem(key)

        if fn is None:
            def fn(e, out=out, in_=in_):
                return e.dma_start(out=out, in_=in_)

        self.q[eng].append((waits, fn, key, incv))
        for t in writes:
            t.w = (key, val)
            t.r = {}
        for t in reads:
            if t.r.get(key, 0) < val:
                t.r[key] = val

    def barrier(self, engs=ENGS):
        deps = {}
        for e in ENGS:
            for ep in range(self.epoch[e] + 1):
                k = (e, ep)
                if k in self.sem:
                    v = self.cnt[e] if ep == self.epoch[e] else self.final[k]
                    if v:
                        deps[k] = v
        for k, v in self.dcnt.items():
            deps[k] = v
        for e in engs:
            assert not self.pending[e]
            waits = self._waits(e, dict(deps))
            if waits:
                self.q[e].append((waits, None, None, 0))

    def emit(self):
        block = self.es.enter_context(self.nc.Block())
        kb = self

        def run(eng, e):
            for waits, fn, key, n in kb.q[eng]:
                for k, v in waits:
                    e.wait_ge(kb.sem[k], v)
                if fn is not None:
                    ins = fn(e)
                    if key is not None:
                        ins.then_inc(kb.sem[key], n)

        @block.tensor
        def _(e):
            run(PE, e)

        @block.scalar
        def _(e):
            run(ACT, e)

        @block.vector
        def _(e):
            run(DVE, e)

        @block.gpsimd
        def _(e):
            run(POOL, e)

        @block.sync
        def _(e):
            run(SP, e)

    def stats(self):
        return {e: len(self.q[e]) for e in ENGS}, len(self.sem)


class Ring:
    def __init__(self, kb, name, nelem, dt, nsem=6):
        self.kb = kb
        self.name = name
        self.buf = kb.sb(name, [128, nelem], dt)
        self.n = nelem
        self.ptr = 0
        self.live = []
        self.c = 0
        self.nsem = nsem

    def alloc(self, n):
        assert n <= self.n
        if self.ptr + n > self.n:
            self.ptr = 0
        s, e = self.ptr, self.ptr + n
        self.ptr = e
        olds = [t for (a, b, t) in self.live if a < e and s < b]
        self.live = [(a, b, t) for (a, b, t) in self.live if not (a < e and s < b)]
        self.c += 1
        t = Tile(f"{self.name}{self.c}")
        t.olds = olds
        self.live.append((s, e, t))
        return self.buf[:, s:e], t

    def semkey(self):
        return f"{self.name}_{self.c % self.nsem}"


def mm(kb, out, lhsT, rhs, start, stop, reads, writes, inc):
    kb.op(PE, lambda e: e.matmul(out, lhsT=lhsT, rhs=rhs, start=start, stop=stop), reads, writes, inc)


def act(kb, out, in_, func, reads, writes, bias=0.0, scale=1.0, accum_out=None):
    if accum_out is None:
        kb.op(ACT, lambda e: e.activation(out=out, in_=in_, func=func, bias=bias, scale=scale), reads, writes)
    else:
        kb.op(ACT, lambda e: e.activation(out=out, in_=in_, func=func, bias=bias, scale=scale,
                                          accum_out=accum_out), reads, writes)


def tt(kb, eng, out, in0, in1, op, reads, writes):
    kb.op(eng, lambda e: e.tensor_tensor(out=out, in0=in0, in1=in1, op=op), reads, writes)


def ts(kb, eng, out, in0, s1, s2, op0, op1, reads, writes):
    if s2 is None:
        kb.op(eng, lambda e: e.tensor_single_scalar(out=out, in_=in0, scalar=s1, op=op0), reads, writes)
    else:
        kb.op(eng, lambda e: e.tensor_scalar(out=out, in0=in0, scalar1=s1, scalar2=s2, op0=op0, op1=op1),
              reads, writes)


def stt(kb, eng, out, in0, scalar, in1, op0, op1, reads, writes):
    kb.op(eng, lambda e: e.scalar_tensor_tensor(out=out, in0=in0, scalar=scalar, in1=in1, op0=op0, op1=op1),
          reads, writes)


def tcopy(kb, eng, out, in_, reads, writes):
    if eng == ACT:
        kb.op(ACT, lambda e: e.copy(out=out, in_=in_), reads, writes)
    else:
        kb.op(eng, lambda e: e.tensor_copy(out=out, in_=in_), reads, writes)


def memset(kb, eng, ap, val, writes):
    kb.op(eng, lambda e: e.memset(ap, val), (), writes)


class TokProg:
    def __init__(self, TL, stages, name="tok"):
        self.TL = TL
        self.TT = TL + CT
        self.stages = stages
        blocks = []
        t0 = 0
        while t0 < TL:
            n = min(512, TL - t0)
            blocks.append((t0, n, 0))
            t0 += n
        blocks.append((TL, CT, 1))
        self.blocks = blocks
        self.nc = bass.Bass("TRN2", target_bir_lowering=False)
        self.dram = {}
        self.build()

    def din(self, name, shape, dt=F32):
        if name not in self.dram:
            self.dram[name] = self.nc.dram_tensor(name, list(shape), dt, kind="ExternalInput").ap()
        return self.dram[name]

    def dout(self, name, shape, dt=F32):
        if name not in self.dram:
            self.dram[name] = self.nc.dram_tensor(name, list(shape), dt, kind="ExternalOutput").ap()
        return self.dram[name]

    def build(self):
        nc = self.nc
        TT = self.TT
        with ExitStack() as es:
            kb = KB(nc, es)
            self.kb = kb
            self.lat = kb.sb("lat", [128, KD, TT], F32)
            self.u = kb.sb("u", [128, KD, TT], BF16)
            self.lat_t = [kb.tile(f"lat{b}") for b in range(len(self.blocks))]
            self.u_t = [kb.tile(f"u{b}") for b in range(len(self.blocks))]
            self.wring = Ring(kb, "wr", 18432, BF16, nsem=6)
            self.sring = Ring(kb, "sr", 3072, F32)
            self.bigA = kb.sb("bigA", [128, 4096], F32)
            self.bigB = kb.sb("bigB", [128, 4096], F32)
            self.t_bigA = kb.tile("bigA")
            self.t_bigB = kb.tile("bigB")
            self.rsb = [kb.sb(f"rs{x}", [128, 512], F32) for x in range(2)]
            self.t_rsb = [kb.tile(f"rs{x}") for x in range(2)]
            self.rsi = 0
            self.banks = [kb.ps(f"bank{i}", [128, 512], F32) for i in range(8)]
            self.bank_t = [kb.tile(f"bank{i}") for i in range(8)]
            self.rot = {"G": 0, "U": 0, "O": 0, "M": 0}
            self.ones_bf = kb.sb("ones_bf", [128, 128], BF16)
            self.t_ones = kb.tile("ones_bf")
            memset(kb, POOL, self.ones_bf[:], 1.0, [self.t_ones])
            self.cvec = kb.sb("cvec", [128, KD, 2], F32)
            self.scv = kb.sb("scv", [128, KD, 2], BF16)
            self.t_scv = kb.tile("scv")
            if any(st[0] == "mod" for st in self.stages):
                tcv = kb.tile("cvec")
                kb.dma(SP, self.cvec[:], self.din("cvec", [128, KD, 2]), "cvec", writes=[tcv])
                act(kb, self.scv[:], self.cvec[:], AF.Silu, [tcv], [self.t_scv])
            self.adab = kb.sb("adab", [128, 4, 72], F32)
            self.t_adab = kb.tile("adab")
            kb.dma(SP, self.adab[:], self.din("ada_b_l", [128, 4, 72]), "adab", writes=[self.t_adab])
            self.ng = kb.sb("ng", [128, 4, 3, KD], F32)
            self.t_ng = kb.tile("ng")
            kb.dma(SP, self.ng[:], self.din("norm_g_l", [128, 4, 3, KD]), "ng", writes=[self.t_ng])
            self.modraw = [kb.sb(f"modraw{i}", [128, 72, 2], F32) for i in range(4)]
            self.modA = [kb.sb(f"modA{i}", [128, 3, KD, 2], F32) for i in range(4)]
            self.modG = [kb.sb(f"modG{i}", [128, 3, KD, 2], F32) for i in range(4)]
            self.t_mod = [kb.tile(f"mod{i}") for i in range(4)]
            lat_in = self.din("lat_in", [128, KD, TT])
            for b, (t0, n, col) in enumerate(self.blocks):
                kb.dma(SP, self.lat[:, :, t0:t0 + n], lat_in[:, :, t0:t0 + n], f"latio{b}", writes=[self.lat_t[b]])
            for st in self.stages:
                getattr(self, "st_" + st[0])(*st[1:])
            kb.barrier()
            self.stat = kb.stats()
            kb.emit()

    def bank(self, role):
        base = {"G": 0, "U": 2, "O": 4, "M": 6}[role]
        i = base + self.rot[role]
        self.rot[role] ^= 1
        return self.banks[i], self.bank_t[i]

    def wload(self, src_ap, shape_str=None, **kw):
        n = 1
        for s in src_ap.shape[1:]:
            n *= s
        ap, t = self.wring.alloc(n)
        dst = ap
        if len(src_ap.shape) == 3:
            dst = ap.rearrange("p (a b) -> p a b", a=src_ap.shape[1])
        elif len(src_ap.shape) == 4:
            dst = ap.rearrange("p (a b c) -> p a b c", a=src_ap.shape[1], b=src_ap.shape[2])
        self.kb.dma(POOL, dst, src_ap, self.wring.semkey(), writes=[t])
        return dst, t

    def scratch(self, nelem, dt=F32):
        n32 = nelem if dt == F32 else (nelem + 1) // 2
        ap, t = self.sring.alloc(n32)
        if dt != F32:
            ap = ap.bitcast(dt)[:, 0:nelem]
        return ap, t

    def st_mod(self, i):
        kb = self.kb
        par = i
        src = self.din(f"ada_w{i}", [D, 9 * D]).rearrange("(k p) c -> p k c", p=128)
        bk, bt = self.bank("M")
        for cb in range(18):
            w, wt = self.wload(src[:, :, cb * 512:(cb + 1) * 512])
            for jj in range(4):
                j = cb * 4 + jj
                for k in range(KD):
                    mm(kb, bk[:, 2 * j:2 * j + 2], w[:, k, jj * 128:(jj + 1) * 128], self.scv[:, k, :],
                       k == 0, k == KD - 1, [wt, self.t_scv], [bt], inc=(k == KD - 1))
        mr = self.modraw[par]
        tm = self.t_mod[par]
        tt(kb, DVE, mr[:], bk[:, 0:144].rearrange("p (j c) -> p j c", c=2),
           self.adab[:, i, :].unsqueeze(2).to_broadcast([128, 72, 2]), ALU.add, [bt, self.t_adab], [tm])
        self.mod_derive(i)

    def st_modin(self):
        kb = self.kb
        mi = self.din("mod_in", [128, 4, 72, 2])
        for i in range(4):
            kb.dma(SP, self.modraw[i][:], mi[:, i], f"modin{i}", writes=[self.t_mod[i]])
            tt(kb, DVE, self.modraw[i][:], self.modraw[i][:],
               self.adab[:, i, :].unsqueeze(2).to_broadcast([128, 72, 2]), ALU.add, [self.t_mod[i], self.t_adab],
               [self.t_mod[i]])
            self.mod_derive(i)

    def mod_derive(self, i):
        kb = self.kb
        par = i
        mr = self.modraw[par]
        tm = self.t_mod[par]
        for s in range(3):
            stt(kb, DVE, self.modA[par][:, s], mr[:, (3 * s + 1) * 8:(3 * s + 2) * 8, :], 1.0,
                self.ng[:, i, s, :].unsqueeze(2).to_broadcast([128, KD, 2]), ALU.add, ALU.mult,
                [tm, self.t_ng], [tm])
            ts(kb, DVE, self.modG[par][:, s], mr[:, (3 * s + 2) * 8:(3 * s + 3) * 8, :],
               0.5 if s != 1 else 1.0, None, ALU.mult, ALU.bypass, [tm], [tm])

    def mod_views(self, i, s):
        par = i
        A = self.modA[par][:, s]
        S = self.modraw[par][:, 3 * s * 8:(3 * s + 1) * 8, :]
        G = self.modG[par][:, s]
        return A, S, G, self.t_mod[par]

    def rstd_block(self, b):
        kb = self.kb
        t0, n, col = self.blocks[b]
        sq, sqt = self.bigA[:].bitcast(BF16)[:, 0:KD * n].rearrange("p (k t) -> p k t", k=KD), self.t_bigA
        act(kb, sq, self.lat[:, :, t0:t0 + n], AF.Square, [self.lat_t[b]], [sqt])
        bk, bt = self.bank("M")
        for k in range(KD):
            mm(kb, bk[:, 0:n], self.ones_bf[:], sq[:, k, :], k == 0, k == KD - 1, [self.t_ones, sqt], [bt],
               inc=(k == KD - 1))
        rs, rst = self.rsb[self.rsi][:, 0:n], self.t_rsb[self.rsi]
        self.rsi ^= 1
        act(kb, rs, bk[:, 0:n], AF.Sqrt, [bt], [rst], bias=EPS, scale=1.0 / D)
        kb.op(DVE, lambda e: e.reciprocal(out=rs, in_=rs), [rst], [rst])
        return rs, rst

    def norm_mod(self, i, s, nblocks=None):
        kb = self.kb
        A, S, G, tm = self.mod_views(i, s)
        for b, (t0, n, col) in enumerate(self.blocks[:nblocks]):
            rs, rst = self.rstd_block(b)
            for k in range(KD):
                t1, t1t = self.scratch(n)
                tt(kb, DVE, t1, self.lat[:, k, t0:t0 + n], rs, ALU.mult, [self.lat_t[b], rst], [t1t])
                act(kb, self.u[:, k, t0:t0 + n], t1, AF.Identity, [t1t, tm], [self.u_t[b]],
                    bias=S[:, k, col:col + 1], scale=A[:, k, col:col + 1])

    def st_ffn(self, i, hs, do_ctx=True):
        kb = self.kb
        s = 0 if hs == 0 else 2
        nb = len(self.blocks) if do_ctx else len(self.blocks) - 1
        self.norm_mod(i, s, nb)
        A, S, G, tm = self.mod_views(i, s)
        wg = self.din(f"wg_{i}_{hs}", [D, FF]).rearrange("(k p) f -> p k f", p=128)
        wu = self.din(f"wu_{i}_{hs}", [D, FF]).rearrange("(k p) f -> p k f", p=128)
        wd = self.din(f"wd_{i}_{hs}", [FF, D]).rearrange("(j p) m -> p j m", p=128)
        hbv = self.bigB[:].bitcast(BF16)
        hbuf = [(hbv[:, x * 1024:(x + 1) * 1024], self.kb.tile(f"bigB_h{x}")) for x in range(2)]
        self.kb.tile("bigB_h0").olds.append(self.t_bigB)
        self.kb.tile("bigB_h1").olds.append(self.t_bigB)
        prev = None
        step = 0

        def cstep(pv):
            wdt_ap, wdt_t, hb, hbt, b = pv
            t0, n, col = self.blocks[b]
            for m in range(KD):
                self.o4 = (getattr(self, "o4", 0) + 1) % 4
                bk, bt = self.banks[4 + self.o4], self.bank_t[4 + self.o4]
                for jj in range(2):
                    mm(kb, bk[:, 0:n], wdt_ap[:, jj, m * 128:(m + 1) * 128], hb[:, jj, 0:n], jj == 0, jj == 1,
                       [wdt_t, hbt], [bt], inc=(jj == 1))
                stt(kb, DVE, self.lat[:, m, t0:t0 + n], bk[:, 0:n], G[:, m, col:col + 1],
                    self.lat[:, m, t0:t0 + n], ALU.mult, ALU.add, [bt, tm, self.lat_t[b]], [self.lat_t[b]])

        for t in range(NJ // 2):
            wgt, wgt_t = self.wload(wg[:, :, t * 256:(t + 1) * 256])
            wut, wut_t = self.wload(wu[:, :, t * 256:(t + 1) * 256])
            wdt, wdt_t = self.wload(wd[:, 2 * t:2 * t + 2, :])
            for b in range(nb):
                t0, n, col = self.blocks[b]
                hb, hbt = hbuf[step % 2]
                hb = hb.rearrange("p (j t) -> p j t", j=2)
                step += 1
                for jj in range(2):
                    gk, gt = self.bank("G")
                    uk, ut = self.bank("U")
                    for k in range(KD):
                        mm(kb, gk[:, 0:n], wgt[:, k, jj * 128:(jj + 1) * 128], self.u[:, k, t0:t0 + n],
                           k == 0, k == KD - 1, [wgt_t, self.u_t[b]], [gt], inc=(k == KD - 1))
                    for k in range(KD):
                        mm(kb, uk[:, 0:n], wut[:, k, jj * 128:(jj + 1) * 128], self.u[:, k, t0:t0 + n],
                           k == 0, k == KD - 1, [wut_t, self.u_t[b]], [ut], inc=(k == KD - 1))
                    sg, sgt = self.scratch(n)
                    act(kb, sg, gk[:, 0:n], AF.Silu, [gt], [sgt])
                    tt(kb, DVE, hb[:, jj, 0:n], sg, uk[:, 0:n], ALU.mult, [sgt, ut], [hbt])
                if prev is not None:
                    cstep(prev)
                prev = (wdt, wdt_t, hb, hbt, b)
        cstep(prev)
        self.t_bigB.olds += [hbuf[0][1], hbuf[1][1]]

    def scratch_fixed(self, name, nelem, dt):
        if not hasattr(self, "_fixed"):
            self._fixed = {}
        if name not in self._fixed:
            self._fixed[name] = (self.kb.sb(name, [128, nelem], dt), self.kb.tile(name))
        return self._fixed[name]

    def st_umix(self, i):
        kb = self.kb
        self.norm_mod(i, 1)
        uo = self.dout("u_out", [128, KD, self.TT], BF16)
        for b, (t0, n, col) in enumerate(self.blocks):
            kb.dma(SP, uo[:, :, t0:t0 + n], self.u[:, :, t0:t0 + n], f"uio{b}", reads=[self.u_t[b]])

    def st_latout(self):
        kb = self.kb
        lo = self.dout("lat_out", [128, KD, self.TT])
        for b, (t0, n, col) in enumerate(self.blocks):
            kb.dma(SP, lo[:, :, t0:t0 + n], self.lat[:, :, t0:t0 + n], f"latio{b}", reads=[self.lat_t[b]])

    def st_final(self):
        kb = self.kb
        fn = self.kb.sb("fnorm", [128, KD], F32)
        tfn = kb.tile("fnorm")
        kb.dma(SP, fn[:], self.din("final_norm_l", [128, KD]), "fnorm", writes=[tfn])
        out = self.dout("out_fm", [128, KD, self.TL])
        for b, (t0, n, col) in enumerate(self.blocks[:-1]):
            rs, rst = self.rstd_block(b)
            for k in range(KD):
                t1, t1t = self.scratch(n)
                tt(kb, DVE, t1, self.lat[:, k, t0:t0 + n], rs, ALU.mult, [self.lat_t[b], rst], [t1t])
                act(kb, self.lat[:, k, t0:t0 + n], t1, AF.Identity, [t1t, tfn], [self.lat_t[b]],
                    scale=fn[:, k:k + 1])
            kb.dma(SP, out[:, :, t0:t0 + n], self.lat[:, :, t0:t0 + n], f"latio{b}", reads=[self.lat_t[b]])

    def st_sconv(self, i):
        kb = self.kb
        self.norm_mod(i, 1)
        A, S, G, tm = self.mod_views(i, 1)
        win = self.din("sc_in", [1, D, 3 * D])[0].rearrange("(k p) (c m q) -> p k c m q", p=128, c=3, m=KD)
        wout = self.din("sc_out", [1, D, D])[0].rearrange("(k p) m -> p k m", p=128)
        cw = kb.sb("sc_cw", [128, 3, KD], F32)
        tcw = kb.tile("sc_cw")
        kb.dma(SP, cw[:], self.din("sc_conv_l", [128, 3, KD]), "sc_cw", writes=[tcw])
        nbk = len(self.blocks)
        for b, (t0, n, col) in enumerate(self.blocks):
            z, zt = self.bigB[:].bitcast(BF16)[:, 0:KD * n].rearrange("p (k t) -> p k t", k=KD), self.t_bigB
            R = 64 if col == 0 else n
            for m in range(KD):
                wis = [self.wload(win[:, :, c, m, :]) for c in range(3)]
                pb, pbt = self.bank("G")
                pc, pct = self.bank("U")
                ph, pht = self.bank("O")
                for c, (pk, pt) in enumerate(((pb, pbt), (pc, pct), (ph, pht))):
                    wi, wit = wis[c]
                    for k in range(KD):
                        mm(kb, pk[:, 0:n], wi[:, k, :], self.u[:, k, t0:t0 + n], k == 0, k == KD - 1,
                           [wit, self.u_t[b]], [pt], inc=(k == KD - 1))
                hv, hvt = self.scratch(n)
                tcopy(kb, ACT, hv, ph[:, 0:n], [pht], [hvt])
                v, vt = self.scratch(n)
                tt(kb, DVE, v, pc[:, 0:n], hv, ALU.mult, [pct, hvt], [vt])
                vc, vct = self.scratch(n)
                ts(kb, DVE, vc, v, cw[:, 1, m:m + 1], None, ALU.mult, ALU.bypass, [vt, tcw], [vct])
                v3 = v.rearrange("p (r w) -> p r w", w=R)
                vc3 = vc.rearrange("p (r w) -> p r w", w=R)
                stt(kb, DVE, vc3[:, :, 1:R], v3[:, :, 0:R - 1], cw[:, 0, m:m + 1], vc3[:, :, 1:R],
                    ALU.mult, ALU.add, [vt, tcw, vct], [vct])
                stt(kb, DVE, vc3[:, :, 0:R - 1], v3[:, :, 1:R], cw[:, 2, m:m + 1], vc3[:, :, 0:R - 1],
                    ALU.mult, ALU.add, [vt, tcw, vct], [vct])
                tt(kb, DVE, z[:, m, :], vc, pb[:, 0:n], ALU.mult, [vct, pbt], [zt])
            wo, wot = self.wload(wout)
            for m in range(KD):
                bk, bt = self.bank("M")
                for k in range(KD):
                    mm(kb, bk[:, 0:n], wo[:, k, m * 128:(m + 1) * 128], z[:, k, :], k == 0, k == KD - 1,
                       [wot, zt], [bt], inc=(k == KD - 1))
                stt(kb, DVE, self.lat[:, m, t0:t0 + n], bk[:, 0:n], G[:, m, col:col + 1],
                    self.lat[:, m, t0:t0 + n], ALU.mult, ALU.add, [bt, tm, self.lat_t[b]], [self.lat_t[b]])

    def st_gmlp(self, i):
        kb = self.kb
        self.norm_mod(i, 1)
        A, S, G, tm = self.mod_views(i, 1)
        gin = self.din("gm_in", [1, D, 4096])[0].rearrange("(k p) c -> p k c", p=128)
        gout = self.din("gm_out", [1, 2048, D])[0].rearrange("(c p) m -> p c m", p=128)
        wsT = kb.sb("gm_wsT", [128, 8, 128], BF16)
        twsT = kb.tile("gm_wsT")
        kb.dma(POOL, wsT[:], self.din("gm_wsT_l", [128, 8, 128]), "gm_wsT", writes=[twsT])
        bsr = kb.sb("gm_bs", [1, 8, 128], BF16)
        tbsr = kb.tile("gm_bs")
        kb.dma(POOL, bsr[:], self.din("gm_bs_l", [1, 8, 128]), "gm_bs", writes=[tbsr])
        vn = kb.sb("gm_vn", [128, 2048], BF16)
        tvn = kb.tile("gm_vn")
        kb.dma(POOL, vn[:], self.din("gm_vn_l", [128, 2048]), "gm_vn", writes=[tvn])
        for b, (t0, n, col) in enumerate(self.blocks):
            nch = n // 128
            zv, zvt = self.bigA[:].bitcast(BF16)[:, 0:nch * 2048].rearrange("p (c f) -> p c f", c=nch), self.t_bigA
            ssum = kb.sb(f"gm_ssum{b}", [128, 4, 4], F32)
            ssq = kb.sb(f"gm_ssq{b}", [128, 4, 4], F32)
            tst = kb.tile(f"gm_stat{b}")
            memset(kb, POOL, ssum[:], 0.0, [tst])
            memset(kb, POOL, ssq[:], 0.0, [tst])
            for nb4 in range(4):
                wv, wvt = self.wload(gin[:, :, 2048 + nb4 * 512:2048 + (nb4 + 1) * 512])
                for c in range(nch):
                    bk, bt = self.bank("G")
                    for k in range(KD):
                        mm(kb, bk[:, :], self.u[:, k, t0 + c * 128:t0 + (c + 1) * 128], wv[:, k, :],
                           k == 0, k == KD - 1, [wvt, self.u_t[b]], [bt], inc=(k == KD - 1))
                    act(kb, zv[:, c, nb4 * 512:(nb4 + 1) * 512], bk[:, :], AF.Gelu_apprx_tanh, [bt], [zvt, tst],
                        accum_out=ssum[:, c, nb4:nb4 + 1])
                    junk, jt = self.scratch(256)
                    act(kb, junk.bitcast(BF16), zv[:, c, nb4 * 512:(nb4 + 1) * 512], AF.Square, [zvt], [jt, tst],
                        accum_out=ssq[:, c, nb4:nb4 + 1])
            st = kb.sb(f"gm_st{b}", [128, 4, 4], F32)
            kb.op(DVE, lambda e, ssum=ssum, st=st: e.reduce_sum(out=st[:, :, 0], in_=ssum[:, :, :],
                                                               axis=mybir.AxisListType.X), [tst], [tst])
            kb.op(DVE, lambda e, ssq=ssq, st=st: e.reduce_sum(out=st[:, :, 1], in_=ssq[:, :, :],
                                                             axis=mybir.AxisListType.X), [tst], [tst])
            ts(kb, DVE, st[:, :, 0], st[:, :, 0], 1.0 / 2048, None, ALU.mult, ALU.bypass, [tst], [tst])
            tt(kb, DVE, st[:, :, 2], st[:, :, 0], st[:, :, 0], ALU.mult, [tst], [tst])
            stt(kb, DVE, st[:, :, 1], st[:, :, 1], 1.0 / 2048, st[:, :, 2], ALU.mult, ALU.subtract, [tst], [tst])
            act(kb, st[:, :, 1], st[:, :, 1], AF.Sqrt, [tst], [tst], bias=EPS)
            kb.op(DVE, lambda e, st=st: e.reciprocal(out=st[:, :, 1], in_=st[:, :, 1]), [tst], [tst])
            stt(kb, DVE, st[:, :, 3], st[:, :, 0], -1.0, st[:, :, 1], ALU.mult, ALU.mult, [tst], [tst])
            for c in range(nch):
                act(kb, zv[:, c, :], zv[:, c, :], AF.Identity, [zvt, tst], [zvt],
                    bias=st[:, c, 3:4], scale=st[:, c, 1:2])
                tt(kb, POOL, zv[:, c, :], zv[:, c, :], vn[:], ALU.mult, [zvt, tvn], [zvt])
            prod, prt = self.bigB[:].bitcast(BF16)[:, 0:16 * n].rearrange("p (c t) -> p c t", c=16), self.t_bigB
            for cc in range(16):
                g = cc // 2
                sk, skt = self.bank("U")
                for c in range(nch):
                    mm(kb, sk[:, c * 128:(c + 1) * 128], zv[:, c, cc * 128:(cc + 1) * 128], wsT[:, g, :],
                       True, False, [zvt, twsT], [skt], inc=False)
                    mm(kb, sk[:, c * 128:(c + 1) * 128], self.ones_bf[0:1, :], bsr[0:1, g, :],
                       False, True, [self.t_ones, tbsr], [skt], inc=(c == nch - 1))
                wuc, wuct = self.wload(gin[:, :, cc * 128:(cc + 1) * 128])
                zk, zkt = self.bank("O")
                for k in range(KD):
                    mm(kb, zk[:, 0:n], wuc[:, k, :], self.u[:, k, t0:t0 + n], k == 0, k == KD - 1,
                       [wuct, self.u_t[b]], [zkt], inc=(k == KD - 1))
                zu, zut = self.scratch(n)
                act(kb, zu, zk[:, 0:n], AF.Gelu_apprx_tanh, [zkt], [zut])
                tt(kb, DVE, prod[:, cc, :], zu, sk[:, 0:n], ALU.mult, [zut, skt], [prt])
            for m in range(KD):
                wo, wot = self.wload(gout[:, :, m * 128:(m + 1) * 128])
                bk, bt = self.bank("M")
                for cc in range(16):
                    mm(kb, bk[:, 0:n], wo[:, cc, :], prod[:, cc, :], cc == 0, cc == 15, [wot, prt], [bt],
                       inc=(cc == 15))
                stt(kb, DVE, self.lat[:, m, t0:t0 + n], bk[:, 0:n], G[:, m, col:col + 1],
                    self.lat[:, m, t0:t0 + n], ALU.mult, ALU.add, [bt, tm, self.lat_t[b]], [self.lat_t[b]])

    def st_ssdout(self, i, j, do_ctx=True):
        kb = self.kb
        A, S, G, tm = self.mod_views(i, 1)
        yn = self.din("yn_in", [128, 16, self.TT], BF16)
        wout = self.din(f"ssd_out{j}", [2048, D]).rearrange("(c p) m -> p c m", p=128)
        nb = len(self.blocks) if do_ctx else len(self.blocks) - 1
        for b, (t0, n, col) in enumerate(self.blocks[:nb]):
            bg, yt = (self.bigA, self.t_bigA) if b % 2 == 0 else (self.bigB, self.t_bigB)
            y = bg[:].bitcast(BF16)[:, 0:16 * n].rearrange("p (c t) -> p c t", c=16)
            kb.dma(SP, y, yn[:, :, t0:t0 + n], f"ynio{b % 2}", writes=[yt])
            for m in range(KD):
                wo, wot = self.wload(wout[:, :, m * 128:(m + 1) * 128])
                bk, bt = self.bank("M")
                for cc in range(16):
                    mm(kb, bk[:, 0:n], wo[:, cc, :], y[:, cc, :], cc == 0, cc == 15, [wot, yt], [bt],
                       inc=(cc == 15))
                stt(kb, DVE, self.lat[:, m, t0:t0 + n], bk[:, 0:n], G[:, m, col:col + 1],
                    self.lat[:, m, t0:t0 + n], ALU.mult, ALU.add, [bt, tm, self.lat_t[b]], [self.lat_t[b]])


def fm(a):
    T = a.shape[0]
    return np.ascontiguousarray(a.T.reshape(KD, 128, T).transpose(1, 0, 2))


def unfm(a):
    T = a.shape[2]
    return np.ascontiguousarray(a.transpose(1, 0, 2).reshape(D, T).T)


def prep_common(inp):
    f = np.float32
    c = {}
    c["ada_w"] = np.asarray(inp["ada_w"], f)
    c["ada_b_l"] = np.ascontiguousarray(np.asarray(inp["ada_b"], f).reshape(4, 72, 128).transpose(2, 0, 1))
    c["norm_g_l"] = np.ascontiguousarray(np.asarray(inp["norm_g"], f).reshape(4, 3, KD, 128).transpose(3, 0, 1, 2))
    c["ffn_wg"] = np.asarray(inp["ffn_wg"], f)
    c["ffn_wu"] = np.asarray(inp["ffn_wu"], f)
    c["ffn_wd"] = np.asarray(inp["ffn_wd"], f)
    c["final_norm_l"] = np.ascontiguousarray(np.asarray(inp["final_norm"], f).reshape(KD, 128).T)
    c["sc_in"] = np.asarray(inp["sc_in"], f)
    c["sc_out"] = np.asarray(inp["sc_out"], f)
    c["sc_conv_l"] = np.ascontiguousarray(np.asarray(inp["sc_conv"], f)[0].reshape(3, KD, 128).transpose(2, 0, 1))
    c["gm_in"] = np.asarray(inp["gm_in"], f)
    c["gm_out"] = np.asarray(inp["gm_out"], f)
    c["gm_wsT_l"] = np.ascontiguousarray(np.asarray(inp["gm_ws"], f)[0].transpose(2, 0, 1))
    c["gm_bs_l"] = np.ascontiguousarray(np.asarray(inp["gm_bs"], f)[0][None])
    c["gm_vn_l"] = np.ascontiguousarray(np.broadcast_to(np.asarray(inp["gm_vnorm"], f)[0][None, :], (128, 2048)))
    c["ssd_out"] = np.asarray(inp["ssd_out"], f)
    return c


def cvec_for(inp, b):
    cv = np.empty((128, KD, 2), np.float32)
    cv[:, :, 0] = np.asarray(inp["c"], np.float32)[b].reshape(KD, 128).T
    cv[:, :, 1] = np.asarray(inp["c_ctx"], np.float32).reshape(KD, 128).T
    return cv


def run_prog(prog, in_maps):
    names = set(prog.dram.keys())
    maps = [{k: v for k, v in m.items() if k in names} for m in in_maps]
    for m in maps:
        missing = [k for k in names if k not in m and k not in ("u_out", "lat_out", "out_fm", "yn", "mod_out", "sprev_out")]
        assert not missing, missing
    res = run_bass_kernel_spmd(prog.nc, maps, core_ids=list(range(NCORES)))
    return res.results


NEG = -1.0e9


class SsdProg:
    def __init__(self, T, name="ssd", dbg=99, phase="both"):
        self.T = T
        self.dbg = dbg
        self.phase = phase
        self.NT = CT + T
        self.nc = bass.Bass("TRN2", target_bir_lowering=False)
        self.dram = {}
        sbs = [(0, CT, 0, CT)]
        t0 = CT
        while t0 < self.NT:
            n = min(512, self.NT - t0)
            sbs.append((t0, n, CT, self.NT))
            t0 += n
        self.sbs = sbs
        self.build()

    din = TokProg.din
    dout = TokProg.dout

    def build(self):
        nc = self.nc
        NT = self.NT
        NCH = NT // 128
        with ExitStack() as es:
            kb = KB(nc, es)
            self.kb = kb
            self.banks = [kb.ps(f"bank{i}", [128, 512], F32) for i in range(8)]
            self.bt = {}
            ident = kb.sb("ident", [128, 128], BF16); t_c = kb.tile("consts")
            Um = kb.sb("Um", [128, 128], F32); Lm = kb.sb("Lm", [128, 128], F32)
            nmf = kb.sb("nmf", [128, 128], BF16); nmb = kb.sb("nmb", [128, 128], BF16)
            ones32 = kb.sb("ones32", [128, 128], F32)
            onesr = kb.sb("onesr", [1, 128], BF16)
            for ap, op, cm, step, base, fill in ((ident, ALU.not_equal, 1, -1, 0, 1.0), (Um, ALU.is_gt, 1, -1, 0, 1.0),
                                                 (Lm, ALU.is_gt, -1, 1, 0, 1.0), (nmf, ALU.is_gt, -1, 1, 1, NEG),
                                                 (nmb, ALU.is_gt, 1, -1, 1, NEG)):
                memset(kb, POOL, ap[:], 0.0, [t_c])
                kb.op(POOL, lambda e, ap=ap, op=op, cm=cm, step=step, base=base, fill=fill: e.affine_select(
                    out=ap[:], in_=ap[:], compare_op=op, fill=fill, base=base, pattern=[[step, 128]],
                    channel_multiplier=cm), [t_c], [t_c])
            memset(kb, POOL, ones32[:], 1.0, [t_c])
            memset(kb, POOL, onesr[:], 1.0, [t_c])
            self.onesr5 = kb.sb("onesr5", [1, 512], BF16)
            memset(kb, POOL, self.onesr5[:], 1.0, [t_c])
            self.ident, self.Um, self.Lm, self.nmf, self.nmb, self.ones32, self.onesr = ident, Um, Lm, nmf, nmb, ones32, onesr
            self.t_c = t_c
            t_p = kb.tile("params")
            self.t_p = t_p
            wx = kb.sb("wx", [128, KD, 768], BF16)
            wz = kb.sb("wz", [128, KD, 512], BF16)
            wdt = kb.sb("wdt", [128, KD, 16], BF16)
            kb.dma(POOL, wx[:], self.din("w_x", [D, 768]).rearrange("(k p) c -> p k c", p=128), "p_wx", writes=[t_p])
            kb.dma(POOL, wz[:], self.din("w_z", [D, 512]).rearrange("(k p) c -> p k c", p=128), "p_wz", writes=[t_p])
            kb.dma(POOL, wdt[:], self.din("w_dt", [D, 16]).rearrange("(k p) c -> p k c", p=128), "p_wdt", writes=[t_p])
            cwp = kb.sb("cwp", [128, 6, 5], F32)
            cbp = kb.sb("cbp", [128, 6], F32)
            cbr = kb.sb("cbr", [1, 768], BF16)
            dtb = kb.sb("dtb", [128, 16], F32)
            abc = kb.sb("abc", [128, 16], F32)
            dsk = kb.sb("dsk", [128, 8], F32)
            nwb = kb.sb("nwb", [128, 512], F32)
            kb.dma(SP, cwp[:], self.din("cw_p", [128, 6, 5]), "p_cwp", writes=[t_p])
            kb.dma(SP, cbp[:], self.din("cb_p", [128, 6]), "p_cbp", writes=[t_p])
            kb.dma(POOL, cbr[:], self.din("cb_r", [1, 768]), "p_cbr", writes=[t_p])
            kb.dma(SP, dtb[:], self.din("dtb_bc", [128, 16]), "p_dtb", writes=[t_p])
            kb.dma(SP, abc[:], self.din("alog_bc", [128, 16]), "p_abc", writes=[t_p])
            kb.dma(SP, dsk[:], self.din("dsk_bc", [128, 8]), "p_dsk", writes=[t_p])
            kb.dma(SP, nwb[:], self.din("nw_bc", [128, 512]), "p_nwb", writes=[t_p])
            act(kb, abc[:], abc[:], AF.Exp, [t_p], [t_p])
            ts(kb, DVE, abc[:], abc[:], -1.0, None, ALU.mult, ALU.bypass, [t_p], [t_p])
            diag = kb.sb("diag", [128, 6, 5, 128], BF16)
            for fc in range(6):
                for tap in range(5):
                    ts(kb, DVE, diag[:, fc, tap, :], ident[:], cwp[:, fc, tap:tap + 1], None, ALU.mult, ALU.bypass,
                       [t_c, t_p], [t_p])
            self.wx, self.wz, self.wdt, self.cbp, self.cbr, self.dtb, self.abc, self.dsk, self.nwb, self.diag = \
                wx, wz, wdt, cbp, cbr, dtb, abc, dsk, nwb, diag
            self.Sprev_b = kb.sb("Sprev_b", [128, NCH, 512], BF16)
            self.t_Sprev = [kb.tile(f"Sprev{c}") for c in range(NCH)]
            self.S = [kb.sb(f"S{d}", [128, 512], F32) for d in range(2)]
            self.Sbf = kb.sb("Sbf", [128, 512], BF16)
            self.t_S = [kb.tile(f"S{d}") for d in range(2)]
            self.t_Sbf = kb.tile("Sbf")
            memset(kb, DVE, self.S[0][:], 0.0, [self.t_S[0]])
            memset(kb, DVE, self.S[1][:], 0.0, [self.t_S[1]])
            memset(kb, DVE, self.Sbf[:], 0.0, [self.t_Sbf])
            self.ublk = [kb.sb(f"ublk{x}", [128, KD, 516], BF16) for x in range(2)]
            self.t_ublk = [kb.tile(f"ublk{x}") for x in range(2)]
            self.pre = kb.sb("pre", [128, 6, 516], BF16); self.t_pre = kb.tile("pre")
            self.Bfm = kb.sb("Bfm", [128, 512], BF16); self.Cfm = kb.sb("Cfm", [128, 512], BF16)
            self.t_BC = kb.tile("BCfm")
            self.Braw = kb.sb("Braw", [128, 512], F32); self.Craw = kb.sb("Craw", [128, 512], F32)
            self.t_BCraw = kb.tile("BCraw")
            self.ynT = kb.sb("ynT", [128, 4, 512], BF16); self.t_ynT = kb.tile("ynT")
            self.ui = 0
            u_all = self.din("u_all", [128, KD, NT], BF16)
            yn = self.dout("yn", [128, 4, NT], BF16)
            order = [self.sbs[0]] + list(reversed(self.sbs[1:]))
            if self.phase == "main":
                order = []
                kb.dma(SP, self.Sprev_b[:], self.din("sprev_in", [128, NCH, 512], BF16), "sprev", writes=self.t_Sprev)
            nxt = self.load_u(u_all, *order[0]) if order else None
            for oi, (s0, ns, lo, hi) in enumerate(order):
                ub, ubt = nxt
                if oi + 1 < len(order):
                    nxt = self.load_u(u_all, *order[oi + 1])
                self.preconv(ub, ubt, ns, fcs=(0, 1, 2, 3, 4))
                for c in reversed(range(ns // 128)):
                    if self.dbg >= 2:
                        self.chunk(ub, ubt, s0, c, main=False)
                if s0 == 0:
                    pass
            if self.phase == "pre":
                kb.dma(SP, self.dout("sprev_out", [128, NCH, 512], BF16), self.Sprev_b[:], "sprev", reads=self.t_Sprev)
            msbs = (self.sbs if self.phase != "pre" else [])
            nxt = self.load_u(u_all, *msbs[0]) if msbs else None
            for oi, (s0, ns, lo, hi) in enumerate(msbs):
                ub, ubt = nxt
                if oi + 1 < len(msbs):
                    nxt = self.load_u(u_all, *msbs[oi + 1])
                self.preconv(ub, ubt, ns, fcs=(0, 1, 2, 3, 4, 5) if not os.environ.get("V2") else (0, 1, 2, 3, 4))
                if not os.environ.get("V1"):
                    self.conv_fm(ns)
                if os.environ.get("V3"):
                    kb.barrier()
                for c in range(ns // 128):
                    if self.dbg == 29:
                        self.chunk(ub, ubt, s0, c, main=False)
                    elif self.dbg >= 3:
                        self.chunk(ub, ubt, s0, c, main=True)
                if self.dbg < 3 or self.dbg == 29:
                    memset(kb, DVE, self.ynT[:], 0.0, [self.t_ynT])
                kb.dma(SP, yn[:, :, s0:s0 + ns], self.ynT[:, :, 0:ns], "ynout", reads=[self.t_ynT])
            kb.barrier()
            self.stat = kb.stats()
            kb.emit()

    def B(self, i, name):
        return self.banks[i], self.kb.tile(f"bank{i}")

    def load_u(self, u_all, s0, ns, lo, hi):
        kb = self.kb
        x = self.ui
        self.ui ^= 1
        ub, ubt = self.ublk[x], self.t_ublk[x]
        a = max(s0 - 2, lo)
        b = min(s0 + ns + 2, hi)
        if a > s0 - 2:
            memset(kb, DVE, ub[:, :, 0:2], 0.0, [ubt])
        if b < s0 + ns + 2:
            memset(kb, DVE, ub[:, :, ns + 2:ns + 4], 0.0, [ubt])
        kb.dma(SP, ub[:, :, a - (s0 - 2):b - (s0 - 2)], u_all[:, :, a:b], f"uload{x}", writes=[ubt])
        return ub, ubt

    def preconv(self, ub, ubt, ns, fcs):
        kb = self.kb
        for n_i, fc in enumerate(fcs):
            bk, bt = self.B(0 if n_i % 2 == 0 else 7, "pre")
            bh, bht = self.B(1, "small")
            for k in range(KD):
                mm(kb, bk[:, 0:ns], self.wx[:, k, fc * 128:(fc + 1) * 128], ub[:, k, 0:ns], k == 0, k == KD - 1,
                   [self.t_p, ubt], [bt], inc=(k == KD - 1))
            for k in range(KD):
                mm(kb, bh[:, 0:4], self.wx[:, k, fc * 128:(fc + 1) * 128], ub[:, k, ns:ns + 4], k == 0, k == KD - 1,
                   [self.t_p, ubt], [bht], inc=(k == KD - 1))
            tcopy(kb, ACT, self.pre[:, fc, 0:ns], bk[:, 0:ns], [bt], [self.t_pre])
            tcopy(kb, DVE, self.pre[:, fc, ns:ns + 4], bh[:, 0:4], [bht], [self.t_pre])

    def conv_fm(self, ns):
        kb = self.kb
        for fc, dst in ((4, self.Bfm), (5, self.Cfm)):
            bk, bt = self.B(0 if fc == 4 else 7, "pre")
            for tap in range(5):
                mm(kb, bk[:, 0:ns], self.diag[:, fc, tap, :], self.pre[:, fc, tap:tap + ns], tap == 0, False,
                   [self.t_p, self.t_pre], [bt], inc=False)
            mm(kb, bk[:, 0:ns], self.cbr[0:1, fc * 128:(fc + 1) * 128], self.onesr5[0:1, 0:ns], False, True,
               [self.t_p, self.t_c], [bt], inc=True)
            raw = self.Braw if fc == 4 else self.Craw
            act(kb, raw[:, 0:ns], bk[:, 0:ns], AF.Identity, [bt], [self.t_BCraw])
            if os.environ.get("V7"):
                dm, dmt = self.tmp("dummy", [128, 8], F32)
                act(kb, dm[:], self.cbp[:, 0:6].rearrange("p a -> p a")[:, 0:6] if False else self.dtb[:, 0:8], AF.Identity, [self.t_p], [dmt])

    def tmp(self, name, shape, dt):
        name = f"{name}_{getattr(self, 'par', 0)}"
        if not hasattr(self, "_tmp"):
            self._tmp = {}
        if name not in self._tmp:
            self._tmp[name] = (self.kb.sb(name, shape, dt), self.kb.tile(name))
        return self._tmp[name]

    def chunk(self, ub, ubt, s0, c, main):
        kb = self.kb
        co = c * 128
        gch = (s0 + co) // 128
        self.par = gch % 2
        ident, onesr = self.ident, self.onesr
        tc_, tp = self.t_c, self.t_p
        bx, bxt = self.B(2, "xs")
        for fc in range(4):
            for tap in range(5):
                mm(kb, bx[:, fc * 128:(fc + 1) * 128], self.pre[:, fc, co + tap:co + tap + 128],
                   self.diag[:, fc, tap, :], tap == 0, False, [self.t_pre, tp], [bxt], inc=False)
            mm(kb, bx[:, fc * 128:(fc + 1) * 128], onesr[0:1, :], self.cbr[0:1, fc * 128:(fc + 1) * 128],
               False, True, [tc_, tp], [bxt], inc=(fc == 3))
        xs, xst = self.tmp("xs", [128, 512], F32)
        act(kb, xs[:], bx[:, :], AF.Silu, [bxt], [xst])
        bs, bst = self.B(1, "small")
        for tap in range(5):
            mm(kb, bs[:, 128:256], self.pre[:, 4, co + tap:co + tap + 128], self.diag[:, 4, tap, :],
               tap == 0, False, [self.t_pre, tp], [bst], inc=False)
        mm(kb, bs[:, 128:256], onesr[0:1, :], self.cbr[0:1, 512:640], False, True, [tc_, tp], [bst], inc=False)
        for k in range(KD):
            mm(kb, bs[:, 16:32], ub[:, k, 2 + co:2 + co + 128], self.wdt[:, k, :], k == 0, k == KD - 1,
               [ubt, tp], [bst], inc=(k == KD - 1))
        Btm, Btmt = self.tmp("Btm", [128, 128], BF16)
        act(kb, Btm[:], bs[:, 128:256], AF.Silu, [bst], [Btmt])
        if main:
            act(kb, self.Bfm[:, co:co + 128], self.Braw[:, co:co + 128], AF.Silu, [self.t_BCraw], [self.t_BC])
            act(kb, self.Cfm[:, co:co + 128], self.Craw[:, co:co + 128], AF.Silu, [self.t_BCraw], [self.t_BC])
        sm, smt = self.tmp("sm", [128, 8, 16], F32)
        tt(kb, DVE, sm[:, 0, :], bs[:, 16:32], self.dtb[:], ALU.add, [bst, tp], [smt])
        ts(kb, DVE, sm[:, 1, :], sm[:, 0, :], -1.0, None, ALU.mult, ALU.bypass, [smt], [smt])
        tt(kb, DVE, sm[:, 1, :], sm[:, 1, :], sm[:, 0, :], ALU.max, [smt], [smt])
        act(kb, sm[:, 1, :], sm[:, 1, :], AF.Exp, [smt], [smt], scale=-1.0)
        act(kb, sm[:, 1, :], sm[:, 1, :], AF.Ln, [smt], [smt], bias=1.0)
        stt(kb, DVE, sm[:, 2, :], sm[:, 0, :], 0.0, sm[:, 1, :], ALU.max, ALU.add, [smt], [smt])
        tt(kb, DVE, sm[:, 3, :], sm[:, 2, :], self.abc[:], ALU.mult, [smt, tp], [smt])
        mm(kb, bs[:, 32:40], self.Um[:], sm[:, 3, 0:8], True, True, [tc_, smt], [bst], inc=False)
        mm(kb, bs[:, 40:48], self.Lm[:], sm[:, 3, 8:16], True, True, [tc_, smt], [bst], inc=False)
        mm(kb, bs[:, 48:64], self.ones32[:], sm[:, 3, :], True, True, [tc_, smt], [bst], inc=True)
        sm2, sm2t = self.tmp("sm2", [128, 6, 16], F32)
        tcopy(kb, DVE, sm2[:, 0, :], bs[:, 32:48], [bst], [sm2t])
        ts(kb, DVE, sm2[:, 1, :], sm2[:, 0, :], -1.0, None, ALU.mult, ALU.bypass, [sm2t], [sm2t])
        act(kb, sm2[:, 2, :], sm2[:, 0, :], AF.Exp, [sm2t], [sm2t])
        tt(kb, DVE, sm2[:, 3, :], bs[:, 48:64], sm2[:, 0, :], ALU.subtract, [bst, sm2t], [sm2t])
        act(kb, sm2[:, 3, :], sm2[:, 3, :], AF.Exp, [sm2t], [sm2t])
        act(kb, sm2[:, 4, :], bs[:, 48:64], AF.Exp, [bst], [sm2t])
        tt(kb, DVE, sm2[:, 5, :], sm2[:, 3, :], sm[:, 2, :], ALU.mult, [sm2t, smt], [sm2t])
        xs3 = xs[:].rearrange("p (h q) -> p h q", h=8)

        def bc(ap):
            return ap.unsqueeze(2).to_broadcast([128, 8, 64])

        dirs = (0,) if main else (1,)
        xte = {}
        for d in dirs:
            xt_, xtt = self.tmp(f"xte{d}", [128, 8, 64], BF16)
            tt(kb, POOL, xt_[:], xs3, bc(sm2[:, 5, d * 8:(d + 1) * 8]), ALU.mult,
               [xst, sm2t], [xtt])
            xte[d] = (xt_, xtt)
        if main:
            if self.dbg == 30:
                memset(kb, DVE, self.ynT[:, :, co:co + 128], 0.0, [self.t_ynT])
                return
            bz, bzt = self.B(3, "z")
            for k in range(KD):
                mm(kb, bz[:, :], ub[:, k, 2 + co:2 + co + 128], self.wz[:, k, :], k == 0, k == KD - 1,
                   [ubt, tp], [bzt], inc=(k == KD - 1))
            sz, szt = self.tmp("sz", [128, 512], F32)
            act(kb, sz[:], bz[:, :], AF.Silu, [bzt], [szt])
            if self.dbg == 31:
                memset(kb, DVE, self.ynT[:, :, co:co + 128], 0.0, [self.t_ynT])
                return
            xdt = {}
            for d in (0, 1):
                x_, x_t = self.tmp(f"xdt{d}", [128, 8, 64], BF16)
                tt(kb, DVE if d == 0 else POOL, x_[:], xs3, bc(sm[:, 2, d * 8:(d + 1) * 8]), ALU.mult,
                   [xst, smt], [x_t])
                xdt[d] = (x_, x_t)
            xsd, xsdt = self.tmp("xsd", [128, 8, 64], BF16)
            tt(kb, POOL, xsd[:], xs3, bc(self.dsk[:]), ALU.mult, [xst, tp], [xsdt])
            if self.dbg == 32:
                memset(kb, DVE, self.ynT[:, :, co:co + 128], 0.0, [self.t_ynT])
                return
            mm(kb, bs[:, 256:384], self.Bfm[:, co:co + 128], self.Cfm[:, co:co + 128], True, True,
               [self.t_BC], [bst], inc=True)
            cbT, cbTt = self.tmp("cbT", [128, 128], F32)
            tcopy(kb, ACT, cbT[:], bs[:, 256:384], [bst], [cbTt])
            if self.dbg == 33:
                memset(kb, DVE, self.ynT[:, :, co:co + 128], 0.0, [self.t_ynT])
                return
            W = {}
            for d in (0, 1):
                for hq in range(2):
                    br, brt = self.B(4 + (d * 2 + hq) % 2, "R")
                    for hh in range(4):
                        h = hq * 4 + hh
                        col = d * 8 + h
                        mm(kb, br[:, hh * 128:(hh + 1) * 128], sm[:, 3, col:col + 1].to_broadcast([128, 128]),
                           self.Um[:] if d == 0 else self.Lm[:], True, False, [smt, tc_], [brt], inc=False)
                        mm(kb, br[:, hh * 128:(hh + 1) * 128], ident[:], (self.nmf if d == 0 else self.nmb)[:],
                           False, True, [tc_], [brt], inc=(hh == 3))
                    Dm, Dmt = self.tmp(f"Dm{d}{hq}", [128, 4, 128], F32)
                    for hh in range(4):
                        col = d * 8 + hq * 4 + hh
                        act(kb, Dm[:, hh, :], br[:, hh * 128:(hh + 1) * 128], AF.Exp, [brt, sm2t], [Dmt],
                            bias=sm2[:, 1, col:col + 1])
                    Wt, Wtt = self.tmp(f"W{d}{hq}", [128, 4, 128], BF16)
                    tt(kb, DVE, Wt[:], Dm[:], cbT[:].unsqueeze(1).to_broadcast([128, 4, 128]), ALU.mult,
                       [Dmt, cbTt], [Wtt])
                    W[(d, hq)] = (Wt, Wtt)
            if self.dbg == 3:
                memset(kb, DVE, self.ynT[:, :, co:co + 128], 0.0, [self.t_ynT])
                return
            by, byt = self.B(6, "Y")
            for h in range(8):
                hq, hh = h // 4, h % 4
                mm(kb, by[:, h * 64:(h + 1) * 64], W[(0, hq)][0][:, hh, :], xdt[0][0][:, h, :], True, False,
                   [W[(0, hq)][1], xdt[0][1]], [byt], inc=False)
                mm(kb, by[:, h * 64:(h + 1) * 64], W[(1, hq)][0][:, hh, :], xdt[1][0][:, h, :], False, False,
                   [W[(1, hq)][1], xdt[1][1]], [byt], inc=False)
                mm(kb, by[:, h * 64:(h + 1) * 64], ident[:], xsd[:, h, :], False, True, [tc_, xsdt], [byt],
                   inc=(h == 7))
            if self.dbg == 4:
                memset(kb, DVE, self.ynT[:, :, co:co + 128], 0.0, [self.t_ynT])
                return
            bo0, bo0t = self.B(2, "xs")
            mm(kb, bo0[:, :], self.Cfm[:, co:co + 128], self.Sbf[:], True, True, [self.t_BC, self.t_Sbf], [bo0t],
               inc=True)
            bo1, bo1t = self.B(3, "z")
            mm(kb, bo1[:, :], self.Cfm[:, co:co + 128], self.Sprev_b[:, gch, :], True, True,
               [self.t_BC, self.t_Sprev[gch]], [bo1t], inc=True)
            y, yt_ = self.tmp("y", [128, 512], F32)
            t1, t1t = self.tmp("t1", [128, 512], F32)
            y3 = y[:].rearrange("p (h q) -> p h q", h=8)
            t13 = t1[:].rearrange("p (h q) -> p h q", h=8)
            tt(kb, DVE, y3, bo0[:, :].rearrange("p (h q) -> p h q", h=8), bc(sm2[:, 2, 0:8]), ALU.mult,
               [bo0t, sm2t], [yt_])
            tt(kb, DVE, t13, bo1[:, :].rearrange("p (h q) -> p h q", h=8), bc(sm2[:, 2, 8:16]), ALU.mult,
               [bo1t, sm2t], [t1t])
            tt(kb, DVE, y[:], y[:], by[:, :], ALU.add, [yt_, byt], [yt_])
            tt(kb, POOL, y[:], y[:], t1[:], ALU.add, [yt_, t1t], [yt_])
            if self.dbg == 5:
                memset(kb, DVE, self.ynT[:, :, co:co + 128], 0.0, [self.t_ynT])
                return
            tt(kb, DVE, y[:], y[:], sz[:], ALU.mult, [yt_, szt], [yt_])
            ss, sst = self.tmp("ss", [128, 2], F32)
            memset(kb, POOL, ss[:], 0.0, [sst])
            act(kb, t1[:], y[:], AF.Square, [yt_, t1t], [t1t, sst], accum_out=ss[:, 0:1])
            act(kb, ss[:, 1:2], ss[:, 0:1], AF.Sqrt, [sst], [sst], bias=EPS, scale=1.0 / 512)
            kb.op(DVE, lambda e, ss=ss: e.reciprocal(out=ss[:, 1:2], in_=ss[:, 1:2]), [sst], [sst])
            ynb, ynbt = self.tmp("ynb", [128, 512], BF16)
            stt(kb, DVE, ynb[:], y[:], ss[:, 1:2], self.nwb[:], ALU.mult, ALU.mult, [yt_, sst, tp], [ynbt])
            if self.dbg == 6:
                memset(kb, DVE, self.ynT[:, :, co:co + 128], 0.0, [self.t_ynT])
                return
            btr, btrt = self.B(0, "pre")
            trv = btr[:, 0:256].bitcast(BF16)
            for fc in range(4):
                kb.op(PE, lambda e, fc=fc, trv=trv, ynb=ynb: e.transpose(trv[:, fc * 128:(fc + 1) * 128],
                                                                       ynb[:, fc * 128:(fc + 1) * 128], ident[:]),
                      [ynbt, tc_], [btrt], inc=(fc == 3))
            tcopy(kb, ACT, self.ynT[:, :, co:co + 128], trv.rearrange("p (f t) -> p f t", f=4), [btrt], [self.t_ynT])
        for d in dirs:
            if d == 1:
                tcopy(kb, ACT, self.Sprev_b[:, gch, :], self.S[1][:], [self.t_S[1]], [self.t_Sprev[gch]])
            bst_, bstt = self.B(7 if d == 0 else 6, "St")
            if not main:
                bst_, bstt = self.B(6, "St")
            mm(kb, bst_[:, :], Btm[:], xte[d][0][:].rearrange("p h q -> p (h q)"), True, True,
               [Btmt, xte[d][1]], [bstt], inc=True)
            S3 = self.S[d][:].rearrange("p (h q) -> p h q", h=8)
            tt(kb, DVE, S3, S3, bc(sm2[:, 4, d * 8:(d + 1) * 8]), ALU.mult, [self.t_S[d], sm2t], [self.t_S[d]])
            tt(kb, DVE, self.S[d][:], self.S[d][:], bst_[:, :], ALU.add, [self.t_S[d], bstt], [self.t_S[d]])
            if d == 0:
                tcopy(kb, ACT, self.Sbf[:], self.S[0][:], [self.t_S[0]], [self.t_Sbf])


def prep_ssd(inp, j, g):
    f = np.float32
    w_in = np.asarray(inp["ssd_in"], f)[j]
    m = {}
    m["w_z"] = np.ascontiguousarray(w_in[:, g * 512:(g + 1) * 512])
    m["w_x"] = np.ascontiguousarray(np.concatenate([
        w_in[:, 2048 + g * 512:2048 + (g + 1) * 512],
        w_in[:, 4096 + g * 128:4096 + (g + 1) * 128],
        w_in[:, 4608 + g * 128:4608 + (g + 1) * 128]], axis=1))
    m["w_dt"] = np.ascontiguousarray(np.concatenate([
        w_in[:, 5120 + g * 8:5120 + (g + 1) * 8],
        w_in[:, 5152 + g * 8:5152 + (g + 1) * 8]], axis=1))
    ch = np.concatenate([np.arange(g * 512, (g + 1) * 512), 2048 + np.arange(g * 128, (g + 1) * 128),
                         2560 + np.arange(g * 128, (g + 1) * 128)])
    cw = np.asarray(inp["ssd_conv_w"], f)[j][:, ch]
    cb = np.asarray(inp["ssd_conv_b"], f)[j][ch]
    m["cw_p"] = np.ascontiguousarray(cw.reshape(5, 6, 128).transpose(2, 1, 0))
    m["cb_p"] = np.ascontiguousarray(cb.reshape(6, 128).T)
    m["cb_r"] = np.ascontiguousarray(cb[None, :])
    hs = slice(g * 8, (g + 1) * 8)
    dtb = np.asarray(inp["ssd_dt_bias"], f)[j][:, hs].reshape(16)
    alog = np.asarray(inp["ssd_a_log"], f)[j][:, hs].reshape(16)
    m["dtb_bc"] = np.ascontiguousarray(np.broadcast_to(dtb[None], (128, 16)))
    m["alog_bc"] = np.ascontiguousarray(np.broadcast_to(alog[None], (128, 16)))
    m["dsk_bc"] = np.ascontiguousarray(np.broadcast_to(np.asarray(inp["ssd_d"], f)[j][hs][None], (128, 8)))
    m["nw_bc"] = np.ascontiguousarray(np.broadcast_to(np.asarray(inp["ssd_norm"], f)[j][g * 512:(g + 1) * 512][None], (128, 512)))
    return m


class ModProg:
    def __init__(self):
        self.nc = bass.Bass("TRN2", target_bir_lowering=False)
        self.dram = {}
        nc = self.nc
        with ExitStack() as es:
            kb = KB(nc, es)
            self.kb = kb
            cv = kb.sb("cv3", [128, KD, 3], F32); tcv = kb.tile("cv3")
            scv = kb.sb("scv3", [128, KD, 3], BF16); tscv = kb.tile("scv3")
            kb.dma(SP, cv[:], self.din("cvec3", [128, KD, 3]), "cv3", writes=[tcv])
            act(kb, scv[:], cv[:], AF.Silu, [tcv], [tscv])
            bank = kb.ps("bank0", [128, 512], F32); tb = kb.tile("bank0")
            res = kb.sb("res", [128, 4, 9, 3], F32); tres = kb.tile("res")
            aw = self.din("ada_w_sh", [4, D, 1152])
            ws = [kb.sb(f"w{x}", [128, KD, 1152], BF16) for x in range(2)]
            for i in range(4):
                w = ws[i % 2]; wt = kb.tile(f"w{i % 2}")
                kb.dma(POOL, w[:], aw[i].rearrange("(k p) c -> p k c", p=128), f"w{i % 2}", writes=[wt])
                for jj in range(9):
                    for k in range(KD):
                        mm(kb, bank[:, (i * 9 + jj) * 3:(i * 9 + jj) * 3 + 3], w[:, k, jj * 128:(jj + 1) * 128],
                           scv[:, k, :], k == 0, k == KD - 1, [wt, tscv], [tb], inc=(k == KD - 1))
            tcopy(kb, DVE, res[:], bank[:, 0:108].rearrange("p (i j c) -> p i j c", i=4, j=9), [tb], [tres])
            kb.dma(SP, self.dout("mod_part", [128, 4, 9, 3]), res[:], "res", reads=[tres])
            kb.barrier()
            kb.emit()

    din = TokProg.din
    dout = TokProg.dout


_PROGS = {}


def _prog(key, ctor):
    if key not in _PROGS:
        _PROGS[key] = ctor()
    return _PROGS[key]


def run_prog(prog, in_maps):
    names = set(prog.dram.keys())
    outs = ("u_out", "lat_out", "out_fm", "yn", "mod_part", "sprev_out")
    maps = [{k: v for k, v in m.items() if k in names} for m in in_maps]
    for m in maps:
        missing = [k for k in names if k not in m and k not in outs]
        assert not missing, missing
    res = run_bass_kernel_spmd(prog.nc, maps, core_ids=list(range(NCORES)))
    return res.results


def kernel(x, c, ctx, c_ctx, ada_w, ada_b, norm_g, ffn_wg, ffn_wu, ffn_wd,
           ssd_in, ssd_conv_w, ssd_conv_b, ssd_dt_bias, ssd_a_log, ssd_d, ssd_norm, ssd_out,
           sc_in, sc_conv, sc_out, gm_in, gm_vnorm, gm_ws, gm_bs, gm_out, final_norm):
    f = np.float32
    inp = dict(x=x, c=c, ctx=ctx, c_ctx=c_ctx, ada_w=ada_w, ada_b=ada_b, norm_g=norm_g, ffn_wg=ffn_wg,
               ffn_wu=ffn_wu, ffn_wd=ffn_wd, ssd_in=ssd_in, ssd_conv_w=ssd_conv_w, ssd_conv_b=ssd_conv_b,
               ssd_dt_bias=ssd_dt_bias, ssd_a_log=ssd_a_log, ssd_d=ssd_d, ssd_norm=ssd_norm, ssd_out=ssd_out,
               sc_in=sc_in, sc_conv=sc_conv, sc_out=sc_out, gm_in=gm_in, gm_vnorm=gm_vnorm, gm_ws=gm_ws,
               gm_bs=gm_bs, gm_out=gm_out, final_norm=final_norm)
    inp = {k: np.asarray(v) for k, v in inp.items()}
    B, SEQ = inp["x"].shape[0], inp["x"].shape[1]
    TL = SEQ // 4
    bf = ml_dtypes.bfloat16
    com = {}
    com["ada_b_l"] = np.ascontiguousarray(inp["ada_b"].astype(f).reshape(4, 72, 128).transpose(2, 0, 1))
    com["norm_g_l"] = np.ascontiguousarray(inp["norm_g"].astype(f).reshape(4, 3, KD, 128).transpose(3, 0, 1, 2))
    for i in range(4):
        for hs in range(2):
            com[f"wg_{i}_{hs}"] = np.ascontiguousarray(inp["ffn_wg"][i, hs], f)
            com[f"wu_{i}_{hs}"] = np.ascontiguousarray(inp["ffn_wu"][i, hs], f)
            com[f"wd_{i}_{hs}"] = np.ascontiguousarray(inp["ffn_wd"][i, hs], f)
    com["final_norm_l"] = np.ascontiguousarray(inp["final_norm"].astype(f).reshape(KD, 128).T)
    com["sc_in"] = inp["sc_in"].astype(f)
    com["sc_out"] = inp["sc_out"].astype(f)
    com["sc_conv_l"] = np.ascontiguousarray(inp["sc_conv"].astype(f)[0].reshape(3, KD, 128).transpose(2, 0, 1))
    com["gm_in"] = inp["gm_in"].astype(f)
    com["gm_out"] = inp["gm_out"].astype(f)
    com["gm_wsT_l"] = np.ascontiguousarray(inp["gm_ws"].astype(f)[0].transpose(2, 0, 1))
    com["gm_bs_l"] = np.ascontiguousarray(inp["gm_bs"].astype(f)[0][None])
    com["gm_vn_l"] = np.ascontiguousarray(np.broadcast_to(inp["gm_vnorm"].astype(f)[0][None, :], (128, 2048)))
    com["ssd_out0"] = np.ascontiguousarray(inp["ssd_out"][0], f)
    com["ssd_out1"] = np.ascontiguousarray(inp["ssd_out"][1], f)

    cv3 = np.empty((128, KD, 3), f)
    cv3[:, :, 0] = inp["c"].astype(f)[0].reshape(KD, 128).T
    cv3[:, :, 1] = inp["c"].astype(f)[1].reshape(KD, 128).T
    cv3[:, :, 2] = inp["c_ctx"].astype(f).reshape(KD, 128).T
    mp = _prog("mod", ModProg)
    aw = inp["ada_w"].astype(f)
    rm = run_prog(mp, [{"cvec3": cv3, "ada_w_sh": np.ascontiguousarray(aw[:, :, r * 1152:(r + 1) * 1152])}
                       for r in range(NCORES)])
    mod_in = []
    for b in range(B):
        m = np.empty((128, 4, 72, 2), f)
        for r in range(NCORES):
            m[:, :, r * 9:(r + 1) * 9, 0] = rm[r]["mod_part"][:, :, :, b]
            m[:, :, r * 9:(r + 1) * 9, 1] = rm[r]["mod_part"][:, :, :, 2]
        mod_in.append(m)

    def cores():
        for core in range(NCORES):
            yield core, core // 4, core % 4

    maps = []
    for core, b, r in cores():
        m = dict(com)
        m["lat_in"] = np.concatenate([fm(inp["x"][b, r * TL:(r + 1) * TL].astype(f)), fm(inp["ctx"][b].astype(f))], axis=2)
        m["mod_in"] = mod_in[b]
        maps.append(m)
    pA = _prog(("tokA", TL), lambda: TokProg(TL, [("modin",), ("ffn", 0, 0), ("umix", 0), ("latout",)]))
    rA = run_prog(pA, maps)

    def ssd_layer(res_tok, j):
        smaps = []
        for core, b, g in cores():
            m = prep_ssd(inp, j, g)
            parts = [res_tok[b * 4]["u_out"][:, :, TL:TL + CT]] + [res_tok[b * 4 + r]["u_out"][:, :, 0:TL] for r in range(4)]
            m["u_all"] = np.ascontiguousarray(np.concatenate(parts, axis=2))
            smaps.append(m)
        p2 = _prog(("ssd", SEQ), lambda: SsdProg(SEQ, phase="both"))
        r2 = run_prog(p2, smaps)
        yn_in = []
        for core, b, r in cores():
            y = np.empty((128, 16, TL + CT), bf)
            for g in range(4):
                yg = r2[b * 4 + g]["yn"]
                y[:, g * 4:(g + 1) * 4, 0:TL] = yg[:, :, CT + r * TL:CT + (r + 1) * TL]
                y[:, g * 4:(g + 1) * 4, TL:TL + CT] = yg[:, :, 0:CT]
            yn_in.append(y)
        return yn_in

    yn0 = ssd_layer(rA, 0)
    maps = []
    for core, b, r in cores():
        m = dict(com)
        m["lat_in"] = rA[core]["lat_out"]
        m["mod_in"] = mod_in[b]
        m["yn_in"] = yn0[core]
        maps.append(m)
    stB = [("modin",), ("ssdout", 0, 0), ("ffn", 0, 1),
           ("ffn", 1, 0), ("sconv", 1), ("ffn", 1, 1),
           ("ffn", 2, 0), ("gmlp", 2), ("ffn", 2, 1),
           ("ffn", 3, 0), ("umix", 3), ("latout",)]
    pB = _prog(("tokB", TL), lambda: TokProg(TL, stB))
    rB = run_prog(pB, maps)
    yn3 = ssd_layer(rB, 1)
    maps = []
    for core, b, r in cores():
        m = dict(com)
        m["lat_in"] = rB[core]["lat_out"]
        m["mod_in"] = mod_in[b]
        m["yn_in"] = yn3[core]
        maps.append(m)
    pC = _prog(("tokC", TL), lambda: TokProg(TL, [("modin",), ("ssdout", 3, 1, False), ("ffn", 3, 1, False), ("final",)]))
    rC = run_prog(pC, maps)
    out = np.empty((B, SEQ, D), f)
    for core, b, r in cores():
        out[b, r * TL:(r + 1) * TL] = unfm(rC[core]["out_fm"])
    return out
```
